# Optimizing a Trainium2 kernel written in Bass

```python
import jax, jax.numpy as jnp
from jax import lax
import numpy as np

D_MODEL = 4096
BATCH = 4
SEQ = 4096
DEPTH = 2

CTX_LEN = 256
GRID_W = 64
N_MIXERS = 2
D_RNN = D_MODEL
RNN_BLOCKS = 16
RNN_BW = D_RNN // RNN_BLOCKS
CONV_W = 4
CONV_PAD_LO = 2
LRU_C = 8.0
LRU_A_MIN = 0.9
LRU_A_MAX = 0.999
HEAD_DIM = 128
N_HEADS = D_MODEL // HEAD_DIM
N_KV_HEADS = 8
GQA_GROUPS = N_HEADS // N_KV_HEADS
ATTN_WIDTH = N_HEADS * HEAD_DIM
KV_WIDTH = N_KV_HEADS * HEAD_DIM
WINDOW = 128
ATTN_BLOCK = 128
ROPE_BASE = 10000.0
NORM_EPS = 1e-6
NEG_INF = -1e30

kernel_name = 'hybrid_rglru_swa_dit_prefix'


def rms_norm(x, w):
    xf = x.astype(jnp.float32)
    xf = xf * lax.rsqrt(jnp.mean(xf * xf, axis=-1, keepdims=True) + NORM_EPS)
    return (xf * w.astype(jnp.float32)).astype(x.dtype)


def adaln_input(x, norm_w, shift, scale):
    return rms_norm(x, norm_w) * (1.0 + scale) + shift


def depthwise_conv(u, w, b):
    out = lax.conv_general_dilated(
        u, w[:, None, :], window_strides=(1,),
        padding=[(CONV_PAD_LO, CONV_W - 1 - CONV_PAD_LO)],
        dimension_numbers=('NWC', 'WIO', 'NWC'),
        feature_group_count=u.shape[-1])
    return out + b


def rglru_coeffs(u, w_r, b_r, w_i, b_i, lam):
    uf = u.astype(jnp.float32)
    ub = uf.reshape(uf.shape[:-1] + (RNN_BLOCKS, RNN_BW))
    r = jax.nn.sigmoid(jnp.einsum('blhi,hij->blhj', ub, w_r.astype(jnp.float32)).reshape(uf.shape)
                       + b_r.astype(jnp.float32))
    i = jax.nn.sigmoid(jnp.einsum('blhi,hij->blhj', ub, w_i.astype(jnp.float32)).reshape(uf.shape)
                       + b_i.astype(jnp.float32))
    log_a = -LRU_C * r * jax.nn.softplus(-lam.astype(jnp.float32))
    a = jnp.exp(log_a)
    b = jnp.sqrt(-jnp.expm1(2.0 * log_a)) * (i * uf)
    return a, b


def _lin_combine(e1, e2):
    a1, b1 = e1
    a2, b2 = e2
    return a1 * a2, a2 * b1 + b2


def linear_scan(a, b, reverse):
    return lax.associative_scan(_lin_combine, (a, b), axis=1, reverse=reverse)[1]


def rglru_branch(h, hc, w_in, conv_w, conv_b, w_r, b_r, w_i, b_i, lam, w_out, with_ctx_out):
    u_lat, g_lat = jnp.split(h @ w_in, 2, axis=-1)
    if with_ctx_out:
        u_ctx, g_ctx = jnp.split(hc @ w_in, 2, axis=-1)
    else:
        u_ctx = hc @ w_in[:, :D_RNN]
    u_lat = depthwise_conv(u_lat, conv_w, conv_b)
    u_ctx = depthwise_conv(u_ctx, conv_w, conv_b)
    lat_states, ctx_states = [], []
    for d, reverse in enumerate((False, True)):
        a_c, b_c = rglru_coeffs(u_ctx, w_r[d], b_r[d], w_i[d], b_i[d], lam[d])
        s_ctx = linear_scan(a_c, b_c, reverse)
        h0 = s_ctx[:, 0] if reverse else s_ctx[:, -1]
        a_l, b_l = rglru_coeffs(u_lat, w_r[d], b_r[d], w_i[d], b_i[d], lam[d])
        start = -1 if reverse else 0
        b_l = b_l.at[:, start].add(a_l[:, start] * h0)
        lat_states.append(linear_scan(a_l, b_l, reverse))
        ctx_states.append(s_ctx)
    y_lat = ((lat_states[0] + lat_states[1]).astype(h.dtype) * jax.nn.silu(g_lat)) @ w_out
    y_ctx = None
    if with_ctx_out:
        y_ctx = ((ctx_states[0] + ctx_states[1]).astype(hc.dtype) * jax.nn.silu(g_ctx)) @ w_out
    return y_lat, y_ctx


def head_rms_norm(t, w):
    return rms_norm(t, w)


def grid_positions(S):
    rows = S // GRID_W
    row = jnp.repeat(jnp.arange(rows, dtype=jnp.int32), GRID_W)
    col = jnp.tile(jnp.arange(GRID_W, dtype=jnp.int32), rows)
    return row, col


def rope_1d(xf, pos, dim):
    half = dim // 2
    inv_freq = ROPE_BASE ** (-jnp.arange(half, dtype=jnp.float32) * (2.0 / dim))
    ang = pos.astype(jnp.float32)[:, None] * inv_freq[None, :]
    cos = jnp.cos(ang)[None, :, None, :]
    sin = jnp.sin(ang)[None, :, None, :]
    x1, x2 = xf[..., :half], xf[..., half:]
    return jnp.concatenate([x1 * cos - x2 * sin, x2 * cos + x1 * sin], axis=-1)


def rope_2d(t, row, col):
    tf = t.astype(jnp.float32)
    hd = t.shape[-1] // 2
    out = jnp.concatenate([rope_1d(tf[..., :hd], row, hd), rope_1d(tf[..., hd:], col, hd)], axis=-1)
    return out.astype(t.dtype)


def sink_column(sink, like):
    s = sink.astype(jnp.float32).reshape(N_KV_HEADS, GQA_GROUPS)[:, :, None, None]
    return jnp.broadcast_to(s, like.shape[:-1] + (1,))


def banded_window_attention(q, k, v, kc, vc, sink):
    B, S = q.shape[:2]
    C = kc.shape[1]
    NB = S // ATTN_BLOCK
    scale = HEAD_DIM ** -0.5
    qb = q.reshape(B, NB, ATTN_BLOCK, N_KV_HEADS, GQA_GROUPS, HEAD_DIM)

    def band(t):
        tb = t.reshape(B, NB, ATTN_BLOCK, N_KV_HEADS, HEAD_DIM)
        tp = jnp.pad(tb, ((0, 0), (1, 1), (0, 0), (0, 0), (0, 0)))
        return jnp.concatenate([tp[:, :-2], tp[:, 1:-1], tp[:, 2:]], axis=2)

    kw, vw = band(k), band(v)
    s_win = jnp.einsum('bnqkgd,bnskd->bnkgqs', qb, kw, preferred_element_type=jnp.float32) * scale
    s_ctx = jnp.einsum('bnqkgd,bckd->bnkgqc', qb, kc, preferred_element_type=jnp.float32) * scale
    qi = jnp.arange(ATTN_BLOCK)[:, None]
    ki = jnp.arange(3 * ATTN_BLOCK)[None, :]
    rel = ki - ATTN_BLOCK - qi
    key_pos = (jnp.arange(NB)[:, None, None] - 1) * ATTN_BLOCK + ki[None]
    mask = (jnp.abs(rel) <= WINDOW)[None] & (key_pos >= 0) & (key_pos < S)
    s_win = jnp.where(mask[None, :, None, None], s_win, NEG_INF)
    logits = jnp.concatenate([sink_column(sink, s_ctx), s_ctx, s_win], axis=-1)
    p = jax.nn.softmax(logits, axis=-1).astype(v.dtype)
    o = (jnp.einsum('bnkgqc,bckd->bnqkgd', p[..., 1:1 + C], vc)
         + jnp.einsum('bnkgqs,bnskd->bnqkgd', p[..., 1 + C:], vw))
    return o.reshape(B, S, ATTN_WIDTH)


def context_attention(qc, kc, vc, sink):
    B, C = qc.shape[:2]
    scale = HEAD_DIM ** -0.5
    qg = qc.reshape(B, C, N_KV_HEADS, GQA_GROUPS, HEAD_DIM)
    s = jnp.einsum('bqkgd,bckd->bkgqc', qg, kc, preferred_element_type=jnp.float32) * scale
    p = jax.nn.softmax(jnp.concatenate([sink_column(sink, s), s], axis=-1), axis=-1).astype(vc.dtype)
    o = jnp.einsum('bkgqc,bckd->bqkgd', p[..., 1:], vc)
    return o.reshape(B, C, ATTN_WIDTH)


def window_attn_branch(h, hc, w_in, q_norm, k_norm, sink, w_out, with_ctx_out):
    B, S, _ = h.shape
    C = hc.shape[1]
    cuts = [ATTN_WIDTH, ATTN_WIDTH + KV_WIDTH, ATTN_WIDTH + 2 * KV_WIDTH]
    q, k, v, g = jnp.split(h @ w_in, cuts, axis=-1)
    q = head_rms_norm(q.reshape(B, S, N_HEADS, HEAD_DIM), q_norm)
    k = head_rms_norm(k.reshape(B, S, N_KV_HEADS, HEAD_DIM), k_norm)
    v = v.reshape(B, S, N_KV_HEADS, HEAD_DIM)
    row, col = grid_positions(S)
    q = rope_2d(q, row, col)
    k = rope_2d(k, row, col)
    if with_ctx_out:
        qc, kc, vc, gc = jnp.split(hc @ w_in, cuts, axis=-1)
    else:
        kc, vc = jnp.split(hc @ w_in[:, ATTN_WIDTH:ATTN_WIDTH + 2 * KV_WIDTH], 2, axis=-1)
    kc = head_rms_norm(kc.reshape(B, C, N_KV_HEADS, HEAD_DIM), k_norm)
    vc = vc.reshape(B, C, N_KV_HEADS, HEAD_DIM)
    o = banded_window_attention(q, k, v, kc, vc, sink)
    y_lat = (o * jax.nn.silu(g)) @ w_out
    y_ctx = None
    if with_ctx_out:
        qc = head_rms_norm(qc.reshape(B, C, N_HEADS, HEAD_DIM), q_norm)
        oc = context_attention(qc, kc, vc, sink)
        y_ctx = (oc * jax.nn.silu(gc)) @ w_out
    return y_lat, y_ctx


def setup_inputs(seed: int = 0) -> dict:
    key = jax.random.key(seed)
    ks = jax.random.split(key, 24)
    f32 = jnp.float32
    n_a = (DEPTH + 1) // 2
    n_b = DEPTH // 2
    nrm = lambda k, shape, s: jax.random.normal(k, shape, f32) * s
    u = jax.random.uniform(ks[13], (n_a, 2, D_RNN), f32, LRU_A_MIN, LRU_A_MAX)
    a0 = u ** (1.0 / LRU_C)
    return {
        'x': nrm(ks[0], (BATCH, SEQ, D_MODEL), 1.0),
        'c': nrm(ks[1], (BATCH, D_MODEL), 1.0),
        'ctx': nrm(ks[2], (BATCH, CTX_LEN, D_MODEL), 1.0),
        'c_ctx': nrm(ks[3], (D_MODEL,), 1.0),
        'w_mod': nrm(ks[4], (DEPTH, D_MODEL, 3 * D_MODEL), D_MODEL ** -0.5),
        'b_mod': nrm(ks[5], (DEPTH, 3 * D_MODEL), 0.01),
        'norm_w': 1.0 + nrm(ks[6], (DEPTH, D_MODEL), 0.02),
        'rg_w_in': nrm(ks[7], (n_a, D_MODEL, 2 * D_RNN), D_MODEL ** -0.5),
        'rg_conv_w': nrm(ks[8], (n_a, CONV_W, D_RNN), CONV_W ** -0.5),
        'rg_conv_b': nrm(ks[9], (n_a, D_RNN), 0.01),
        'rg_w_r': nrm(ks[10], (n_a, 2, RNN_BLOCKS, RNN_BW, RNN_BW), RNN_BW ** -0.5),
        'rg_b_r': nrm(ks[11], (n_a, 2, D_RNN), 0.01),
        'rg_w_i': nrm(ks[12], (n_a, 2, RNN_BLOCKS, RNN_BW, RNN_BW), RNN_BW ** -0.5),
        'rg_b_i': nrm(ks[14], (n_a, 2, D_RNN), 0.01),
        'rg_lam': jnp.log(a0) - jnp.log1p(-a0),
        'rg_w_out': nrm(ks[15], (n_a, D_RNN, D_MODEL), D_RNN ** -0.5),
        'at_w_in': nrm(ks[16], (n_b, D_MODEL, 2 * ATTN_WIDTH + 2 * KV_WIDTH), D_MODEL ** -0.5),
        'at_q_norm': 1.0 + nrm(ks[17], (n_b, HEAD_DIM), 0.02),
        'at_k_norm': 1.0 + nrm(ks[18], (n_b, HEAD_DIM), 0.02),
        'at_sink': nrm(ks[19], (n_b, N_HEADS), 0.5),
        'at_w_out': nrm(ks[20], (n_b, ATTN_WIDTH, D_MODEL), ATTN_WIDTH ** -0.5),
    }


def reference(x, c, ctx, c_ctx, w_mod, b_mod, norm_w,
              rg_w_in, rg_conv_w, rg_conv_b, rg_w_r, rg_b_r, rg_w_i, rg_b_i, rg_lam, rg_w_out,
              at_w_in, at_q_norm, at_k_norm, at_sink, at_w_out):
    for layer in range(DEPTH):
        with_ctx_out = layer < DEPTH - 1
        mod = jax.nn.silu(c) @ w_mod[layer] + b_mod[layer]
        shift, scale, gate = jnp.split(mod, 3, axis=-1)
        mod_c = jax.nn.silu(c_ctx) @ w_mod[layer] + b_mod[layer]
        shift_c, scale_c, gate_c = jnp.split(mod_c, 3, axis=-1)
        h = adaln_input(x, norm_w[layer], shift[:, None, :], scale[:, None, :])
        hc = adaln_input(ctx, norm_w[layer], shift_c, scale_c)
        j = layer // N_MIXERS
        if layer % N_MIXERS == 0:
            y, yc = rglru_branch(h, hc, rg_w_in[j], rg_conv_w[j], rg_conv_b[j], rg_w_r[j], rg_b_r[j],
                                 rg_w_i[j], rg_b_i[j], rg_lam[j], rg_w_out[j], with_ctx_out)
        else:
            y, yc = window_attn_branch(h, hc, at_w_in[j], at_q_norm[j], at_k_norm[j], at_sink[j],
                                       at_w_out[j], with_ctx_out)
        x = x + gate[:, None, :] * y
        if with_ctx_out:
            ctx = ctx + gate_c * yc
    return x
```

```python
import numpy as np
import ml_dtypes
import concourse.bass as bass
import concourse.mybir as mybir
from concourse.bass_utils import run_bass_kernel_spmd

F32 = mybir.dt.float32
BF16 = mybir.dt.bfloat16
ALU = mybir.AluOpType
AF = mybir.ActivationFunctionType

D = 4096
KC = 32
NOWN = 2048
NH = 2176
NU = 2178
NLOC = 2304
CTX = 256
EPS = 1e-6
ENG = ("pe", "act", "dve", "pool", "sp")
DEBUG = False
REUSE_DSEMS = True


class Tk:
    __slots__ = ("ap", "lw", "rd", "dsem", "const")

    def __init__(self, ap=None, dsem=None, const=False):
        self.ap = ap
        self.lw = None
        self.rd = {}
        self.dsem = dsem
        self.const = const

    def __getitem__(self, k):
        return self.ap[k]


class Prog:
    def __init__(self, nc, stack):
        self.nc = nc
        self.stack = stack
        self.prog = {e: [] for e in ENG}
        self.cnt = {}
        self.seen = {e: {} for e in ENG}
        self.esem = {}
        for e in ENG:
            s = stack.enter_context(nc.semaphore("es_" + e))
            self.esem[e] = s
            self.cnt[s] = 0
        self.free_dsems = []
        self.ndsem = 0
        self.stage_dsems = []

    def dsem(self):
        if self.free_dsems:
            s = self.free_dsems.pop()
        else:
            s = self.stack.enter_context(self.nc.semaphore("ds%d" % self.ndsem))
            self.ndsem += 1
            self.cnt[s] = 0
        self.stage_dsems.append(s)
        return s

    def sb(self, st, name, shape, dt, dma=False, const=False):
        self.uid = getattr(self, "uid", 0) + 1
        name = "%s_u%d" % (name, self.uid)
        t = st.enter_context(self.nc.sbuf_tensor(name, list(shape), dt))
        return Tk(t, self.dsem() if dma else None, const)

    def ps(self, st, name, shape, dt):
        self.uid = getattr(self, "uid", 0) + 1
        name = "%s_u%d" % (name, self.uid)
        t = st.enter_context(self.nc.psum_tensor(name, list(shape), dt))
        return Tk(t)

    def dram(self, ap=None):
        return Tk(ap)

    def op(self, eng, fn, reads=(), writes=(), dma=None):
        waits = {}
        seen = self.seen[eng]

        def need(tok):
            if tok is None:
                return
            sem, val = tok
            if seen.get(sem, 0) >= val:
                return
            if waits.get(sem, 0) < val:
                waits[sem] = val

        for t in reads:
            need(t.lw)
        for t in writes:
            need(t.lw)
            for sem, val in t.rd.items():
                need((sem, val))
        if eng == "pool" and dma is not None:
            hist = self.__dict__.setdefault("pool_hist", [])
            if len(hist) >= 3:
                need(hist[-3])
        for sem, val in waits.items():
            seen[sem] = val
        if dma is not None:
            sem = dma.dsem
            self.cnt[sem] += 16
            inc = (sem, 16)
        else:
            sem = self.esem[eng]
            self.cnt[sem] += 1
            inc = (sem, 1)
        tok = (sem, self.cnt[sem])
        if eng == "pool" and dma is not None:
            self.pool_hist.append(tok)
        self.prog[eng].append((list(waits.items()), fn, inc))
        for t in reads:
            if not t.const:
                if t.rd.get(sem, 0) < tok[1]:
                    t.rd[sem] = tok[1]
        for t in writes:
            t.lw = tok
            t.rd = {}
        return tok

    def barrier(self):
        for e in ENG:
            waits = []
            for sem, c in self.cnt.items():
                if c > 0 and self.seen[e].get(sem, 0) < c and sem is not self.esem[e]:
                    waits.append((sem, c))
                    self.seen[e][sem] = c
            self.prog[e].append((waits, None, None))

    def end_stage(self):
        self.barrier()
        if REUSE_DSEMS:
            self.free_dsems.extend(self.stage_dsems)
        self.stage_dsems = []

    def check(self):
        pos = {e: 0 for e in ENG}
        val = {}
        progress = True
        while progress:
            progress = False
            for e in ENG:
                lst = self.prog[e]
                while pos[e] < len(lst):
                    waits, fn, inc = lst[pos[e]]
                    if any(val.get(sem, 0) < v for sem, v in waits):
                        break
                    if inc is not None:
                        val[inc[0]] = val.get(inc[0], 0) + inc[1]
                    pos[e] += 1
                    progress = True
        stuck = {e: (pos[e], len(self.prog[e])) for e in ENG if pos[e] < len(self.prog[e])}
        for e in stuck:
            waits, fn, inc = self.prog[e][pos[e]]
            print("STUCK", e, stuck[e], [(str(sem), v, val.get(sem, 0)) for sem, v in waits])
        bad = {str(sem): (val.get(sem, 0), c) for sem, c in self.cnt.items() if val.get(sem, 0) != c}
        print("check: stuck=%s mismatched=%s" % (bool(stuck), bad))

    def emit(self):
        nc = self.nc
        prog = self.prog

        def replay(lst, e):
            for waits, fn, inc in lst:
                for sem, val in waits:
                    e.wait_ge(sem, val)
                if fn is not None:
                    ins = fn(e)
                    ins.then_inc(inc[0], inc[1])

        with nc.Block() as block:
            @block.tensor
            def _(e):
                replay(prog["pe"], e)

            @block.scalar
            def _(e):
                replay(prog["act"], e)

            @block.vector
            def _(e):
                replay(prog["dve"], e)

            @block.gpsimd
            def _(e):
                replay(prog["pool"], e)

            @block.sync
            def _(e):
                replay(prog["sp"], e)


class Ring:
    def __init__(self, tiles):
        self.tiles = tiles
        self.i = 0

    def next(self):
        t = self.tiles[self.i % len(self.tiles)]
        self.i += 1
        return t


def build_program(debug=False, stop_after=None, only=None, asub=None):
    import contextlib
    nc = bass.Bass("TRN2", target_bir_lowering=False)
    dk = "ExternalOutput" if debug else "Internal"

    NEED_A = ("at_w_in", "qk_norm_fm", "sink_bc", "cos_fm", "sin_fm", "rot_m", "ident_in", "mask_in",
              "normw_fm", "bmod_fm")

    def din(name, shape, dt=F32):
        if only is not None and name not in NEED_A:
            return None
        return nc.dram_tensor(name, list(shape), dt, kind="ExternalInput").ap()

    def dscr(name, shape, dt):
        if debug and (debug is True or name in debug):
            return nc.dram_tensor(name, list(shape), dt, kind="ExternalOutput").ap()
        return nc.dram_tensor(name, list(shape), dt).ap()

    x_loc = din("x_loc", [NLOC, D])
    ctx_loc = din("ctx_loc", [CTX, D])
    c_fm = din("c_fm", [128, KC, 2])
    w_mod = din("w_mod", [2, D, 3 * D])
    bmod_fm = din("bmod_fm", [128, 2, 96])
    normw_fm = din("normw_fm", [128, 2, KC])
    rg_w_in = din("rg_w_in", [D, 2 * D])
    conv5_fm = din("conv5_fm", [128, KC, 5])
    convb_fm = din("convb_fm", [128, KC])
    gate_w = din("gate_w", [4, 16, 256, 256])
    gate_b_fm = din("gate_b_fm", [128, 4, KC])
    lam_fm = din("lam_fm", [128, 2, KC])
    rg_w_out = din("rg_w_out", [D, D])
    at_w_in = din("at_w_in", [D, 10240])
    qk_norm_fm = din("qk_norm_fm", [128, 2])
    sink_bc = din("sink_bc", [128, 32])
    at_w_out = din("at_w_out", [D, D])
    cos_fm = din("cos_fm", [128, NLOC])
    sin_fm = din("sin_fm", [128, NLOC])
    rot_m = din("rot_m", [128, 128])
    ident_in = din("ident_in", [128, 128])
    mask_in = din("mask_in", [2, 128, 512])
    sel_in = din("sel_in", [128, 2])
    out = nc.dram_tensor("out", [NOWN, D], F32, kind="ExternalOutput").ap()

    hT = dscr("hT", [KC, 128, NLOC], BF16)
    hTc = dscr("hTc", [KC, 128, CTX], BF16)
    Z0 = dscr("Z0", [D, NH], F32)
    ZC = dscr("ZC", [D, NH], F32)
    ZCTX = dscr("ZCTX", [D, CTX], BF16)
    X1 = dscr("X1", [NH, D], F32)
    CTX1 = dscr("CTX1", [CTX, D], F32)
    Z1T = dscr("Z1T", [D, NOWN], BF16)
    grow = dscr("grow", [2, 2, D], F32)
    gin = nc.dram_tensor("gin", [128, KC], F32)
    gout = nc.dram_tensor("gout", [256, KC], F32)

    with contextlib.ExitStack() as gst:
        p = Prog(nc, gst)
        gst.enter_context(nc.allow_non_contiguous_dma(reason="small param scatter"))
        gst.enter_context(nc.allow_low_precision(reason="bf16 matmul operands"))
        ccsem = gst.enter_context(nc.semaphore("ccsem"))
        p.cnt[ccsem] = 0

        ident_bf = p.sb(gst, "ident_bf", [128, 128], BF16, dma=True)
        ident_f = p.sb(gst, "ident_f", [128, 128], F32, dma=True)
        mod = [p.sb(gst, "mod%d" % l, [128, 96, 2], F32) for l in range(2)]
        Avec = [p.sb(gst, "Avec%d" % l, [128, KC, 2], F32) for l in range(2)]
        normw = p.sb(gst, "normw", [128, 2, KC], F32, dma=True)
        bmod = p.sb(gst, "bmod", [128, 2, 96], F32, dma=True)
        Sin = p.sb(gst, "Sin", [128, KC], F32)

        def dma_ld(eng, dst_tk, dst_ap, src_ap, reads=(), extra_w=()):
            p.op(eng, lambda e: e.dma_start(out=dst_ap, in_=src_ap), reads=reads,
                 writes=(dst_tk,) + tuple(extra_w), dma=dst_tk)

        def dma_st(eng, src_tk, dst_ap, src_ap, dst_tk=None):
            p.op(eng, lambda e: e.dma_start(out=dst_ap, in_=src_ap), reads=(src_tk,),
                 writes=(dst_tk,) if dst_tk is not None else (), dma=src_tk)

        dma_ld("sp", ident_f, ident_f[:, :], ident_in)
        dma_ld("pool", ident_bf, ident_bf[:, :], ident_in)
        dma_ld("sp", normw, normw[:, :, :], normw_fm)
        dma_ld("sp", bmod, bmod[:, :, :], bmod_fm)

        with contextlib.ExitStack() as st:
          if only is None:
              cf = p.sb(st, "cf", [128, KC, 2], F32, dma=True)
              scb = p.sb(st, "scb", [128, KC, 2], BF16)
              Wm = Ring([p.sb(st, "Wm%d" % i, [128, KC, 512], BF16, dma=True) for i in range(2)])
              psm = [p.ps(st, "psm%d" % l, [128, 512], F32) for l in range(2)]
              dma_ld("sp", cf, cf[:, :, :], c_fm)
              p.op("act", lambda e: e.activation(out=scb[:, :, :], in_=cf[:, :, :], func=AF.Silu),
                   reads=(cf,), writes=(scb,))
              for l in range(2):
                  for blk in range(24):
                      W = Wm.next()
                      src = w_mod[l, :, blk * 512:(blk + 1) * 512].rearrange("(k p) n -> p k n", p=128)
                      dma_ld("pool", W, W[:, :, :], src)

                      def mm(e, W=W, blk=blk, l=l):
                          ins = None
                          for j in range(4):
                              n = blk * 4 + j
                              for k in range(KC):
                                  ins = e.matmul(psm[l][:, n * 2:n * 2 + 2], lhsT=W[:, k, j * 128:(j + 1) * 128],
                                                 rhs=scb[:, k, :], start=(k == 0), stop=(k == KC - 1))
                          return ins
                      p.op("pe", mm, reads=(W, scb), writes=(psm[l],))
                  for r in range(2):
                      p.op("dve", lambda e, l=l, r=r: e.tensor_tensor(
                          out=mod[l][:, :, r], in0=psm[l][:, 0:192].rearrange("p (n r) -> p n r", r=2)[:, :, r],
                          in1=bmod[:, l, :], op=ALU.add), reads=(psm[l], bmod), writes=(mod[l],))
                  for r in range(2):
                      p.op("dve", lambda e, l=l, r=r: e.scalar_tensor_tensor(
                          out=Avec[l][:, :, r], in0=mod[l][:, 32:64, r], scalar=1.0, in1=normw[:, l, :],
                          op0=ALU.add, op1=ALU.mult), reads=(mod[l], normw), writes=(Avec[l],))
                      p.op("sp", lambda e, l=l, r=r: e.dma_start(
                          out=grow[l, r].rearrange("(k p) -> p k", p=128), in_=mod[l][:, 64:96, r]),
                          reads=(mod[l],), writes=(), dma=cf)
              p.end_stage()

        def stage_norm(l, src_lat, ntiles, src_ctx):
            with contextlib.ExitStack() as st:
                xt_r = Ring([p.sb(st, "xt%d" % i, [128, D], F32, dma=True) for i in range(2)])
                xn_r = Ring([p.sb(st, "xn%d" % i, [128, D], BF16) for i in range(2)])
                junk = p.sb(st, "junk", [128, D], BF16)
                ss_r = Ring([p.sb(st, "ss%d" % i, [128, 1], F32) for i in range(4)])
                rs_r = Ring([p.sb(st, "rs%d" % i, [128, 1], F32) for i in range(4)])
                pt_r = Ring([p.ps(st, "pt%d" % i, [128, D], BF16) for i in range(2)])
                hb_r = Ring([p.sb(st, "hb%d" % i, [128, KC, 512], BF16, dma=True) for i in range(2)])
                jobs = []
                nblk = (ntiles + 3) // 4
                for b in range(nblk):
                    tl = list(range(b * 4, min(ntiles, b * 4 + 4)))
                    jobs.append((src_lat, tl, hT, b * 512, 0))
                jobs.append((src_ctx, [0, 1], hTc, 0, 1))
                for src, tl, dst, c0, r in jobs:
                    hb = hb_r.next()
                    for ti, t in enumerate(tl):
                        xt = xt_r.next(); xn = xn_r.next(); ss = ss_r.next(); rs = rs_r.next(); pt = pt_r.next()
                        dma_ld("sp", xt, xt[:, :], src[t * 128:(t + 1) * 128, :])
                        p.op("act", lambda e, xt=xt, ss=ss: e.activation(
                            out=junk[:, :], in_=xt[:, :], func=AF.Square, accum_out=ss[:, :]),
                            reads=(xt,), writes=(junk, ss))
                        p.op("dve", lambda e, ss=ss, rs=rs: e.tensor_scalar(
                            out=rs[:, :], in0=ss[:, :], scalar1=1.0 / D, scalar2=EPS, op0=ALU.mult, op1=ALU.add),
                            reads=(ss,), writes=(rs,))
                        p.op("act", lambda e, rs=rs: e.activation(out=rs[:, :], in_=rs[:, :], func=AF.Sqrt),
                             reads=(rs,), writes=(rs,))
                        p.op("dve", lambda e, rs=rs: e.reciprocal(out=rs[:, :], in_=rs[:, :]),
                            reads=(rs,), writes=(rs,))
                        p.op("dve", lambda e, xt=xt, xn=xn, rs=rs: e.tensor_scalar(
                            out=xn[:, :], in0=xt[:, :], scalar1=rs[:, 0:1], scalar2=None, op0=ALU.mult),
                            reads=(xt, rs), writes=(xn,))

                        def tr(e, xn=xn, pt=pt):
                            ins = None
                            for j in range(KC):
                                ins = e.transpose(out=pt[:, j * 128:(j + 1) * 128], in_=xn[:, j * 128:(j + 1) * 128],
                                                  identity=ident_bf[:, :])
                            return ins
                        p.op("pe", tr, reads=(xn, ident_bf), writes=(pt,))

                        def ev(e, pt=pt, hb=hb, ti=ti, r=r):
                            ins = None
                            for j in range(KC):
                                ins = e.activation(out=hb[:, j, ti * 128:(ti + 1) * 128], in_=pt[:, j * 128:(j + 1) * 128],
                                                   func=AF.Identity, scale=Avec[l][:, j, r:r + 1],
                                                   bias=mod[l][:, j, r:r + 1])
                            return ins
                        p.op("act", ev, reads=(pt, Avec[l], mod[l]), writes=(hb,))
                    nt = len(tl) * 128
                    dma_st("sp", hb, dst.rearrange("k p t -> p k t")[:, :, c0:c0 + nt], hb[:, :, 0:nt])
                p.end_stage()

        order = ["M", "N0", "R", "O0", "N1", "A", "O1"]
        nstage = len(order) if stop_after is None else order.index(stop_after) + 1
        if only is not None:
            nstage = 6 if "A" in only else 0
        if nstage >= 2 and only is None:
            stage_norm(0, x_loc, 18, ctx_loc)

        with contextlib.ExitStack() as st:
          if nstage >= 3 and only is None:
              Wr = Ring([p.sb(st, "Wp%d" % i, [128, KC, 128], BF16, dma=True) for i in range(6)])
              hb_r = Ring([p.sb(st, "hs%d" % i, [128, KC, 256], BF16, dma=True) for i in range(3)])
              u = [p.sb(st, "u%d" % j, [128, NU + 4], F32) for j in range(2)]
              ucx = [p.sb(st, "ucx%d" % j, [128, CTX + 4], F32) for j in range(2)]
              uc = [p.sb(st, "uc%d" % j, [128, NH], F32) for j in range(2)]
              ucc = [p.sb(st, "ucc%d" % j, [128, CTX], F32) for j in range(2)]
              ucb = [p.sb(st, "ucb%d" % j, [128, NH], BF16) for j in range(2)]
              uccb = [p.sb(st, "uccb%d" % j, [128, CTX], BF16) for j in range(2)]
              sg = [[p.sb(st, "sg%d_%d" % (q, j), [128, NH], BF16) for j in range(2)] for q in range(2)]
              sgc = [[p.sb(st, "sgc%d_%d" % (q, j), [128, CTX], BF16) for j in range(2)] for q in range(2)]
              hA = p.sb(st, "hA", [128, NH], F32)
              hAc = p.sb(st, "hAc", [128, CTX], F32)
              hBc = p.sb(st, "hBc", [128, CTX], F32)
              zcb_r = Ring([p.sb(st, "zcb%d" % i, [128, CTX], BF16, dma=True) for i in range(1)])
              t1_r = Ring([p.sb(st, "t1_%d" % i, [128, 512], F32) for i in range(2)])
              t2_r = Ring([p.sb(st, "t2_%d" % i, [128, 512], F32) for i in range(2)])
              t3_r = Ring([p.sb(st, "t3_%d" % i, [128, 512], F32, dma=True) for i in range(2)])
              t4_r = Ring([p.sb(st, "t4_%d" % i, [128, 512], F32, dma=True) for i in range(2)])
              zeros = p.sb(st, "zeros", [128, 512], F32, const=True)
              GW_r = Ring([p.sb(st, "GW%d" % i, [128, 4, 2, 256], BF16, dma=True) for i in range(1)])
              st3_r = Ring([p.sb(st, "st3_%d" % i, [128, 1], F32) for i in range(4)])
              st4_r = Ring([p.sb(st, "st4_%d" % i, [128, 1], F32) for i in range(4)])
              conv5 = p.sb(st, "conv5", [128, KC, 5], F32, dma=True)
              convb = p.sb(st, "convb", [128, KC], F32, dma=True)
              gb_t = p.sb(st, "gb_t", [128, 4, KC], F32, dma=True)
              lam_t = p.sb(st, "lam_t", [128, 2, KC], F32, dma=True)
              cneg = p.sb(st, "cneg", [128, 2, KC], F32)
              SAbuf = p.sb(st, "SAbuf", [128, KC], F32, dma=True)
              gsb = p.sb(st, "gsb", [128, 2, KC], F32, dma=True)
              sel = p.sb(st, "sel", [128, 2], F32, dma=True)
              pp_r = Ring([p.ps(st, "pp%d" % i, [128, 512], F32) for i in range(3)])
              pr_r = Ring([p.ps(st, "pr%d" % i, [128, 512], F32) for i in range(2)])
              pi_r = Ring([p.ps(st, "pi%d" % i, [128, 512], F32) for i in range(2)])

              dma_ld("sp", conv5, conv5[:, :, :], conv5_fm)
              dma_ld("sp", convb, convb[:, :], convb_fm)
              dma_ld("sp", gb_t, gb_t[:, :, :], gate_b_fm)
              dma_ld("sp", lam_t, lam_t[:, :, :], lam_fm)
              dma_ld("sp", sel, sel[:, :], sel_in)
              p.op("dve", lambda e: e.memset(zeros[:, :], 0.0), writes=(zeros,))
              for j in range(2):
                  p.op("dve", lambda e, j=j: e.memset(u[j][:, :], 0.0), writes=(u[j],))
                  p.op("dve", lambda e, j=j: e.memset(ucx[j][:, :], 0.0), writes=(ucx[j],))
              p.op("act", lambda e: e.activation(out=cneg[:, :, :], in_=lam_t[:, :, :], func=AF.Exp, scale=-1.0),
                   reads=(lam_t,), writes=(cneg,))
              p.op("act", lambda e: e.activation(out=cneg[:, :, :], in_=cneg[:, :, :], func=AF.Ln, bias=1.0),
                   reads=(cneg,), writes=(cneg,))
              p.op("dve", lambda e: e.tensor_scalar(out=cneg[:, :, :], in0=cneg[:, :, :], scalar1=-8.0, scalar2=None,
                                                     op0=ALU.mult), reads=(cneg,), writes=(cneg,))

              hT_v = hT.rearrange("k p t -> p k t")
              hTc_v = hTc.rearrange("k p t -> p k t")
              tblocks = [(i * 256, 256) for i in range(8)] + [(2048, 130)]
              chunks = [(0, 512), (512, 512), (1024, 512), (1536, 512), (2048, 128)]

              def gb_body(gbi):
                  q = gbi % 2
                  cols = [gbi * 256, gbi * 256 + 128, D + gbi * 256, D + gbi * 256 + 128]
                  Ws = []
                  for c0 in cols:
                      W = Wr.next()
                      dma_ld("pool", W, W[:, :, :], rg_w_in[:, c0:c0 + 128].rearrange("(k p) n -> p k n", p=128))
                      Ws.append(W)
                  GW = GW_r.next()
                  for gi in range(4):
                      p.op("pool", lambda e, GW=GW, gi=gi, gbi=gbi: e.dma_start(
                          out=GW[:, gi, :, :], in_=gate_w[gi, gbi].rearrange("(kh p) j -> p kh j", p=128)),
                          writes=(GW,), dma=GW)
                  seqs = [("c", 0, CTX)] + [("l", t0, n) for t0, n in tblocks]
                  for kind, t0, n in seqs:
                      hb = hb_r.next()
                      srcv = hTc_v[:, :, 0:CTX] if kind == "c" else hT_v[:, :, t0:t0 + n]
                      dma_ld("sp", hb, hb[:, :, 0:n], srcv)
                      for ci in range(4):
                          pp = pp_r.next()

                          def mm(e, W=Ws[ci], hb=hb, pp=pp, n=n):
                              ins = None
                              for k in range(KC):
                                  ins = e.matmul(pp[:, 0:n], lhsT=W[:, k, :], rhs=hb[:, k, 0:n],
                                                 start=(k == 0), stop=(k == KC - 1))
                              return ins
                          p.op("pe", mm, reads=(Ws[ci], hb), writes=(pp,))
                          j = ci % 2
                          if ci < 2:
                              if kind == "c":
                                  dstt, dsta = ucx[j], ucx[j][:, 2:2 + CTX]
                              else:
                                  dstt, dsta = u[j], u[j][:, 2 + t0:2 + t0 + n]
                              p.op("act", lambda e, pp=pp, dsta=dsta, n=n: e.activation(
                                  out=dsta, in_=pp[:, 0:n], func=AF.Copy), reads=(pp,), writes=(dstt,))
                          else:
                              n2 = min(n, 128) if (kind == "l" and t0 == 2048) else n
                              if kind == "c":
                                  dstt, dsta = sgc[q][j], sgc[q][j][:, 0:CTX]
                              else:
                                  dstt, dsta = sg[q][j], sg[q][j][:, t0:t0 + n2]
                              p.op("act", lambda e, pp=pp, dsta=dsta, n2=n2: e.activation(
                                  out=dsta, in_=pp[:, 0:n2], func=AF.Silu), reads=(pp,), writes=(dstt,))
                  for j in range(2):
                      ch = gbi * 2 + j
                      for (ut, uct, ucbt, nn) in ((ucx[j], ucc[j], uccb[j], CTX), (u[j], uc[j], ucb[j], NH)):
                          p.op("dve", lambda e, ut=ut, uct=uct, nn=nn, ch=ch: e.tensor_scalar(
                              out=uct[:, 0:nn], in0=ut[:, 0:nn], scalar1=conv5[:, ch, 0:1], scalar2=convb[:, ch:ch + 1],
                              op0=ALU.mult, op1=ALU.add), reads=(ut, conv5, convb), writes=(uct,))
                          for k in range(1, 5):
                              p.op("dve", lambda e, ut=ut, uct=uct, nn=nn, ch=ch, k=k: e.scalar_tensor_tensor(
                                  out=uct[:, 0:nn], in0=ut[:, k:k + nn], scalar=conv5[:, ch, k:k + 1], in1=uct[:, 0:nn],
                                  op0=ALU.mult, op1=ALU.add), reads=(ut, conv5, uct), writes=(uct,))
                          p.op("act", lambda e, uct=uct, ucbt=ucbt, nn=nn: e.activation(
                              out=ucbt[:, 0:nn], in_=uct[:, 0:nn], func=AF.Copy), reads=(uct,), writes=(ucbt,))
                  def half_body(j):
                      ch = gbi * 2 + j

                      def gate_ab(d, ucb_pair, uct, c0, n):
                          pr = pr_r.next(); pi = pi_r.next()
                          t1 = t1_r.next(); t2 = t2_r.next()

                          def mm(e, pr=pr, pi=pi):
                              ins = None
                              for (pt_, gi) in ((pr, 2 * d), (pi, 2 * d + 1)):
                                  for kh in range(2):
                                      ins = e.matmul(pt_[:, 0:n], lhsT=GW[:, gi, kh, j * 128:(j + 1) * 128],
                                                     rhs=ucb_pair[kh][:, c0:c0 + n], start=(kh == 0), stop=(kh == 1))
                              return ins
                          p.op("pe", mm, reads=(GW, ucb_pair[0], ucb_pair[1]), writes=(pr, pi))
                          p.op("act", lambda e: e.activation(out=t1[:, 0:n], in_=pr[:, 0:n], func=AF.Sigmoid,
                                                             bias=gb_t[:, 2 * d, ch:ch + 1]),
                               reads=(pr, gb_t), writes=(t1,))
                          p.op("act", lambda e: e.activation(out=t2[:, 0:n], in_=pi[:, 0:n], func=AF.Sigmoid,
                                                             bias=gb_t[:, 2 * d + 1, ch:ch + 1]),
                               reads=(pi, gb_t), writes=(t2,))
                          p.op("act", lambda e: e.activation(out=t1[:, 0:n], in_=t1[:, 0:n], func=AF.Exp,
                                                             scale=cneg[:, d, ch:ch + 1]),
                               reads=(t1, cneg), writes=(t1,))
                          return t1, t2

                      def finish_b(t1, t2, t3, uct, c0, n):
                          p.op("dve", lambda e: e.tensor_tensor(out=t2[:, 0:n], in0=t2[:, 0:n], in1=uct[:, c0:c0 + n],
                                                                op=ALU.mult), reads=(t2, uct), writes=(t2,))
                          p.op("act", lambda e: e.activation(out=t3[:, 0:n], in_=t1[:, 0:n], func=AF.Square),
                               reads=(t1,), writes=(t3,))
                          p.op("act", lambda e: e.activation(out=t3[:, 0:n], in_=t3[:, 0:n], func=AF.Sqrt,
                                                             scale=-1.0, bias=1.0), reads=(t3,), writes=(t3,))
                          p.op("dve", lambda e: e.tensor_tensor(out=t2[:, 0:n], in0=t2[:, 0:n], in1=t3[:, 0:n],
                                                                op=ALU.mult), reads=(t2, t3), writes=(t2,))

                      t1, t2 = gate_ab(0, uccb, ucc[j], 0, CTX)
                      t3 = t3_r.next()
                      finish_b(t1, t2, t3, ucc[j], 0, CTX)
                      p.op("dve", lambda e, t1=t1, t2=t2: e.tensor_tensor_scan(
                          out=hAc[:, :], data0=t1[:, 0:CTX], data1=t2[:, 0:CTX], initial=0.0,
                          op0=ALU.mult, op1=ALU.add), reads=(t1, t2), writes=(hAc,))
                      for ci_, (c0, n) in enumerate(chunks):
                          t1, t2 = gate_ab(0, ucb, uc[j], c0, n)
                          t3 = t3_r.next()
                          finish_b(t1, t2, t3, uc[j], c0, n)
                          init = hAc[:, CTX - 1:CTX] if ci_ == 0 else hA[:, c0 - 1:c0]
                          p.op("dve", lambda e, t1=t1, t2=t2, c0=c0, n=n, init=init: e.tensor_tensor_scan(
                              out=hA[:, c0:c0 + n], data0=t1[:, 0:n], data1=t2[:, 0:n], initial=init,
                              op0=ALU.mult, op1=ALU.add), reads=(t1, t2, hA, hAc), writes=(hA,))
                      p.op("act", lambda e, ch=ch: e.activation(out=SAbuf[:, ch:ch + 1], in_=hA[:, 1919:1920],
                                                                func=AF.Copy), reads=(hA,), writes=(SAbuf,))
                      t1, t2 = gate_ab(1, uccb, ucc[j], 0, CTX)
                      t3 = t3_r.next()
                      finish_b(t1, t2, t3, ucc[j], 0, CTX)
                      p.op("dve", lambda e, t1=t1, t2=t2: e.tensor_tensor_scan(
                          out=hBc[:, ::-1], data0=t1[:, 0:CTX][:, ::-1], data1=t2[:, 0:CTX][:, ::-1], initial=0.0,
                          op0=ALU.mult, op1=ALU.add), reads=(t1, t2), writes=(hBc,))
                      zcb = zcb_r.next()
                      p.op("dve", lambda e: e.tensor_tensor(out=hBc[:, :], in0=hBc[:, :], in1=hAc[:, :], op=ALU.add),
                           reads=(hBc, hAc), writes=(hBc,))
                      p.op("dve", lambda e, zcb=zcb: e.tensor_tensor(out=zcb[:, :], in0=hBc[:, :], in1=sgc[q][j][:, :],
                                                                    op=ALU.mult), reads=(hBc, sgc[q][j]), writes=(zcb,))
                      dma_st("sp", zcb, ZCTX[ch * 128:(ch + 1) * 128, :], zcb[:, :])
                      prev3 = None; prev4 = None; prevn = None
                      for ci_ in range(len(chunks) - 1, -1, -1):
                          c0, n = chunks[ci_]
                          t1, t2 = gate_ab(1, ucb, uc[j], c0, n)
                          t3 = t3_r.next(); t4 = t4_r.next()
                          finish_b(t1, t2, t3, uc[j], c0, n)
                          if prev4 is None:
                              p.op("dve", lambda e, t1=t1, t4=t4, n=n: e.tensor_tensor_scan(
                                  out=t4[:, 0:n][:, ::-1], data0=t1[:, 0:n][:, ::-1], data1=zeros[:, 0:n], initial=1.0,
                                  op0=ALU.mult, op1=ALU.add), reads=(t1, zeros), writes=(t4,))
                              p.op("dve", lambda e, t1=t1, t2=t2, t3=t3, n=n: e.tensor_tensor_scan(
                                  out=t3[:, 0:n][:, ::-1], data0=t1[:, 0:n][:, ::-1], data1=t2[:, 0:n][:, ::-1],
                                  initial=0.0, op0=ALU.mult, op1=ALU.add), reads=(t1, t2), writes=(t3,))
                          else:
                              p.op("dve", lambda e, t1=t1, t4=t4, n=n, st4=st4: e.tensor_tensor_scan(
                                  out=t4[:, 0:n][:, ::-1], data0=t1[:, 0:n][:, ::-1], data1=zeros[:, 0:n],
                                  initial=st4[:, 0:1], op0=ALU.mult, op1=ALU.add), reads=(t1, zeros, st4), writes=(t4,))
                              p.op("dve", lambda e, t1=t1, t2=t2, t3=t3, n=n, st3=st3: e.tensor_tensor_scan(
                                  out=t3[:, 0:n][:, ::-1], data0=t1[:, 0:n][:, ::-1], data1=t2[:, 0:n][:, ::-1],
                                  initial=st3[:, 0:1], op0=ALU.mult, op1=ALU.add), reads=(t1, t2, st3), writes=(t3,))
                          st3 = st3_r.next()
                          st4 = st4_r.next()
                          p.op("act", lambda e, t3=t3, st3=st3: e.activation(out=st3[:, :], in_=t3[:, 0:1], func=AF.Copy),
                               reads=(t3,), writes=(st3,))
                          p.op("act", lambda e, t4=t4, st4=st4: e.activation(out=st4[:, :], in_=t4[:, 0:1], func=AF.Copy),
                               reads=(t4,), writes=(st4,))
                          prev4 = t4
                          p.op("dve", lambda e, t3=t3, c0=c0, n=n: e.tensor_tensor(
                              out=t3[:, 0:n], in0=t3[:, 0:n], in1=hA[:, c0:c0 + n], op=ALU.add),
                              reads=(t3, hA), writes=(t3,))
                          p.op("dve", lambda e, t3=t3, c0=c0, n=n: e.tensor_tensor(
                              out=t3[:, 0:n], in0=t3[:, 0:n], in1=sg[q][j][:, c0:c0 + n], op=ALU.mult),
                              reads=(t3, sg[q][j]), writes=(t3,))
                          p.op("dve", lambda e, t4=t4, c0=c0, n=n: e.tensor_tensor(
                              out=t4[:, 0:n], in0=t4[:, 0:n], in1=sg[q][j][:, c0:c0 + n], op=ALU.mult),
                              reads=(t4, sg[q][j]), writes=(t4,))
                          dma_st("sp", t3, Z0[ch * 128:(ch + 1) * 128, c0:c0 + n], t3[:, 0:n])
                          dma_st("sp", t4, ZC[ch * 128:(ch + 1) * 128, c0:c0 + n], t4[:, 0:n])
                  for j_ in range(2):
                      half_body(j_)
              for gbi_ in range(16):
                  gb_body(gbi_)
              p.op("pool", lambda e: e.dma_start(out=gin[:, :], in_=SAbuf[:, :]), reads=(SAbuf,), dma=SAbuf)
              p.barrier()
              p.cnt[ccsem] += 1

              def cc(e):
                  return e.collective_compute("AllGather", ALU.bypass,
                                              replica_groups=[[0, 1], [2, 3], [4, 5], [6, 7]],
                                              ins=[gin.ap().opt()], outs=[gout.ap().opt()])
              p.prog["pool"].append(([], cc, (ccsem, 1)))
              p.prog["pool"].append(([(ccsem, p.cnt[ccsem])], None, None))
              p.seen["pool"][ccsem] = p.cnt[ccsem]
              dma_ld("pool", gsb, gsb[:, :, :], gout.ap().rearrange("(r p) k -> p r k", p=128))
              p.op("dve", lambda e: e.tensor_scalar(out=Sin[:, :], in0=gsb[:, 0, :], scalar1=sel[:, 0:1], scalar2=None,
                                                     op0=ALU.mult), reads=(gsb, sel), writes=(Sin,))
              p.op("dve", lambda e: e.scalar_tensor_tensor(out=Sin[:, :], in0=gsb[:, 1, :], scalar=sel[:, 1:2],
                                                            in1=Sin[:, :], op0=ALU.mult, op1=ALU.add),
                   reads=(gsb, sel, Sin), writes=(Sin,))
              p.end_stage()

        def stage_out(l, w_out_ap, blocks):
            with contextlib.ExitStack() as st:
                zT_r = Ring([p.sb(st, "zT%d" % i, [128, KC, 512], BF16, dma=True) for i in range(2)])
                Wo_r = Ring([p.sb(st, "Wo%d" % i, [128, KC, 512], BF16, dma=True) for i in range(2)])
                z0_r = Ring([p.sb(st, "z0_%d" % i, [128, 512], F32, dma=True) for i in range(3)])
                zc_r = Ring([p.sb(st, "zc_%d" % i, [128, 512], F32, dma=True) for i in range(3)])
                gbc = [p.sb(st, "gbc%d" % r, [128, D], F32, dma=True) for r in range(2)]
                xc_r = Ring([p.sb(st, "xc%d" % i, [128, 512], F32, dma=True) for i in range(3)])
                yo_r = Ring([p.sb(st, "yo%d" % i, [128, 512], F32, dma=True) for i in range(3)])
                po_r = Ring([p.ps(st, "po%d" % i, [128, 512], F32) for i in range(4)])
                for r in range(2):
                    dma_ld("sp", gbc[r], gbc[r][:, :], grow[l, r].partition_broadcast(128))
                for blk in blocks:
                    n = blk["n"]; t0 = blk["t0"]
                    zT = zT_r.next()
                    if blk["zmode"] == "corr":
                        for c in range(KC):
                            z0 = z0_r.next(); zc = zc_r.next()
                            dma_ld("sp", z0, z0[:, 0:n], Z0[c * 128:(c + 1) * 128, t0:t0 + n])
                            dma_ld("sp", zc, zc[:, 0:n], ZC[c * 128:(c + 1) * 128, t0:t0 + n])
                            p.op("dve", lambda e, z0=z0, zc=zc, zT=zT, c=c, n=n: e.scalar_tensor_tensor(
                                out=zT[:, c, 0:n], in0=zc[:, 0:n], scalar=Sin[:, c:c + 1], in1=z0[:, 0:n],
                                op0=ALU.mult, op1=ALU.add), reads=(z0, zc, Sin), writes=(zT,))
                    else:
                        zsrc = blk["zsrc"]
                        dma_ld("sp", zT, zT[:, :, 0:n], zsrc.rearrange("(k p) t -> p k t", p=128)[:, :, t0:t0 + n])
                    for nb in range(8):
                        Wo = Wo_r.next()
                        dma_ld("pool", Wo, Wo[:, :, :],
                               w_out_ap[:, nb * 512:(nb + 1) * 512].rearrange("(k p) n -> p k n", p=128))
                        for tt in range(n // 128):
                            po = po_r.next(); xc = xc_r.next(); yo = yo_r.next()
                            r0 = t0 + tt * 128

                            def mm(e, zT=zT, Wo=Wo, po=po, tt=tt):
                                ins = None
                                for k in range(KC):
                                    ins = e.matmul(po[:, :], lhsT=zT[:, k, tt * 128:(tt + 1) * 128], rhs=Wo[:, k, :],
                                                   start=(k == 0), stop=(k == KC - 1))
                                return ins
                            p.op("pe", mm, reads=(zT, Wo), writes=(po,))
                            dma_ld("sp", xc, xc[:, :], blk["xsrc"][r0:r0 + 128, nb * 512:(nb + 1) * 512])
                            g = gbc[blk["grow"]]
                            p.op("dve", lambda e, po=po, yo=yo, g=g, nb=nb: e.tensor_tensor(
                                out=yo[:, :], in0=po[:, :], in1=g[:, nb * 512:(nb + 1) * 512], op=ALU.mult),
                                reads=(po, g), writes=(yo,))
                            p.op("dve", lambda e, yo=yo, xc=xc: e.tensor_tensor(
                                out=yo[:, :], in0=yo[:, :], in1=xc[:, :], op=ALU.add), reads=(yo, xc), writes=(yo,))
                            dma_st("sp", yo, blk["dst"][r0:r0 + 128, nb * 512:(nb + 1) * 512], yo[:, :])
                p.end_stage()

        blocks0 = [dict(zmode="ctx", zsrc=ZCTX, t0=0, n=CTX, xsrc=ctx_loc, dst=CTX1, grow=1)]
        for t0, n in [(0, 512), (512, 512), (1024, 512), (1536, 512), (2048, 128)]:
            blocks0.append(dict(zmode="corr", t0=t0, n=n, xsrc=x_loc, dst=X1, grow=0))
        if nstage >= 4 and only is None:
            stage_out(0, rg_w_out, blocks0)
        if (nstage >= 5 and only is None) or (only is not None and 'N1' in only):
            stage_norm(1, X1, 17, CTX1)

        with contextlib.ExitStack() as st:
          if nstage >= 6:
              Wr = Ring([p.sb(st, "Wq%d" % i, [128, KC, 128], BF16, dma=True) for i in range(6)])
              hb_r = Ring([p.sb(st, "ha%d" % i, [128, KC, 512], BF16, dma=True) for i in range(2)])
              QT = p.sb(st, "QT", [128, 4, NOWN], BF16)
              KT = p.sb(st, "KT", [128, NH + CTX], BF16)
              Vt = p.sb(st, "Vt", [128, 19, 128], BF16)
              OT = p.sb(st, "OT", [128, 4, NOWN], BF16)
              cosT = p.sb(st, "cosT", [128, NH], F32, dma=True)
              sinT = p.sb(st, "sinT", [128, NH], F32, dma=True)
              rotm = p.sb(st, "rotm", [128, 128], F32, dma=True)
              ones_f = p.sb(st, "ones_f", [128, 128], F32)
              ones_b = p.sb(st, "ones_b", [128, 128], BF16)
              qkn = p.sb(st, "qkn", [128, 2], F32, dma=True)
              esink = p.sb(st, "esink", [128, 32], F32, dma=True)
              masks = p.sb(st, "masks", [128, 2, 512], BF16, dma=True)
              sq_r = Ring([p.sb(st, "sq%d" % i, [128, 512], F32) for i in range(1)])
              qr_r = Ring([p.sb(st, "qr%d" % i, [128, 512], F32) for i in range(2)])
              rs_r = Ring([p.sb(st, "rsa%d" % i, [128, 512], F32) for i in range(2)])
              tq_r = Ring([p.sb(st, "tq%d" % i, [128, 512], F32) for i in range(1)])
              vb_r = Ring([p.sb(st, "vb%d" % i, [128, 512], BF16) for i in range(2)])
              PT_r = Ring([p.sb(st, "PT%d" % i, [128, 512], BF16) for i in range(3)])
              den_r = Ring([p.sb(st, "den%d" % i, [128, 512], F32) for i in range(1)])
              zt_r = Ring([p.sb(st, "zt%d" % i, [128, 512], BF16, dma=True) for i in range(3)])
              gs_r = Ring([p.sb(st, "gs%d" % i, [128, 512], F32) for i in range(2)])
              pp_r = Ring([p.ps(st, "pa%d" % i, [128, 512], F32) for i in range(2)])
              px_r = Ring([p.ps(st, "px%d" % i, [128, 512], F32) for i in range(2)])
              pS_r = Ring([p.ps(st, "pS%d" % i, [128, 512], F32) for i in range(2)])
              pO = p.ps(st, "pO", [128, 512], F32)
              pR = p.ps(st, "pR", [128, 512], F32)

              dma_ld("sp", cosT, cosT[:, :], cos_fm[:, 0:NH])
              dma_ld("sp", sinT, sinT[:, :], sin_fm[:, 0:NH])
              dma_ld("sp", rotm, rotm[:, :], rot_m)
              dma_ld("sp", qkn, qkn[:, :], qk_norm_fm)
              dma_ld("sp", esink, esink[:, :], sink_bc)
              p.op("act", lambda e: e.activation(out=esink[:, :], in_=esink[:, :], func=AF.Exp),
                   reads=(esink,), writes=(esink,))
              p.op("pool", lambda e: e.dma_start(out=masks[:, :, :], in_=mask_in.rearrange("m p n -> p m n")),
                   writes=(masks,), dma=masks)
              p.op("dve", lambda e: e.memset(ones_f[:, :], 1.0), writes=(ones_f,))
              p.op("dve", lambda e: e.memset(ones_b[:, :], 1.0), writes=(ones_b,))

              hT_v = hT.rearrange("k p t -> p k t")
              hTc_v = hTc.rearrange("k p t -> p k t")
              SCALE = 128.0 ** -0.5

              def project(cols, seqs, evac):
                  Ws = []
                  for c0 in cols:
                      W = Wr.next()
                      dma_ld("pool", W, W[:, :, :], at_w_in[:, c0:c0 + 128].rearrange("(k p) n -> p k n", p=128))
                      Ws.append(W)
                  for kind, t0, n in seqs:
                      hb = hb_r.next()
                      srcv = hTc_v[:, :, 0:CTX] if kind == "c" else hT_v[:, :, t0:t0 + n]
                      dma_ld("sp", hb, hb[:, :, 0:n], srcv)
                      for ci in range(len(cols)):
                          if not evac(ci, kind, t0, n, None):
                              continue
                          pp = pp_r.next()

                          def mm(e, W=Ws[ci], hb=hb, pp=pp, n=n):
                              ins = None
                              for k in range(KC):
                                  ins = e.matmul(pp[:, 0:n], lhsT=W[:, k, :], rhs=hb[:, k, 0:n],
                                                 start=(k == 0), stop=(k == KC - 1))
                              return ins
                          p.op("pe", mm, reads=(Ws[ci], hb), writes=(pp,))
                          evac(ci, kind, t0, n, pp)

              def norm_rope(pp, n, wcol, rope_t0, dst_tk, dst_ap):
                  sq = sq_r.next(); qr = qr_r.next(); rs = rs_r.next(); tq = tq_r.next()
                  px = px_r.next()
                  p.op("act", lambda e: e.activation(out=sq[:, 0:n], in_=pp[:, 0:n], func=AF.Square),
                       reads=(pp,), writes=(sq,))
                  p.op("act", lambda e: e.activation(out=qr[:, 0:n], in_=pp[:, 0:n], func=AF.Copy),
                       reads=(pp,), writes=(qr,))
                  p.op("pe", lambda e: e.matmul(px[:, 0:n], lhsT=ones_f[:, :], rhs=sq[:, 0:n], start=True, stop=True),
                       reads=(ones_f, sq), writes=(px,))
                  p.op("dve", lambda e: e.tensor_scalar(out=rs[:, 0:n], in0=px[:, 0:n], scalar1=1.0 / 128, scalar2=EPS,
                                                         op0=ALU.mult, op1=ALU.add), reads=(px,), writes=(rs,))
                  p.op("act", lambda e: e.activation(out=rs[:, 0:n], in_=rs[:, 0:n], func=AF.Sqrt),
                       reads=(rs,), writes=(rs,))
                  p.op("dve", lambda e: e.reciprocal(out=rs[:, 0:n], in_=rs[:, 0:n]), reads=(rs,), writes=(rs,))
                  if rope_t0 is None:
                      p.op("dve", lambda e: e.scalar_tensor_tensor(
                          out=dst_ap, in0=qr[:, 0:n], scalar=qkn[:, wcol:wcol + 1], in1=rs[:, 0:n],
                          op0=ALU.mult, op1=ALU.mult), reads=(qr, qkn, rs), writes=(dst_tk,))
                      return
                  p.op("dve", lambda e: e.scalar_tensor_tensor(
                      out=qr[:, 0:n], in0=qr[:, 0:n], scalar=qkn[:, wcol:wcol + 1], in1=rs[:, 0:n],
                      op0=ALU.mult, op1=ALU.mult), reads=(qr, qkn, rs), writes=(qr,))
                  px2 = px_r.next()
                  p.op("pe", lambda e: e.matmul(px2[:, 0:n], lhsT=rotm[:, :], rhs=qr[:, 0:n], start=True, stop=True),
                       reads=(rotm, qr), writes=(px2,))
                  p.op("dve", lambda e: e.tensor_tensor(out=tq[:, 0:n], in0=px2[:, 0:n],
                                                         in1=sinT[:, rope_t0:rope_t0 + n], op=ALU.mult),
                       reads=(px2, sinT), writes=(tq,))
                  p.op("dve", lambda e: e.tensor_tensor(out=qr[:, 0:n], in0=qr[:, 0:n],
                                                         in1=cosT[:, rope_t0:rope_t0 + n], op=ALU.mult),
                       reads=(qr, cosT), writes=(qr,))
                  p.op("dve", lambda e: e.tensor_tensor(out=dst_ap, in0=qr[:, 0:n], in1=tq[:, 0:n], op=ALU.add),
                       reads=(qr, tq), writes=(dst_tk,))

              own_blocks = [("l", 0, 512), ("l", 512, 512), ("l", 1024, 512), ("l", 1536, 512)]
              seqsA = [("c", 0, CTX)] + own_blocks + [("l", 2048, 128)]

              for h in range(8 if asub is None else asub.get('nh', 8)):
                  colsA = [D + h * 128, D + 1024 + h * 128] + [h * 512 + g * 128 for g in range(4)]

                  def evacA(ci, kind, t0, n, pp, h=h):
                      if ci >= 2 and (kind == "c" or t0 >= NOWN):
                          return False
                      if pp is None:
                          return True
                      if asub is not None and asub.get('simple', 0):
                          vb = vb_r.next()
                          p.op("act", lambda e: e.activation(out=vb[:, 0:n], in_=pp[:, 0:n], func=AF.Copy),
                               reads=(pp,), writes=(vb,))
                          return True
                      if ci == 0:
                          if kind == "c":
                              norm_rope(pp, n, 1, None, KT, KT[:, NH:NH + CTX])
                          else:
                              norm_rope(pp, n, 1, t0, KT, KT[:, t0:t0 + n])
                      elif ci == 1:
                          vb = vb_r.next()
                          p.op("act", lambda e: e.activation(out=vb[:, 0:n], in_=pp[:, 0:n], func=AF.Copy),
                               reads=(pp,), writes=(vb,))
                          px = px_r.next()
                          pxb = px[:, :].bitcast(BF16)

                          def tr(e):
                              ins = None
                              for i in range(n // 128):
                                  ins = e.transpose(out=pxb[:, i * 128:(i + 1) * 128], in_=vb[:, i * 128:(i + 1) * 128],
                                                    identity=ident_bf[:, :])
                              return ins
                          p.op("pe", tr, reads=(vb, ident_bf), writes=(px,))
                          kb0 = 17 if kind == "c" else t0 // 128
                          nb_ = n // 128
                          p.op("act", lambda e: e.activation(
                              out=Vt[:, kb0:kb0 + nb_, :],
                              in_=pxb[:, 0:nb_ * 128].rearrange("p (b d) -> p b d", d=128), func=AF.Copy),
                              reads=(px,), writes=(Vt,))
                      else:
                          g = ci - 2
                          norm_rope(pp, n, 0, t0, QT, QT[:, g, t0:t0 + n])
                      return True
                  project(colsA, seqsA, evacA)

                  for i in range(16 if asub is None else asub.get('nq', 16)):
                      kbs = [("c", 17, NH), ("c", 18, NH + 128)]
                      if i > 0:
                          kbs.append(("p", i - 1, (i - 1) * 128))
                      kbs.append(("o", i, i * 128))
                      kbs.append(("n", i + 1, (i + 1) * 128))
                      for ki, (kk, vb_i, kc0) in enumerate(kbs):
                          pS = pS_r.next(); PT = PT_r.next()
                          p.op("pe", lambda e, pS=pS, kc0=kc0, i=i: e.matmul(
                              pS[:, :].rearrange("p (g q) -> p g q", g=4), lhsT=KT[:, kc0:kc0 + 128],
                              rhs=QT[:, :, i * 128:(i + 1) * 128], start=True, stop=True),
                              reads=(KT, QT), writes=(pS,))
                          p.op("act", lambda e, pS=pS, PT=PT: e.activation(out=PT[:, :], in_=pS[:, :], func=AF.Exp,
                                                                           scale=SCALE), reads=(pS,), writes=(PT,))
                          if kk in ("p", "n"):
                              mi = 0 if kk == "p" else 1
                              p.op("dve", lambda e, PT=PT, mi=mi: e.tensor_tensor(
                                  out=PT[:, :], in0=PT[:, :], in1=masks[:, mi, :], op=ALU.mult),
                                  reads=(PT, masks), writes=(PT,))

                          def pv(e, PT=PT, vb_i=vb_i, ki=ki, last=(ki == len(kbs) - 1)):
                              e.matmul(pO[:, :], lhsT=Vt[:, vb_i, :], rhs=PT[:, :], start=(ki == 0), stop=last)
                              return e.matmul(pR[:, :], lhsT=ones_b[:, :], rhs=PT[:, :], start=(ki == 0), stop=last)
                          p.op("pe", pv, reads=(Vt, PT, ones_b), writes=(pO, pR))
                      den = den_r.next()
                      for g in range(4):
                          p.op("dve", lambda e, den=den, g=g, h=h: e.tensor_scalar(
                              out=den[:, g * 128:(g + 1) * 128], in0=pR[:, g * 128:(g + 1) * 128],
                              scalar1=esink[:, h * 4 + g:h * 4 + g + 1], scalar2=None, op0=ALU.add),
                              reads=(pR, esink), writes=(den,))
                      p.op("dve", lambda e, den=den: e.reciprocal(out=den[:, :], in_=den[:, :]),
                           reads=(den,), writes=(den,))
                      p.op("dve", lambda e, den=den, i=i: e.tensor_tensor(
                          out=OT[:, :, i * 128:(i + 1) * 128], in0=pO[:, :].rearrange("p (g q) -> p g q", g=4),
                          in1=den[:, :].rearrange("p (g q) -> p g q", g=4), op=ALU.mult),
                          reads=(pO, den), writes=(OT,))

                  colsB = [6144 + h * 512 + g * 128 for g in range(4)]

                  def evacB(ci, kind, t0, n, pp, h=h):
                      if pp is None:
                          return True
                      gs = gs_r.next(); zt = zt_r.next()
                      p.op("act", lambda e: e.activation(out=gs[:, 0:n], in_=pp[:, 0:n], func=AF.Silu),
                           reads=(pp,), writes=(gs,))
                      p.op("dve", lambda e: e.tensor_tensor(out=zt[:, 0:n], in0=gs[:, 0:n], in1=OT[:, ci, t0:t0 + n],
                                                             op=ALU.mult), reads=(gs, OT), writes=(zt,))
                      r0 = (h * 4 + ci) * 128
                      dma_st("sp", zt, Z1T[r0:r0 + 128, t0:t0 + n], zt[:, 0:n])
                      return True
                  if asub is None or asub.get('pb', 1):
                      project(colsB, own_blocks, evacB)
              p.end_stage()

        blocks1 = [dict(zmode="direct", zsrc=Z1T, t0=t0, n=512, xsrc=X1, dst=out, grow=0)
                   for t0 in (0, 512, 1024, 1536)]
        if nstage >= 7:
            stage_out(1, at_w_out, blocks1)

        p.check()
        p.emit()
    return nc


def _fm(v):
    v = np.asarray(v, np.float32)
    lead = v.shape[:-1]
    a = v.reshape(lead + (KC, 128))
    a = np.moveaxis(a, -1, 0)
    return np.ascontiguousarray(a)


def prepare_inputs(x, c, ctx, c_ctx, w_mod, b_mod, norm_w, rg_w_in, rg_conv_w, rg_conv_b, rg_w_r, rg_b_r,
                   rg_w_i, rg_b_i, rg_lam, rg_w_out, at_w_in, at_q_norm, at_k_norm, at_sink, at_w_out):
    f32 = np.float32
    shared = {}
    shared["w_mod"] = np.ascontiguousarray(w_mod, f32)
    shared["bmod_fm"] = np.ascontiguousarray(
        np.asarray(b_mod, f32).reshape(2, 96, 128).transpose(2, 0, 1))
    shared["normw_fm"] = _fm(norm_w)
    shared["rg_w_in"] = np.ascontiguousarray(rg_w_in[0], f32)
    shared["convb_fm"] = _fm(rg_conv_b[0])
    shared["rg_w_out"] = np.ascontiguousarray(rg_w_out[0], f32)
    shared["at_w_in"] = np.ascontiguousarray(at_w_in[0], f32)
    shared["qk_norm_fm"] = np.ascontiguousarray(np.stack([at_q_norm[0], at_k_norm[0]], axis=1), f32)
    shared["sink_bc"] = np.ascontiguousarray(np.broadcast_to(np.asarray(at_sink[0], f32)[None, :], (128, 32)))
    shared["at_w_out"] = np.ascontiguousarray(at_w_out[0], f32)
    ident = np.eye(128, dtype=f32)
    shared["ident_in"] = ident
    rot = np.zeros((128, 128), f32)
    for m in range(128):
        if (m % 64) < 32:
            rot[m + 32, m] = -1.0
        else:
            rot[m - 32, m] = 1.0
    shared["rot_m"] = rot
    kj = np.arange(128)[:, None]
    qi = np.arange(128)[None, :]
    mprev = (kj >= qi).astype(f32)
    mnext = (kj <= qi).astype(f32)
    shared["mask_in"] = np.ascontiguousarray(np.stack([np.tile(mprev, (1, 4)), np.tile(mnext, (1, 4))], 0))

    conv_w = np.asarray(rg_conv_w[0], f32)
    zero = np.zeros((1, D), f32)
    per_core = []
    for core in range(8):
        b, half = core // 2, core % 2
        m = dict(shared)
        if half == 0:
            idx = np.arange(NLOC)
            m["ctx_loc"] = np.ascontiguousarray(ctx[b], f32)
            conv5 = np.concatenate([conv_w, zero], 0)
        else:
            idx = 4095 - np.arange(NLOC)
            m["ctx_loc"] = np.ascontiguousarray(np.asarray(ctx[b], f32)[::-1])
            conv5 = np.concatenate([zero, conv_w[::-1]], 0)
        m["x_loc"] = np.ascontiguousarray(np.asarray(x[b], f32)[idx])
        m["conv5_fm"] = np.ascontiguousarray(np.moveaxis(_fm(conv5), 1, 2))
        m["c_fm"] = np.ascontiguousarray(np.moveaxis(_fm(np.stack([c[b], c_ctx], 0)), 1, 2))
        dA, dB = half, 1 - half
        m["gate_w"] = np.ascontiguousarray(np.stack([rg_w_r[0, dA], rg_w_i[0, dA], rg_w_r[0, dB], rg_w_i[0, dB]], 0), f32)
        m["gate_b_fm"] = _fm(np.stack([rg_b_r[0, dA], rg_b_i[0, dA], rg_b_r[0, dB], rg_b_i[0, dB]], 0))
        m["lam_fm"] = _fm(np.stack([rg_lam[0, dA], rg_lam[0, dB]], 0))
        t = idx.astype(np.float64)
        row = np.floor(t / 64.0)
        col = t - row * 64.0
        inv = 10000.0 ** (-np.arange(32, dtype=np.float64) * (2.0 / 64.0))
        dd = np.arange(128)
        pos = np.where(dd[:, None] < 64, row[None, :], col[None, :])
        ang = pos * inv[dd % 32][:, None]
        m["cos_fm"] = np.ascontiguousarray(np.cos(ang), f32)
        m["sin_fm"] = np.ascontiguousarray(np.sin(ang), f32)
        selv = np.zeros((128, 2), f32)
        selv[:, 1 - half] = 1.0
        m["sel_in"] = selv
        per_core.append(m)
    return per_core


_NC_CACHE = {}


def kernel(**inputs):
    inputs = {k: np.asarray(v) for k, v in inputs.items()}
    per_core = prepare_inputs(**inputs)
    if "nc" not in _NC_CACHE:
        _NC_CACHE["nc"] = build_program(DEBUG)
    nc = _NC_CACHE["nc"]
    res = run_bass_kernel_spmd(nc, per_core, core_ids=list(range(8)))
    outp = np.empty((4, 4096, D), np.float32)
    for core in range(8):
        b, half = core // 2, core % 2
        o = np.asarray(res.results[core]["out"], np.float32)
        if half == 0:
            outp[b, 0:NOWN] = o
        else:
            outp[b, NOWN:] = o[::-1]
    if DEBUG:
        kernel.last = res
    return outp
```

```python
import numpy as np
import ml_dtypes
import concourse.bass as bass
import concourse.mybir as mybir
from concourse.bass_utils import run_bass_kernel_spmd

F32 = mybir.dt.float32
BF16 = mybir.dt.bfloat16
ALU = mybir.AluOpType
AF = mybir.ActivationFunctionType

D = 4096
KC = 32
NOWN = 2048
NH = 2176
NU = 2178
NLOC = 2304
CTX = 256
EPS = 1e-6
ENG = ("pe", "act", "dve", "pool", "sp")
DEBUG = False
REUSE_DSEMS = True


class Tk:
    __slots__ = ("ap", "lw", "rd", "dsem", "const")

    def __init__(self, ap=None, dsem=None, const=False):
        self.ap = ap
        self.lw = None
        self.rd = {}
        self.dsem = dsem
        self.const = const

    def __getitem__(self, k):
        return self.ap[k]


class Prog:
    def __init__(self, nc, stack):
        self.nc = nc
        self.stack = stack
        self.prog = {e: [] for e in ENG}
        self.cnt = {}
        self.seen = {e: {} for e in ENG}
        self.esem = {}
        for e in ENG:
            s = stack.enter_context(nc.semaphore("es_" + e))
            self.esem[e] = s
            self.cnt[s] = 0
        self.free_dsems = []
        self.ndsem = 0
        self.stage_dsems = []

    def dsem(self):
        if self.free_dsems:
            s = self.free_dsems.pop()
        else:
            s = self.stack.enter_context(self.nc.semaphore("ds%d" % self.ndsem))
            self.ndsem += 1
            self.cnt[s] = 0
        self.stage_dsems.append(s)
        return s

    def sb(self, st, name, shape, dt, dma=False, const=False):
        self.uid = getattr(self, "uid", 0) + 1
        name = "%s_u%d" % (name, self.uid)
        t = st.enter_context(self.nc.sbuf_tensor(name, list(shape), dt))
        return Tk(t, self.dsem() if dma else None, const)

    def ps(self, st, name, shape, dt):
        self.uid = getattr(self, "uid", 0) + 1
        name = "%s_u%d" % (name, self.uid)
        t = st.enter_context(self.nc.psum_tensor(name, list(shape), dt))
        return Tk(t)

    def dram(self, ap=None):
        return Tk(ap)

    def op(self, eng, fn, reads=(), writes=(), dma=None):
        waits = {}
        seen = self.seen[eng]

        def need(tok):
            if tok is None:
                return
            sem, val = tok
            if seen.get(sem, 0) >= val:
                return
            if waits.get(sem, 0) < val:
                waits[sem] = val

        for t in reads:
            need(t.lw)
        for t in writes:
            need(t.lw)
            for sem, val in t.rd.items():
                need((sem, val))
        if eng == "pool" and dma is not None:
            hist = self.__dict__.setdefault("pool_hist", [])
            if len(hist) >= 3:
                need(hist[-3])
        for sem, val in waits.items():
            seen[sem] = val
        if dma is not None:
            sem = dma.dsem
            self.cnt[sem] += 16
            inc = (sem, 16)
        else:
            sem = self.esem[eng]
            self.cnt[sem] += 1
            inc = (sem, 1)
        tok = (sem, self.cnt[sem])
        if eng == "pool" and dma is not None:
            self.pool_hist.append(tok)
        self.prog[eng].append((list(waits.items()), fn, inc))
        for t in reads:
            if not t.const:
                if t.rd.get(sem, 0) < tok[1]:
                    t.rd[sem] = tok[1]
        for t in writes:
            t.lw = tok
            t.rd = {}
        return tok

    def barrier(self):
        for e in ENG:
            waits = []
            for sem, c in self.cnt.items():
                if c > 0 and self.seen[e].get(sem, 0) < c and sem is not self.esem[e]:
                    waits.append((sem, c))
                    self.seen[e][sem] = c
            self.prog[e].append((waits, None, None))

    def end_stage(self):
        self.barrier()
        if REUSE_DSEMS:
            self.free_dsems.extend(self.stage_dsems)
        self.stage_dsems = []

    def check(self):
        pos = {e: 0 for e in ENG}
        val = {}
        progress = True
        while progress:
            progress = False
            for e in ENG:
                lst = self.prog[e]
                while pos[e] < len(lst):
                    waits, fn, inc = lst[pos[e]]
                    if any(val.get(sem, 0) < v for sem, v in waits):
                        break
                    if inc is not None:
                        val[inc[0]] = val.get(inc[0], 0) + inc[1]
                    pos[e] += 1
                    progress = True
        stuck = {e: (pos[e], len(self.prog[e])) for e in ENG if pos[e] < len(self.prog[e])}
        for e in stuck:
            waits, fn, inc = self.prog[e][pos[e]]
            print("STUCK", e, stuck[e], [(str(sem), v, val.get(sem, 0)) for sem, v in waits])
        bad = {str(sem): (val.get(sem, 0), c) for sem, c in self.cnt.items() if val.get(sem, 0) != c}
        print("check: stuck=%s mismatched=%s" % (bool(stuck), bad))

    def emit(self):
        nc = self.nc
        prog = self.prog

        def replay(lst, e):
            for waits, fn, inc in lst:
                for sem, val in waits:
                    e.wait_ge(sem, val)
                if fn is not None:
                    ins = fn(e)
                    ins.then_inc(inc[0], inc[1])

        with nc.Block() as block:
            @block.tensor
            def _(e):
                replay(prog["pe"], e)

            @block.scalar
            def _(e):
                replay(prog["act"], e)

            @block.vector
            def _(e):
                replay(prog["dve"], e)

            @block.gpsimd
            def _(e):
                replay(prog["pool"], e)

            @block.sync
            def _(e):
                replay(prog["sp"], e)


class Ring:
    def __init__(self, tiles):
        self.tiles = tiles
        self.i = 0

    def next(self):
        t = self.tiles[self.i % len(self.tiles)]
        self.i += 1
        return t


def build_program(debug=False, stop_after=None, only=None, asub=None):
    import contextlib
    nc = bass.Bass("TRN2", target_bir_lowering=False)
    dk = "ExternalOutput" if debug else "Internal"

    NEED_A = ("at_w_in", "qk_norm_fm", "sink_bc", "cos_fm", "sin_fm", "rot_m", "ident_in", "mask_in",
              "normw_fm", "bmod_fm")

    def din(name, shape, dt=F32):
        if only is not None and name not in NEED_A:
            return None
        return nc.dram_tensor(name, list(shape), dt, kind="ExternalInput").ap()

    def dscr(name, shape, dt):
        if debug and (debug is True or name in debug):
            return nc.dram_tensor(name, list(shape), dt, kind="ExternalOutput").ap()
        return nc.dram_tensor(name, list(shape), dt).ap()

    x_loc = din("x_loc", [NLOC, D])
    ctx_loc = din("ctx_loc", [CTX, D])
    c_fm = din("c_fm", [128, KC, 2])
    w_mod = din("w_mod", [2, D, 3 * D])
    bmod_fm = din("bmod_fm", [128, 2, 96])
    normw_fm = din("normw_fm", [128, 2, KC])
    rg_w_in = din("rg_w_in", [D, 2 * D])
    conv5_fm = din("conv5_fm", [128, KC, 5])
    convb_fm = din("convb_fm", [128, KC])
    gate_w = din("gate_w", [4, 16, 256, 256])
    gate_b_fm = din("gate_b_fm", [128, 4, KC])
    lam_fm = din("lam_fm", [128, 2, KC])
    rg_w_out = din("rg_w_out", [D, D])
    at_w_in = din("at_w_in", [D, 10240])
    qk_norm_fm = din("qk_norm_fm", [128, 2])
    sink_bc = din("sink_bc", [128, 32])
    at_w_out = din("at_w_out", [D, D])
    cos_fm = din("cos_fm", [128, NLOC])
    sin_fm = din("sin_fm", [128, NLOC])
    rot_m = din("rot_m", [128, 128])
    ident_in = din("ident_in", [128, 128])
    mask_in = din("mask_in", [2, 128, 512])
    sel_in = din("sel_in", [128, 2])
    out = nc.dram_tensor("out", [NOWN, D], F32, kind="ExternalOutput").ap()

    hT = dscr("hT", [KC, 128, NLOC], BF16)
    hTc = dscr("hTc", [KC, 128, CTX], BF16)
    Z0 = dscr("Z0", [D, NH], F32)
    ZC = dscr("ZC", [D, NH], F32)
    ZCTX = dscr("ZCTX", [D, CTX], BF16)
    X1 = dscr("X1", [NH, D], F32)
    CTX1 = dscr("CTX1", [CTX, D], F32)
    Z1T = dscr("Z1T", [D, NOWN], BF16)
    grow = dscr("grow", [2, 2, D], F32)
    gin = nc.dram_tensor("gin", [128, KC], F32)
    gout = nc.dram_tensor("gout", [256, KC], F32)

    with contextlib.ExitStack() as gst:
        p = Prog(nc, gst)
        gst.enter_context(nc.allow_non_contiguous_dma(reason="small param scatter"))
        gst.enter_context(nc.allow_low_precision(reason="bf16 matmul operands"))
        ccsem = gst.enter_context(nc.semaphore("ccsem"))
        p.cnt[ccsem] = 0

        ident_bf = p.sb(gst, "ident_bf", [128, 128], BF16, dma=True)
        ident_f = p.sb(gst, "ident_f", [128, 128], F32, dma=True)
        mod = [p.sb(gst, "mod%d" % l, [128, 96, 2], F32) for l in range(2)]
        Avec = [p.sb(gst, "Avec%d" % l, [128, KC, 2], F32) for l in range(2)]
        normw = p.sb(gst, "normw", [128, 2, KC], F32, dma=True)
        bmod = p.sb(gst, "bmod", [128, 2, 96], F32, dma=True)
        Sin = p.sb(gst, "Sin", [128, KC], F32)

        def dma_ld(eng, dst_tk, dst_ap, src_ap, reads=(), extra_w=()):
            p.op(eng, lambda e: e.dma_start(out=dst_ap, in_=src_ap), reads=reads,
                 writes=(dst_tk,) + tuple(extra_w), dma=dst_tk)

        def dma_st(eng, src_tk, dst_ap, src_ap, dst_tk=None):
            p.op(eng, lambda e: e.dma_start(out=dst_ap, in_=src_ap), reads=(src_tk,),
                 writes=(dst_tk,) if dst_tk is not None else (), dma=src_tk)

        dma_ld("sp", ident_f, ident_f[:, :], ident_in)
        dma_ld("pool", ident_bf, ident_bf[:, :], ident_in)
        dma_ld("sp", normw, normw[:, :, :], normw_fm)
        dma_ld("sp", bmod, bmod[:, :, :], bmod_fm)

        with contextlib.ExitStack() as st:
          if only is None:
              cf = p.sb(st, "cf", [128, KC, 2], F32, dma=True)
              scb = p.sb(st, "scb", [128, KC, 2], BF16)
              Wm = Ring([p.sb(st, "Wm%d" % i, [128, KC, 512], BF16, dma=True) for i in range(2)])
              psm = [p.ps(st, "psm%d" % l, [128, 512], F32) for l in range(2)]
              dma_ld("sp", cf, cf[:, :, :], c_fm)
              p.op("act", lambda e: e.activation(out=scb[:, :, :], in_=cf[:, :, :], func=AF.Silu),
                   reads=(cf,), writes=(scb,))
              for l in range(2):
                  for blk in range(24):
                      W = Wm.next()
                      src = w_mod[l, :, blk * 512:(blk + 1) * 512].rearrange("(k p) n -> p k n", p=128)
                      dma_ld("pool", W, W[:, :, :], src)

                      def mm(e, W=W, blk=blk, l=l):
                          ins = None
                          for j in range(4):
                              n = blk * 4 + j
                              for k in range(KC):
                                  ins = e.matmul(psm[l][:, n * 2:n * 2 + 2], lhsT=W[:, k, j * 128:(j + 1) * 128],
                                                 rhs=scb[:, k, :], start=(k == 0), stop=(k == KC - 1))
                          return ins
                      p.op("pe", mm, reads=(W, scb), writes=(psm[l],))
                  for r in range(2):
                      p.op("dve", lambda e, l=l, r=r: e.tensor_tensor(
                          out=mod[l][:, :, r], in0=psm[l][:, 0:192].rearrange("p (n r) -> p n r", r=2)[:, :, r],
                          in1=bmod[:, l, :], op=ALU.add), reads=(psm[l], bmod), writes=(mod[l],))
                  for r in range(2):
                      p.op("dve", lambda e, l=l, r=r: e.scalar_tensor_tensor(
                          out=Avec[l][:, :, r], in0=mod[l][:, 32:64, r], scalar=1.0, in1=normw[:, l, :],
                          op0=ALU.add, op1=ALU.mult), reads=(mod[l], normw), writes=(Avec[l],))
                      p.op("sp", lambda e, l=l, r=r: e.dma_start(
                          out=grow[l, r].rearrange("(k p) -> p k", p=128), in_=mod[l][:, 64:96, r]),
                          reads=(mod[l],), writes=(), dma=cf)
              p.end_stage()

        def stage_norm(l, src_lat, ntiles, src_ctx):
            with contextlib.ExitStack() as st:
                xt_r = Ring([p.sb(st, "xt%d" % i, [128, D], F32, dma=True) for i in range(2)])
                xn_r = Ring([p.sb(st, "xn%d" % i, [128, D], BF16) for i in range(2)])
                junk = p.sb(st, "junk", [128, D], BF16)
                ss_r = Ring([p.sb(st, "ss%d" % i, [128, 1], F32) for i in range(4)])
                rs_r = Ring([p.sb(st, "rs%d" % i, [128, 1], F32) for i in range(4)])
                pt_r = Ring([p.ps(st, "pt%d" % i, [128, D], BF16) for i in range(2)])
                hb_r = Ring([p.sb(st, "hb%d" % i, [128, KC, 512], BF16, dma=True) for i in range(2)])
                jobs = []
                nblk = (ntiles + 3) // 4
                for b in range(nblk):
                    tl = list(range(b * 4, min(ntiles, b * 4 + 4)))
                    jobs.append((src_lat, tl, hT, b * 512, 0))
                jobs.append((src_ctx, [0, 1], hTc, 0, 1))
                for src, tl, dst, c0, r in jobs:
                    hb = hb_r.next()
                    for ti, t in enumerate(tl):
                        xt = xt_r.next(); xn = xn_r.next(); ss = ss_r.next(); rs = rs_r.next(); pt = pt_r.next()
                        dma_ld("sp", xt, xt[:, :], src[t * 128:(t + 1) * 128, :])
                        p.op("act", lambda e, xt=xt, ss=ss: e.activation(
                            out=junk[:, :], in_=xt[:, :], func=AF.Square, accum_out=ss[:, :]),
                            reads=(xt,), writes=(junk, ss))
                        p.op("dve", lambda e, ss=ss, rs=rs: e.tensor_scalar(
                            out=rs[:, :], in0=ss[:, :], scalar1=1.0 / D, scalar2=EPS, op0=ALU.mult, op1=ALU.add),
                            reads=(ss,), writes=(rs,))
                        p.op("act", lambda e, rs=rs: e.activation(out=rs[:, :], in_=rs[:, :], func=AF.Sqrt),
                             reads=(rs,), writes=(rs,))
                        p.op("dve", lambda e, rs=rs: e.reciprocal(out=rs[:, :], in_=rs[:, :]),
                            reads=(rs,), writes=(rs,))
                        p.op("dve", lambda e, xt=xt, xn=xn, rs=rs: e.tensor_scalar(
                            out=xn[:, :], in0=xt[:, :], scalar1=rs[:, 0:1], scalar2=None, op0=ALU.mult),
                            reads=(xt, rs), writes=(xn,))

                        def tr(e, xn=xn, pt=pt):
                            ins = None
                            for j in range(KC):
                                ins = e.transpose(out=pt[:, j * 128:(j + 1) * 128], in_=xn[:, j * 128:(j + 1) * 128],
                                                  identity=ident_bf[:, :])
                            return ins
                        p.op("pe", tr, reads=(xn, ident_bf), writes=(pt,))

                        def ev(e, pt=pt, hb=hb, ti=ti, r=r):
                            ins = None
                            for j in range(KC):
                                ins = e.activation(out=hb[:, j, ti * 128:(ti + 1) * 128], in_=pt[:, j * 128:(j + 1) * 128],
                                                   func=AF.Identity, scale=Avec[l][:, j, r:r + 1],
                                                   bias=mod[l][:, j, r:r + 1])
                            return ins
                        p.op("act", ev, reads=(pt, Avec[l], mod[l]), writes=(hb,))
                    nt = len(tl) * 128
                    dma_st("sp", hb, dst.rearrange("k p t -> p k t")[:, :, c0:c0 + nt], hb[:, :, 0:nt])
                p.end_stage()

        order = ["M", "N0", "R", "O0", "N1", "A", "O1"]
        nstage = len(order) if stop_after is None else order.index(stop_after) + 1
        if only is not None:
            nstage = 6 if "A" in only else 0
        if nstage >= 2 and only is None:
            stage_norm(0, x_loc, 18, ctx_loc)

        with contextlib.ExitStack() as st:
          if nstage >= 3 and only is None:
              Wr = Ring([p.sb(st, "Wp%d" % i, [128, KC, 128], BF16, dma=True) for i in range(6)])
              hb_r = Ring([p.sb(st, "hs%d" % i, [128, KC, 256], BF16, dma=True) for i in range(3)])
              u = [p.sb(st, "u%d" % j, [128, NU + 4], F32) for j in range(2)]
              ucx = [p.sb(st, "ucx%d" % j, [128, CTX + 4], F32) for j in range(2)]
              uc = [p.sb(st, "uc%d" % j, [128, NH], F32) for j in range(2)]
              ucc = [p.sb(st, "ucc%d" % j, [128, CTX], F32) for j in range(2)]
              ucb = [p.sb(st, "ucb%d" % j, [128, NH], BF16) for j in range(2)]
              uccb = [p.sb(st, "uccb%d" % j, [128, CTX], BF16) for j in range(2)]
              sg = [[p.sb(st, "sg%d_%d" % (q, j), [128, NH], BF16) for j in range(2)] for q in range(2)]
              sgc = [[p.sb(st, "sgc%d_%d" % (q, j), [128, CTX], BF16) for j in range(2)] for q in range(2)]
              hA = p.sb(st, "hA", [128, NH], F32)
              hAc = p.sb(st, "hAc", [128, CTX], F32)
              hBc = p.sb(st, "hBc", [128, CTX], F32)
              zcb_r = Ring([p.sb(st, "zcb%d" % i, [128, CTX], BF16, dma=True) for i in range(1)])
              t1_r = Ring([p.sb(st, "t1_%d" % i, [128, 512], F32) for i in range(2)])
              t2_r = Ring([p.sb(st, "t2_%d" % i, [128, 512], F32) for i in range(2)])
              t3_r = Ring([p.sb(st, "t3_%d" % i, [128, 512], F32, dma=True) for i in range(2)])
              t4_r = Ring([p.sb(st, "t4_%d" % i, [128, 512], F32, dma=True) for i in range(2)])
              zeros = p.sb(st, "zeros", [128, 512], F32, const=True)
              GW_r = Ring([p.sb(st, "GW%d" % i, [128, 4, 2, 256], BF16, dma=True) for i in range(2)])
              st3_r = Ring([p.sb(st, "st3_%d" % i, [128, 1], F32) for i in range(4)])
              st4_r = Ring([p.sb(st, "st4_%d" % i, [128, 1], F32) for i in range(4)])
              conv5 = p.sb(st, "conv5", [128, KC, 5], F32, dma=True)
              convb = p.sb(st, "convb", [128, KC], F32, dma=True)
              gb_t = p.sb(st, "gb_t", [128, 4, KC], F32, dma=True)
              lam_t = p.sb(st, "lam_t", [128, 2, KC], F32, dma=True)
              cneg = p.sb(st, "cneg", [128, 2, KC], F32)
              SAbuf = p.sb(st, "SAbuf", [128, KC], F32, dma=True)
              gsb = p.sb(st, "gsb", [128, 2, KC], F32, dma=True)
              sel = p.sb(st, "sel", [128, 2], F32, dma=True)
              pp_r = Ring([p.ps(st, "pp%d" % i, [128, 512], F32) for i in range(3)])
              pr_r = Ring([p.ps(st, "pr%d" % i, [128, 512], F32) for i in range(2)])
              pi_r = Ring([p.ps(st, "pi%d" % i, [128, 512], F32) for i in range(2)])

              dma_ld("sp", conv5, conv5[:, :, :], conv5_fm)
              dma_ld("sp", convb, convb[:, :], convb_fm)
              dma_ld("sp", gb_t, gb_t[:, :, :], gate_b_fm)
              dma_ld("sp", lam_t, lam_t[:, :, :], lam_fm)
              dma_ld("sp", sel, sel[:, :], sel_in)
              p.op("dve", lambda e: e.memset(zeros[:, :], 0.0), writes=(zeros,))
              for j in range(2):
                  p.op("dve", lambda e, j=j: e.memset(u[j][:, :], 0.0), writes=(u[j],))
                  p.op("dve", lambda e, j=j: e.memset(ucx[j][:, :], 0.0), writes=(ucx[j],))
              p.op("act", lambda e: e.activation(out=cneg[:, :, :], in_=lam_t[:, :, :], func=AF.Exp, scale=-1.0),
                   reads=(lam_t,), writes=(cneg,))
              p.op("act", lambda e: e.activation(out=cneg[:, :, :], in_=cneg[:, :, :], func=AF.Ln, bias=1.0),
                   reads=(cneg,), writes=(cneg,))
              p.op("dve", lambda e: e.tensor_scalar(out=cneg[:, :, :], in0=cneg[:, :, :], scalar1=-8.0, scalar2=None,
                                                     op0=ALU.mult), reads=(cneg,), writes=(cneg,))

              hT_v = hT.rearrange("k p t -> p k t")
              hTc_v = hTc.rearrange("k p t -> p k t")
              tblocks = [(i * 256, 256) for i in range(8)] + [(2048, 130)]
              chunks = [(0, 512), (512, 512), (1024, 512), (1536, 512), (2048, 128)]

              def proj_gen(gbi):
                  q = gbi % 2
                  cols = [gbi * 256, gbi * 256 + 128, D + gbi * 256, D + gbi * 256 + 128]
                  Ws = []
                  for c0 in cols:
                      W = Wr.next()
                      dma_ld("pool", W, W[:, :, :], rg_w_in[:, c0:c0 + 128].rearrange("(k p) n -> p k n", p=128))
                      Ws.append(W)
                  GW = GW_r.next()
                  GWs[gbi] = GW
                  for gi in range(4):
                      p.op("pool", lambda e, GW=GW, gi=gi, gbi=gbi: e.dma_start(
                          out=GW[:, gi, :, :], in_=gate_w[gi, gbi].rearrange("(kh p) j -> p kh j", p=128)),
                          writes=(GW,), dma=GW)
                  seqs = [("c", 0, CTX)] + [("l", t0, n) for t0, n in tblocks]
                  for kind, t0, n in seqs:
                      hb = hb_r.next()
                      srcv = hTc_v[:, :, 0:CTX] if kind == "c" else hT_v[:, :, t0:t0 + n]
                      dma_ld("sp", hb, hb[:, :, 0:n], srcv)
                      for ci in range(4):
                          pp = pp_r.next()

                          def mm(e, W=Ws[ci], hb=hb, pp=pp, n=n):
                              ins = None
                              for k in range(KC):
                                  ins = e.matmul(pp[:, 0:n], lhsT=W[:, k, :], rhs=hb[:, k, 0:n],
                                                 start=(k == 0), stop=(k == KC - 1))
                              return ins
                          p.op("pe", mm, reads=(Ws[ci], hb), writes=(pp,))
                          j = ci % 2
                          if ci < 2:
                              if kind == "c":
                                  dstt, dsta = ucx[j], ucx[j][:, 2:2 + CTX]
                              else:
                                  dstt, dsta = u[j], u[j][:, 2 + t0:2 + t0 + n]
                              p.op("act", lambda e, pp=pp, dsta=dsta, n=n: e.activation(
                                  out=dsta, in_=pp[:, 0:n], func=AF.Copy), reads=(pp,), writes=(dstt,))
                          else:
                              n2 = min(n, 128) if (kind == "l" and t0 == 2048) else n
                              if kind == "c":
                                  dstt, dsta = sgc[q][j], sgc[q][j][:, 0:CTX]
                              else:
                                  dstt, dsta = sg[q][j], sg[q][j][:, t0:t0 + n2]
                              p.op("act", lambda e, pp=pp, dsta=dsta, n2=n2: e.activation(
                                  out=dsta, in_=pp[:, 0:n2], func=AF.Silu), reads=(pp,), writes=(dstt,))
                      yield
              def elem_gen(gbi):
                  q = gbi % 2
                  GW = GWs[gbi]
                  for j in range(2):
                      ch = gbi * 2 + j
                      for (ut, uct, ucbt, nn) in ((ucx[j], ucc[j], uccb[j], CTX), (u[j], uc[j], ucb[j], NH)):
                          p.op("dve", lambda e, ut=ut, uct=uct, nn=nn, ch=ch: e.tensor_scalar(
                              out=uct[:, 0:nn], in0=ut[:, 0:nn], scalar1=conv5[:, ch, 0:1], scalar2=convb[:, ch:ch + 1],
                              op0=ALU.mult, op1=ALU.add), reads=(ut, conv5, convb), writes=(uct,))
                          for k in range(1, 5):
                              p.op("dve", lambda e, ut=ut, uct=uct, nn=nn, ch=ch, k=k: e.scalar_tensor_tensor(
                                  out=uct[:, 0:nn], in0=ut[:, k:k + nn], scalar=conv5[:, ch, k:k + 1], in1=uct[:, 0:nn],
                                  op0=ALU.mult, op1=ALU.add), reads=(ut, conv5, uct), writes=(uct,))
                          p.op("act", lambda e, uct=uct, ucbt=ucbt, nn=nn: e.activation(
                              out=ucbt[:, 0:nn], in_=uct[:, 0:nn], func=AF.Copy), reads=(uct,), writes=(ucbt,))
                  yield
                  def half_gen(j):
                      ch = gbi * 2 + j

                      def gate_ab(d, ucb_pair, uct, c0, n):
                          pr = pr_r.next(); pi = pi_r.next()
                          t1 = t1_r.next(); t2 = t2_r.next()

                          def mm(e, pr=pr, pi=pi):
                              ins = None
                              for (pt_, gi) in ((pr, 2 * d), (pi, 2 * d + 1)):
                                  for kh in range(2):
                                      ins = e.matmul(pt_[:, 0:n], lhsT=GW[:, gi, kh, j * 128:(j + 1) * 128],
                                                     rhs=ucb_pair[kh][:, c0:c0 + n], start=(kh == 0), stop=(kh == 1))
                              return ins
                          p.op("pe", mm, reads=(GW, ucb_pair[0], ucb_pair[1]), writes=(pr, pi))
                          p.op("act", lambda e: e.activation(out=t1[:, 0:n], in_=pr[:, 0:n], func=AF.Sigmoid,
                                                             bias=gb_t[:, 2 * d, ch:ch + 1]),
                               reads=(pr, gb_t), writes=(t1,))
                          p.op("act", lambda e: e.activation(out=t2[:, 0:n], in_=pi[:, 0:n], func=AF.Sigmoid,
                                                             bias=gb_t[:, 2 * d + 1, ch:ch + 1]),
                               reads=(pi, gb_t), writes=(t2,))
                          p.op("act", lambda e: e.activation(out=t1[:, 0:n], in_=t1[:, 0:n], func=AF.Exp,
                                                             scale=cneg[:, d, ch:ch + 1]),
                               reads=(t1, cneg), writes=(t1,))
                          return t1, t2

                      def finish_b(t1, t2, t3, uct, c0, n):
                          p.op("dve", lambda e: e.tensor_tensor(out=t2[:, 0:n], in0=t2[:, 0:n], in1=uct[:, c0:c0 + n],
                                                                op=ALU.mult), reads=(t2, uct), writes=(t2,))
                          p.op("act", lambda e: e.activation(out=t3[:, 0:n], in_=t1[:, 0:n], func=AF.Square),
                               reads=(t1,), writes=(t3,))
                          p.op("act", lambda e: e.activation(out=t3[:, 0:n], in_=t3[:, 0:n], func=AF.Sqrt,
                                                             scale=-1.0, bias=1.0), reads=(t3,), writes=(t3,))
                          p.op("dve", lambda e: e.tensor_tensor(out=t2[:, 0:n], in0=t2[:, 0:n], in1=t3[:, 0:n],
                                                                op=ALU.mult), reads=(t2, t3), writes=(t2,))

                      t1, t2 = gate_ab(0, uccb, ucc[j], 0, CTX)
                      t3 = t3_r.next()
                      finish_b(t1, t2, t3, ucc[j], 0, CTX)
                      p.op("dve", lambda e, t1=t1, t2=t2: e.tensor_tensor_scan(
                          out=hAc[:, :], data0=t1[:, 0:CTX], data1=t2[:, 0:CTX], initial=0.0,
                          op0=ALU.mult, op1=ALU.add), reads=(t1, t2), writes=(hAc,))
                      yield
                      for ci_, (c0, n) in enumerate(chunks):
                          t1, t2 = gate_ab(0, ucb, uc[j], c0, n)
                          t3 = t3_r.next()
                          finish_b(t1, t2, t3, uc[j], c0, n)
                          init = hAc[:, CTX - 1:CTX] if ci_ == 0 else hA[:, c0 - 1:c0]
                          p.op("dve", lambda e, t1=t1, t2=t2, c0=c0, n=n, init=init: e.tensor_tensor_scan(
                              out=hA[:, c0:c0 + n], data0=t1[:, 0:n], data1=t2[:, 0:n], initial=init,
                              op0=ALU.mult, op1=ALU.add), reads=(t1, t2, hA, hAc), writes=(hA,))
                          yield
                      p.op("act", lambda e, ch=ch: e.activation(out=SAbuf[:, ch:ch + 1], in_=hA[:, 1919:1920],
                                                                func=AF.Copy), reads=(hA,), writes=(SAbuf,))
                      t1, t2 = gate_ab(1, uccb, ucc[j], 0, CTX)
                      t3 = t3_r.next()
                      finish_b(t1, t2, t3, ucc[j], 0, CTX)
                      p.op("dve", lambda e, t1=t1, t2=t2: e.tensor_tensor_scan(
                          out=hBc[:, ::-1], data0=t1[:, 0:CTX][:, ::-1], data1=t2[:, 0:CTX][:, ::-1], initial=0.0,
                          op0=ALU.mult, op1=ALU.add), reads=(t1, t2), writes=(hBc,))
                      zcb = zcb_r.next()
                      p.op("dve", lambda e: e.tensor_tensor(out=hBc[:, :], in0=hBc[:, :], in1=hAc[:, :], op=ALU.add),
                           reads=(hBc, hAc), writes=(hBc,))
                      p.op("dve", lambda e, zcb=zcb: e.tensor_tensor(out=zcb[:, :], in0=hBc[:, :], in1=sgc[q][j][:, :],
                                                                    op=ALU.mult), reads=(hBc, sgc[q][j]), writes=(zcb,))
                      dma_st("sp", zcb, ZCTX[ch * 128:(ch + 1) * 128, :], zcb[:, :])
                      yield
                      prev3 = None; prev4 = None; prevn = None
                      for ci_ in range(len(chunks) - 1, -1, -1):
                          c0, n = chunks[ci_]
                          t1, t2 = gate_ab(1, ucb, uc[j], c0, n)
                          t3 = t3_r.next(); t4 = t4_r.next()
                          finish_b(t1, t2, t3, uc[j], c0, n)
                          if prev4 is None:
                              p.op("dve", lambda e, t1=t1, t4=t4, n=n: e.tensor_tensor_scan(
                                  out=t4[:, 0:n][:, ::-1], data0=t1[:, 0:n][:, ::-1], data1=zeros[:, 0:n], initial=1.0,
                                  op0=ALU.mult, op1=ALU.add), reads=(t1, zeros), writes=(t4,))
                              p.op("dve", lambda e, t1=t1, t2=t2, t3=t3, n=n: e.tensor_tensor_scan(
                                  out=t3[:, 0:n][:, ::-1], data0=t1[:, 0:n][:, ::-1], data1=t2[:, 0:n][:, ::-1],
                                  initial=0.0, op0=ALU.mult, op1=ALU.add), reads=(t1, t2), writes=(t3,))
                          else:
                              p.op("dve", lambda e, t1=t1, t4=t4, n=n, st4=st4: e.tensor_tensor_scan(
                                  out=t4[:, 0:n][:, ::-1], data0=t1[:, 0:n][:, ::-1], data1=zeros[:, 0:n],
                                  initial=st4[:, 0:1], op0=ALU.mult, op1=ALU.add), reads=(t1, zeros, st4), writes=(t4,))
                              p.op("dve", lambda e, t1=t1, t2=t2, t3=t3, n=n, st3=st3: e.tensor_tensor_scan(
                                  out=t3[:, 0:n][:, ::-1], data0=t1[:, 0:n][:, ::-1], data1=t2[:, 0:n][:, ::-1],
                                  initial=st3[:, 0:1], op0=ALU.mult, op1=ALU.add), reads=(t1, t2, st3), writes=(t3,))
                          st3 = st3_r.next()
                          st4 = st4_r.next()
                          p.op("act", lambda e, t3=t3, st3=st3: e.activation(out=st3[:, :], in_=t3[:, 0:1], func=AF.Copy),
                               reads=(t3,), writes=(st3,))
                          p.op("act", lambda e, t4=t4, st4=st4: e.activation(out=st4[:, :], in_=t4[:, 0:1], func=AF.Copy),
                               reads=(t4,), writes=(st4,))
                          prev4 = t4
                          p.op("dve", lambda e, t3=t3, c0=c0, n=n: e.tensor_tensor(
                              out=t3[:, 0:n], in0=t3[:, 0:n], in1=hA[:, c0:c0 + n], op=ALU.add),
                              reads=(t3, hA), writes=(t3,))
                          p.op("dve", lambda e, t3=t3, c0=c0, n=n: e.tensor_tensor(
                              out=t3[:, 0:n], in0=t3[:, 0:n], in1=sg[q][j][:, c0:c0 + n], op=ALU.mult),
                              reads=(t3, sg[q][j]), writes=(t3,))
                          p.op("dve", lambda e, t4=t4, c0=c0, n=n: e.tensor_tensor(
                              out=t4[:, 0:n], in0=t4[:, 0:n], in1=sg[q][j][:, c0:c0 + n], op=ALU.mult),
                              reads=(t4, sg[q][j]), writes=(t4,))
                          dma_st("sp", t3, Z0[ch * 128:(ch + 1) * 128, c0:c0 + n], t3[:, 0:n])
                          dma_st("sp", t4, ZC[ch * 128:(ch + 1) * 128, c0:c0 + n], t4[:, 0:n])
                          yield
                  for j_ in range(2):
                      yield from half_gen(j_)
              GWs = {}
              for _ in proj_gen(0):
                  pass
              for gbi_ in range(16):
                  eg = elem_gen(gbi_)
                  pg = proj_gen(gbi_ + 1) if gbi_ < 15 else None
                  next(eg)
                  ne = 0
                  for _ in eg:
                      ne += 1
                      if pg is not None and ne % 2 == 0:
                          if next(pg, "end") == "end":
                              pg = None
                  if pg is not None:
                      for _ in pg:
                          pass
              p.op("pool", lambda e: e.dma_start(out=gin[:, :], in_=SAbuf[:, :]), reads=(SAbuf,), dma=SAbuf)
              p.barrier()
              p.cnt[ccsem] += 1

              def cc(e):
                  return e.collective_compute("AllGather", ALU.bypass,
                                              replica_groups=[[0, 1], [2, 3], [4, 5], [6, 7]],
                                              ins=[gin.ap().opt()], outs=[gout.ap().opt()])
              p.prog["pool"].append(([], cc, (ccsem, 1)))
              p.prog["pool"].append(([(ccsem, p.cnt[ccsem])], None, None))
              p.seen["pool"][ccsem] = p.cnt[ccsem]
              dma_ld("pool", gsb, gsb[:, :, :], gout.ap().rearrange("(r p) k -> p r k", p=128))
              p.op("dve", lambda e: e.tensor_scalar(out=Sin[:, :], in0=gsb[:, 0, :], scalar1=sel[:, 0:1], scalar2=None,
                                                     op0=ALU.mult), reads=(gsb, sel), writes=(Sin,))
              p.op("dve", lambda e: e.scalar_tensor_tensor(out=Sin[:, :], in0=gsb[:, 1, :], scalar=sel[:, 1:2],
                                                            in1=Sin[:, :], op0=ALU.mult, op1=ALU.add),
                   reads=(gsb, sel, Sin), writes=(Sin,))
              p.end_stage()

        def stage_out(l, w_out_ap, blocks):
            with contextlib.ExitStack() as st:
                zT_r = Ring([p.sb(st, "zT%d" % i, [128, KC, 512], BF16, dma=True) for i in range(2)])
                Wo_r = Ring([p.sb(st, "Wo%d" % i, [128, KC, 512], BF16, dma=True) for i in range(2)])
                z0_r = Ring([p.sb(st, "z0_%d" % i, [128, 512], F32, dma=True) for i in range(3)])
                zc_r = Ring([p.sb(st, "zc_%d" % i, [128, 512], F32, dma=True) for i in range(3)])
                gbc = [p.sb(st, "gbc%d" % r, [128, D], F32, dma=True) for r in range(2)]
                xc_r = Ring([p.sb(st, "xc%d" % i, [128, 512], F32, dma=True) for i in range(3)])
                yo_r = Ring([p.sb(st, "yo%d" % i, [128, 512], F32, dma=True) for i in range(3)])
                po_r = Ring([p.ps(st, "po%d" % i, [128, 512], F32) for i in range(4)])
                for r in range(2):
                    dma_ld("sp", gbc[r], gbc[r][:, :], grow[l, r].partition_broadcast(128))
                for blk in blocks:
                    n = blk["n"]; t0 = blk["t0"]
                    zT = zT_r.next()
                    if blk["zmode"] == "corr":
                        for c in range(KC):
                            z0 = z0_r.next(); zc = zc_r.next()
                            dma_ld("sp", z0, z0[:, 0:n], Z0[c * 128:(c + 1) * 128, t0:t0 + n])
                            dma_ld("sp", zc, zc[:, 0:n], ZC[c * 128:(c + 1) * 128, t0:t0 + n])
                            p.op("dve", lambda e, z0=z0, zc=zc, zT=zT, c=c, n=n: e.scalar_tensor_tensor(
                                out=zT[:, c, 0:n], in0=zc[:, 0:n], scalar=Sin[:, c:c + 1], in1=z0[:, 0:n],
                                op0=ALU.mult, op1=ALU.add), reads=(z0, zc, Sin), writes=(zT,))
                    else:
                        zsrc = blk["zsrc"]
                        dma_ld("sp", zT, zT[:, :, 0:n], zsrc.rearrange("(k p) t -> p k t", p=128)[:, :, t0:t0 + n])
                    for nb in range(8):
                        Wo = Wo_r.next()
                        dma_ld("pool", Wo, Wo[:, :, :],
                               w_out_ap[:, nb * 512:(nb + 1) * 512].rearrange("(k p) n -> p k n", p=128))
                        for tt in range(n // 128):
                            po = po_r.next(); xc = xc_r.next(); yo = yo_r.next()
                            r0 = t0 + tt * 128

                            def mm(e, zT=zT, Wo=Wo, po=po, tt=tt):
                                ins = None
                                for k in range(KC):
                                    ins = e.matmul(po[:, :], lhsT=zT[:, k, tt * 128:(tt + 1) * 128], rhs=Wo[:, k, :],
                                                   start=(k == 0), stop=(k == KC - 1))
                                return ins
                            p.op("pe", mm, reads=(zT, Wo), writes=(po,))
                            dma_ld("sp", xc, xc[:, :], blk["xsrc"][r0:r0 + 128, nb * 512:(nb + 1) * 512])
                            g = gbc[blk["grow"]]
                            p.op("dve", lambda e, po=po, yo=yo, g=g, nb=nb: e.tensor_tensor(
                                out=yo[:, :], in0=po[:, :], in1=g[:, nb * 512:(nb + 1) * 512], op=ALU.mult),
                                reads=(po, g), writes=(yo,))
                            p.op("dve", lambda e, yo=yo, xc=xc: e.tensor_tensor(
                                out=yo[:, :], in0=yo[:, :], in1=xc[:, :], op=ALU.add), reads=(yo, xc), writes=(yo,))
                            dma_st("sp", yo, blk["dst"][r0:r0 + 128, nb * 512:(nb + 1) * 512], yo[:, :])
                p.end_stage()

        blocks0 = [dict(zmode="ctx", zsrc=ZCTX, t0=0, n=CTX, xsrc=ctx_loc, dst=CTX1, grow=1)]
        for t0, n in [(0, 512), (512, 512), (1024, 512), (1536, 512), (2048, 128)]:
            blocks0.append(dict(zmode="corr", t0=t0, n=n, xsrc=x_loc, dst=X1, grow=0))
        if nstage >= 4 and only is None:
            stage_out(0, rg_w_out, blocks0)
        if (nstage >= 5 and only is None) or (only is not None and 'N1' in only):
            stage_norm(1, X1, 17, CTX1)

        with contextlib.ExitStack() as st:
          if nstage >= 6:
              Wr = Ring([p.sb(st, "Wq%d" % i, [128, KC, 128], BF16, dma=True) for i in range(6)])
              hb_r = Ring([p.sb(st, "ha%d" % i, [128, KC, 512], BF16, dma=True) for i in range(2)])
              QT = p.sb(st, "QT", [128, 4, NOWN], BF16)
              KT = p.sb(st, "KT", [128, NH + CTX], BF16)
              Vt = p.sb(st, "Vt", [128, 19, 128], BF16)
              OT = p.sb(st, "OT", [128, 4, NOWN], BF16)
              cosT = p.sb(st, "cosT", [128, NH], F32, dma=True)
              sinT = p.sb(st, "sinT", [128, NH], F32, dma=True)
              rotm = p.sb(st, "rotm", [128, 128], F32, dma=True)
              ones_f = p.sb(st, "ones_f", [128, 128], F32)
              ones_b = p.sb(st, "ones_b", [128, 128], BF16)
              qkn = p.sb(st, "qkn", [128, 2], F32, dma=True)
              esink = p.sb(st, "esink", [128, 32], F32, dma=True)
              masks = p.sb(st, "masks", [128, 2, 512], BF16, dma=True)
              sq_r = Ring([p.sb(st, "sq%d" % i, [128, 512], F32) for i in range(1)])
              qr_r = Ring([p.sb(st, "qr%d" % i, [128, 512], F32) for i in range(2)])
              rs_r = Ring([p.sb(st, "rsa%d" % i, [128, 512], F32) for i in range(2)])
              tq_r = Ring([p.sb(st, "tq%d" % i, [128, 512], F32) for i in range(1)])
              vb_r = Ring([p.sb(st, "vb%d" % i, [128, 512], BF16) for i in range(2)])
              PT_r = Ring([p.sb(st, "PT%d" % i, [128, 512], BF16) for i in range(3)])
              den_r = Ring([p.sb(st, "den%d" % i, [128, 512], F32) for i in range(1)])
              zt_r = Ring([p.sb(st, "zt%d" % i, [128, 512], BF16, dma=True) for i in range(3)])
              gs_r = Ring([p.sb(st, "gs%d" % i, [128, 512], F32) for i in range(2)])
              pp_r = Ring([p.ps(st, "pa%d" % i, [128, 512], F32) for i in range(2)])
              px_r = Ring([p.ps(st, "px%d" % i, [128, 512], F32) for i in range(2)])
              pS_r = Ring([p.ps(st, "pS%d" % i, [128, 512], F32) for i in range(2)])
              pO = p.ps(st, "pO", [128, 512], F32)
              pR = p.ps(st, "pR", [128, 512], F32)

              dma_ld("sp", cosT, cosT[:, :], cos_fm[:, 0:NH])
              dma_ld("sp", sinT, sinT[:, :], sin_fm[:, 0:NH])
              dma_ld("sp", rotm, rotm[:, :], rot_m)
              dma_ld("sp", qkn, qkn[:, :], qk_norm_fm)
              dma_ld("sp", esink, esink[:, :], sink_bc)
              p.op("act", lambda e: e.activation(out=esink[:, :], in_=esink[:, :], func=AF.Exp),
                   reads=(esink,), writes=(esink,))
              p.op("pool", lambda e: e.dma_start(out=masks[:, :, :], in_=mask_in.rearrange("m p n -> p m n")),
                   writes=(masks,), dma=masks)
              p.op("dve", lambda e: e.memset(ones_f[:, :], 1.0), writes=(ones_f,))
              p.op("dve", lambda e: e.memset(ones_b[:, :], 1.0), writes=(ones_b,))

              hT_v = hT.rearrange("k p t -> p k t")
              hTc_v = hTc.rearrange("k p t -> p k t")
              SCALE = 128.0 ** -0.5

              def project(cols, seqs, evac):
                  Ws = []
                  for c0 in cols:
                      W = Wr.next()
                      dma_ld("pool", W, W[:, :, :], at_w_in[:, c0:c0 + 128].rearrange("(k p) n -> p k n", p=128))
                      Ws.append(W)
                  for kind, t0, n in seqs:
                      hb = hb_r.next()
                      srcv = hTc_v[:, :, 0:CTX] if kind == "c" else hT_v[:, :, t0:t0 + n]
                      dma_ld("sp", hb, hb[:, :, 0:n], srcv)
                      for ci in range(len(cols)):
                          if not evac(ci, kind, t0, n, None):
                              continue
                          pp = pp_r.next()

                          def mm(e, W=Ws[ci], hb=hb, pp=pp, n=n):
                              ins = None
                              for k in range(KC):
                                  ins = e.matmul(pp[:, 0:n], lhsT=W[:, k, :], rhs=hb[:, k, 0:n],
                                                 start=(k == 0), stop=(k == KC - 1))
                              return ins
                          p.op("pe", mm, reads=(Ws[ci], hb), writes=(pp,))
                          evac(ci, kind, t0, n, pp)

              def norm_rope(pp, n, wcol, rope_t0, dst_tk, dst_ap):
                  sq = sq_r.next(); qr = qr_r.next(); rs = rs_r.next(); tq = tq_r.next()
                  px = px_r.next()
                  p.op("act", lambda e: e.activation(out=sq[:, 0:n], in_=pp[:, 0:n], func=AF.Square),
                       reads=(pp,), writes=(sq,))
                  p.op("act", lambda e: e.activation(out=qr[:, 0:n], in_=pp[:, 0:n], func=AF.Copy),
                       reads=(pp,), writes=(qr,))
                  p.op("pe", lambda e: e.matmul(px[:, 0:n], lhsT=ones_f[:, :], rhs=sq[:, 0:n], start=True, stop=True),
                       reads=(ones_f, sq), writes=(px,))
                  p.op("dve", lambda e: e.tensor_scalar(out=rs[:, 0:n], in0=px[:, 0:n], scalar1=1.0 / 128, scalar2=EPS,
                                                         op0=ALU.mult, op1=ALU.add), reads=(px,), writes=(rs,))
                  p.op("act", lambda e: e.activation(out=rs[:, 0:n], in_=rs[:, 0:n], func=AF.Sqrt),
                       reads=(rs,), writes=(rs,))
                  p.op("dve", lambda e: e.reciprocal(out=rs[:, 0:n], in_=rs[:, 0:n]), reads=(rs,), writes=(rs,))
                  if rope_t0 is None:
                      p.op("dve", lambda e: e.scalar_tensor_tensor(
                          out=dst_ap, in0=qr[:, 0:n], scalar=qkn[:, wcol:wcol + 1], in1=rs[:, 0:n],
                          op0=ALU.mult, op1=ALU.mult), reads=(qr, qkn, rs), writes=(dst_tk,))
                      return
                  p.op("dve", lambda e: e.scalar_tensor_tensor(
                      out=qr[:, 0:n], in0=qr[:, 0:n], scalar=qkn[:, wcol:wcol + 1], in1=rs[:, 0:n],
                      op0=ALU.mult, op1=ALU.mult), reads=(qr, qkn, rs), writes=(qr,))
                  px2 = px_r.next()
                  p.op("pe", lambda e: e.matmul(px2[:, 0:n], lhsT=rotm[:, :], rhs=qr[:, 0:n], start=True, stop=True),
                       reads=(rotm, qr), writes=(px2,))
                  p.op("dve", lambda e: e.tensor_tensor(out=tq[:, 0:n], in0=px2[:, 0:n],
                                                         in1=sinT[:, rope_t0:rope_t0 + n], op=ALU.mult),
                       reads=(px2, sinT), writes=(tq,))
                  p.op("dve", lambda e: e.tensor_tensor(out=qr[:, 0:n], in0=qr[:, 0:n],
                                                         in1=cosT[:, rope_t0:rope_t0 + n], op=ALU.mult),
                       reads=(qr, cosT), writes=(qr,))
                  p.op("dve", lambda e: e.tensor_tensor(out=dst_ap, in0=qr[:, 0:n], in1=tq[:, 0:n], op=ALU.add),
                       reads=(qr, tq), writes=(dst_tk,))

              own_blocks = [("l", 0, 512), ("l", 512, 512), ("l", 1024, 512), ("l", 1536, 512)]
              seqsA = [("c", 0, CTX)] + own_blocks + [("l", 2048, 128)]

              for h in range(8 if asub is None else asub.get('nh', 8)):
                  colsA = [D + h * 128, D + 1024 + h * 128] + [h * 512 + g * 128 for g in range(4)]

                  def evacA(ci, kind, t0, n, pp, h=h):
                      if ci >= 2 and (kind == "c" or t0 >= NOWN):
                          return False
                      if pp is None:
                          return True
                      if asub is not None and asub.get('simple', 0):
                          vb = vb_r.next()
                          p.op("act", lambda e: e.activation(out=vb[:, 0:n], in_=pp[:, 0:n], func=AF.Copy),
                               reads=(pp,), writes=(vb,))
                          return True
                      if ci == 0:
                          if kind == "c":
                              norm_rope(pp, n, 1, None, KT, KT[:, NH:NH + CTX])
                          else:
                              norm_rope(pp, n, 1, t0, KT, KT[:, t0:t0 + n])
                      elif ci == 1:
                          vb = vb_r.next()
                          p.op("act", lambda e: e.activation(out=vb[:, 0:n], in_=pp[:, 0:n], func=AF.Copy),
                               reads=(pp,), writes=(vb,))
                          px = px_r.next()
                          pxb = px[:, :].bitcast(BF16)

                          def tr(e):
                              ins = None
                              for i in range(n // 128):
                                  ins = e.transpose(out=pxb[:, i * 128:(i + 1) * 128], in_=vb[:, i * 128:(i + 1) * 128],
                                                    identity=ident_bf[:, :])
                              return ins
                          p.op("pe", tr, reads=(vb, ident_bf), writes=(px,))
                          kb0 = 17 if kind == "c" else t0 // 128
                          nb_ = n // 128
                          p.op("act", lambda e: e.activation(
                              out=Vt[:, kb0:kb0 + nb_, :],
                              in_=pxb[:, 0:nb_ * 128].rearrange("p (b d) -> p b d", d=128), func=AF.Copy),
                              reads=(px,), writes=(Vt,))
                      else:
                          g = ci - 2
                          norm_rope(pp, n, 0, t0, QT, QT[:, g, t0:t0 + n])
                      return True
                  project(colsA, seqsA, evacA)

                  for i in range(16 if asub is None else asub.get('nq', 16)):
                      kbs = [("c", 17, NH), ("c", 18, NH + 128)]
                      if i > 0:
                          kbs.append(("p", i - 1, (i - 1) * 128))
                      kbs.append(("o", i, i * 128))
                      kbs.append(("n", i + 1, (i + 1) * 128))
                      for ki, (kk, vb_i, kc0) in enumerate(kbs):
                          pS = pS_r.next(); PT = PT_r.next()
                          p.op("pe", lambda e, pS=pS, kc0=kc0, i=i: e.matmul(
                              pS[:, :].rearrange("p (g q) -> p g q", g=4), lhsT=KT[:, kc0:kc0 + 128],
                              rhs=QT[:, :, i * 128:(i + 1) * 128], start=True, stop=True),
                              reads=(KT, QT), writes=(pS,))
                          p.op("act", lambda e, pS=pS, PT=PT: e.activation(out=PT[:, :], in_=pS[:, :], func=AF.Exp,
                                                                           scale=SCALE), reads=(pS,), writes=(PT,))
                          if kk in ("p", "n"):
                              mi = 0 if kk == "p" else 1
                              p.op("dve", lambda e, PT=PT, mi=mi: e.tensor_tensor(
                                  out=PT[:, :], in0=PT[:, :], in1=masks[:, mi, :], op=ALU.mult),
                                  reads=(PT, masks), writes=(PT,))

                          def pv(e, PT=PT, vb_i=vb_i, ki=ki, last=(ki == len(kbs) - 1)):
                              e.matmul(pO[:, :], lhsT=Vt[:, vb_i, :], rhs=PT[:, :], start=(ki == 0), stop=last)
                              return e.matmul(pR[:, :], lhsT=ones_b[:, :], rhs=PT[:, :], start=(ki == 0), stop=last)
                          p.op("pe", pv, reads=(Vt, PT, ones_b), writes=(pO, pR))
                      den = den_r.next()
                      for g in range(4):
                          p.op("dve", lambda e, den=den, g=g, h=h: e.tensor_scalar(
                              out=den[:, g * 128:(g + 1) * 128], in0=pR[:, g * 128:(g + 1) * 128],
                              scalar1=esink[:, h * 4 + g:h * 4 + g + 1], scalar2=None, op0=ALU.add),
                              reads=(pR, esink), writes=(den,))
                      p.op("dve", lambda e, den=den: e.reciprocal(out=den[:, :], in_=den[:, :]),
                           reads=(den,), writes=(den,))
                      p.op("dve", lambda e, den=den, i=i: e.tensor_tensor(
                          out=OT[:, :, i * 128:(i + 1) * 128], in0=pO[:, :].rearrange("p (g q) -> p g q", g=4),
                          in1=den[:, :].rearrange("p (g q) -> p g q", g=4), op=ALU.mult),
                          reads=(pO, den), writes=(OT,))

                  colsB = [6144 + h * 512 + g * 128 for g in range(4)]

                  def evacB(ci, kind, t0, n, pp, h=h):
                      if pp is None:
                          return True
                      gs = gs_r.next(); zt = zt_r.next()
                      p.op("act", lambda e: e.activation(out=gs[:, 0:n], in_=pp[:, 0:n], func=AF.Silu),
                           reads=(pp,), writes=(gs,))
                      p.op("dve", lambda e: e.tensor_tensor(out=zt[:, 0:n], in0=gs[:, 0:n], in1=OT[:, ci, t0:t0 + n],
                                                             op=ALU.mult), reads=(gs, OT), writes=(zt,))
                      r0 = (h * 4 + ci) * 128
                      dma_st("sp", zt, Z1T[r0:r0 + 128, t0:t0 + n], zt[:, 0:n])
                      return True
                  if asub is None or asub.get('pb', 1):
                      project(colsB, own_blocks, evacB)
              p.end_stage()

        blocks1 = [dict(zmode="direct", zsrc=Z1T, t0=t0, n=512, xsrc=X1, dst=out, grow=0)
                   for t0 in (0, 512, 1024, 1536)]
        if nstage >= 7:
            stage_out(1, at_w_out, blocks1)

        p.check()
        p.emit()
    return nc


def _fm(v):
    v = np.asarray(v, np.float32)
    lead = v.shape[:-1]
    a = v.reshape(lead + (KC, 128))
    a = np.moveaxis(a, -1, 0)
    return np.ascontiguousarray(a)


def prepare_inputs(x, c, ctx, c_ctx, w_mod, b_mod, norm_w, rg_w_in, rg_conv_w, rg_conv_b, rg_w_r, rg_b_r,
                   rg_w_i, rg_b_i, rg_lam, rg_w_out, at_w_in, at_q_norm, at_k_norm, at_sink, at_w_out):
    f32 = np.float32
    shared = {}
    shared["w_mod"] = np.ascontiguousarray(w_mod, f32)
    shared["bmod_fm"] = np.ascontiguousarray(
        np.asarray(b_mod, f32).reshape(2, 96, 128).transpose(2, 0, 1))
    shared["normw_fm"] = _fm(norm_w)
    shared["rg_w_in"] = np.ascontiguousarray(rg_w_in[0], f32)
    shared["convb_fm"] = _fm(rg_conv_b[0])
    shared["rg_w_out"] = np.ascontiguousarray(rg_w_out[0], f32)
    shared["at_w_in"] = np.ascontiguousarray(at_w_in[0], f32)
    shared["qk_norm_fm"] = np.ascontiguousarray(np.stack([at_q_norm[0], at_k_norm[0]], axis=1), f32)
    shared["sink_bc"] = np.ascontiguousarray(np.broadcast_to(np.asarray(at_sink[0], f32)[None, :], (128, 32)))
    shared["at_w_out"] = np.ascontiguousarray(at_w_out[0], f32)
    ident = np.eye(128, dtype=f32)
    shared["ident_in"] = ident
    rot = np.zeros((128, 128), f32)
    for m in range(128):
        if (m % 64) < 32:
            rot[m + 32, m] = -1.0
        else:
            rot[m - 32, m] = 1.0
    shared["rot_m"] = rot
    kj = np.arange(128)[:, None]
    qi = np.arange(128)[None, :]
    mprev = (kj >= qi).astype(f32)
    mnext = (kj <= qi).astype(f32)
    shared["mask_in"] = np.ascontiguousarray(np.stack([np.tile(mprev, (1, 4)), np.tile(mnext, (1, 4))], 0))

    conv_w = np.asarray(rg_conv_w[0], f32)
    zero = np.zeros((1, D), f32)
    per_core = []
    for core in range(8):
        b, half = core // 2, core % 2
        m = dict(shared)
        if half == 0:
            idx = np.arange(NLOC)
            m["ctx_loc"] = np.ascontiguousarray(ctx[b], f32)
            conv5 = np.concatenate([conv_w, zero], 0)
        else:
            idx = 4095 - np.arange(NLOC)
            m["ctx_loc"] = np.ascontiguousarray(np.asarray(ctx[b], f32)[::-1])
            conv5 = np.concatenate([zero, conv_w[::-1]], 0)
        m["x_loc"] = np.ascontiguousarray(np.asarray(x[b], f32)[idx])
        m["conv5_fm"] = np.ascontiguousarray(np.moveaxis(_fm(conv5), 1, 2))
        m["c_fm"] = np.ascontiguousarray(np.moveaxis(_fm(np.stack([c[b], c_ctx], 0)), 1, 2))
        dA, dB = half, 1 - half
        m["gate_w"] = np.ascontiguousarray(np.stack([rg_w_r[0, dA], rg_w_i[0, dA], rg_w_r[0, dB], rg_w_i[0, dB]], 0), f32)
        m["gate_b_fm"] = _fm(np.stack([rg_b_r[0, dA], rg_b_i[0, dA], rg_b_r[0, dB], rg_b_i[0, dB]], 0))
        m["lam_fm"] = _fm(np.stack([rg_lam[0, dA], rg_lam[0, dB]], 0))
        t = idx.astype(np.float64)
        row = np.floor(t / 64.0)
        col = t - row * 64.0
        inv = 10000.0 ** (-np.arange(32, dtype=np.float64) * (2.0 / 64.0))
        dd = np.arange(128)
        pos = np.where(dd[:, None] < 64, row[None, :], col[None, :])
        ang = pos * inv[dd % 32][:, None]
        m["cos_fm"] = np.ascontiguousarray(np.cos(ang), f32)
        m["sin_fm"] = np.ascontiguousarray(np.sin(ang), f32)
        selv = np.zeros((128, 2), f32)
        selv[:, 1 - half] = 1.0
        m["sel_in"] = selv
        per_core.append(m)
    return per_core


_NC_CACHE = {}


def kernel(**inputs):
    inputs = {k: np.asarray(v) for k, v in inputs.items()}
    per_core = prepare_inputs(**inputs)
    if "nc" not in _NC_CACHE:
        _NC_CACHE["nc"] = build_program(DEBUG)
    nc = _NC_CACHE["nc"]
    res = run_bass_kernel_spmd(nc, per_core, core_ids=list(range(8)))
    outp = np.empty((4, 4096, D), np.float32)
    for core in range(8):
        b, half = core // 2, core % 2
        o = np.asarray(res.results[core]["out"], np.float32)
        if half == 0:
            outp[b, 0:NOWN] = o
        else:
            outp[b, NOWN:] = o[::-1]
    if DEBUG:
        kernel.last = res
    return outp
```

```python
import numpy as np
import ml_dtypes
import concourse.bass as bass
import concourse.mybir as mybir
from concourse.bass_utils import run_bass_kernel_spmd

F32 = mybir.dt.float32
BF16 = mybir.dt.bfloat16
ALU = mybir.AluOpType
AF = mybir.ActivationFunctionType

D = 4096
KC = 32
NOWN = 2048
NH = 2176
NU = 2178
NLOC = 2304
CTX = 256
EPS = 1e-6
ENG = ("pe", "act", "dve", "pool", "sp")
DEBUG = False
REUSE_DSEMS = True


class Tk:
    __slots__ = ("ap", "lw", "rd", "dsem", "const")

    def __init__(self, ap=None, dsem=None, const=False):
        self.ap = ap
        self.lw = None
        self.rd = {}
        self.dsem = dsem
        self.const = const

    def __getitem__(self, k):
        return self.ap[k]


class Prog:
    def __init__(self, nc, stack):
        self.nc = nc
        self.stack = stack
        self.prog = {e: [] for e in ENG}
        self.cnt = {}
        self.seen = {e: {} for e in ENG}
        self.esem = {}
        for e in ENG:
            s = stack.enter_context(nc.semaphore("es_" + e))
            self.esem[e] = s
            self.cnt[s] = 0
        self.free_dsems = []
        self.ndsem = 0
        self.stage_dsems = []

    def dsem(self):
        if self.free_dsems:
            s = self.free_dsems.pop()
        else:
            s = self.stack.enter_context(self.nc.semaphore("ds%d" % self.ndsem))
            self.ndsem += 1
            self.cnt[s] = 0
        self.stage_dsems.append(s)
        return s

    def sb(self, st, name, shape, dt, dma=False, const=False):
        self.uid = getattr(self, "uid", 0) + 1
        name = "%s_u%d" % (name, self.uid)
        t = st.enter_context(self.nc.sbuf_tensor(name, list(shape), dt))
        return Tk(t, self.dsem() if dma else None, const)

    def ps(self, st, name, shape, dt):
        self.uid = getattr(self, "uid", 0) + 1
        name = "%s_u%d" % (name, self.uid)
        t = st.enter_context(self.nc.psum_tensor(name, list(shape), dt))
        return Tk(t)

    def dram(self, ap=None):
        return Tk(ap)

    def op(self, eng, fn, reads=(), writes=(), dma=None):
        waits = {}
        seen = self.seen[eng]

        def need(tok):
            if tok is None:
                return
            sem, val = tok
            if seen.get(sem, 0) >= val:
                return
            if waits.get(sem, 0) < val:
                waits[sem] = val

        for t in reads:
            need(t.lw)
        for t in writes:
            need(t.lw)
            for sem, val in t.rd.items():
                need((sem, val))
        if eng == "pool" and dma is not None:
            hist = self.__dict__.setdefault("pool_hist", [])
            if len(hist) >= 3:
                need(hist[-3])
        for sem, val in waits.items():
            seen[sem] = val
        if dma is not None:
            sem = dma.dsem
            self.cnt[sem] += 16
            inc = (sem, 16)
        else:
            sem = self.esem[eng]
            self.cnt[sem] += 1
            inc = (sem, 1)
        tok = (sem, self.cnt[sem])
        if eng == "pool" and dma is not None:
            self.pool_hist.append(tok)
        self.prog[eng].append((list(waits.items()), fn, inc))
        for t in reads:
            if not t.const:
                if t.rd.get(sem, 0) < tok[1]:
                    t.rd[sem] = tok[1]
        for t in writes:
            t.lw = tok
            t.rd = {}
        return tok

    def barrier(self):
        for e in ENG:
            waits = []
            for sem, c in self.cnt.items():
                if c > 0 and self.seen[e].get(sem, 0) < c and sem is not self.esem[e]:
                    waits.append((sem, c))
                    self.seen[e][sem] = c
            self.prog[e].append((waits, None, None))

    def end_stage(self):
        self.barrier()
        if REUSE_DSEMS:
            self.free_dsems.extend(self.stage_dsems)
        self.stage_dsems = []

    def check(self):
        pos = {e: 0 for e in ENG}
        val = {}
        progress = True
        while progress:
            progress = False
            for e in ENG:
                lst = self.prog[e]
                while pos[e] < len(lst):
                    waits, fn, inc = lst[pos[e]]
                    if any(val.get(sem, 0) < v for sem, v in waits):
                        break
                    if inc is not None:
                        val[inc[0]] = val.get(inc[0], 0) + inc[1]
                    pos[e] += 1
                    progress = True
        stuck = {e: (pos[e], len(self.prog[e])) for e in ENG if pos[e] < len(self.prog[e])}
        for e in stuck:
            waits, fn, inc = self.prog[e][pos[e]]
            print("STUCK", e, stuck[e], [(str(sem), v, val.get(sem, 0)) for sem, v in waits])
        bad = {str(sem): (val.get(sem, 0), c) for sem, c in self.cnt.items() if val.get(sem, 0) != c}
        print("check: stuck=%s mismatched=%s" % (bool(stuck), bad))

    def emit(self):
        nc = self.nc
        prog = self.prog

        def replay(lst, e):
            for waits, fn, inc in lst:
                for sem, val in waits:
                    e.wait_ge(sem, val)
                if fn is not None:
                    ins = fn(e)
                    ins.then_inc(inc[0], inc[1])

        with nc.Block() as block:
            @block.tensor
            def _(e):
                replay(prog["pe"], e)

            @block.scalar
            def _(e):
                replay(prog["act"], e)

            @block.vector
            def _(e):
                replay(prog["dve"], e)

            @block.gpsimd
            def _(e):
                replay(prog["pool"], e)

            @block.sync
            def _(e):
                replay(prog["sp"], e)


class Ring:
    def __init__(self, tiles):
        self.tiles = tiles
        self.i = 0

    def next(self):
        t = self.tiles[self.i % len(self.tiles)]
        self.i += 1
        return t


def build_program(debug=False, stop_after=None, only=None, asub=None):
    import contextlib
    nc = bass.Bass("TRN2", target_bir_lowering=False)
    dk = "ExternalOutput" if debug else "Internal"

    NEED_A = ("at_w_in", "qk_norm_fm", "sink_bc", "cos_fm", "sin_fm", "rot_m", "ident_in", "mask_in",
              "normw_fm", "bmod_fm")

    def din(name, shape, dt=F32):
        if only is not None and name not in NEED_A:
            return None
        return nc.dram_tensor(name, list(shape), dt, kind="ExternalInput").ap()

    def dscr(name, shape, dt):
        if debug and (debug is True or name in debug):
            return nc.dram_tensor(name, list(shape), dt, kind="ExternalOutput").ap()
        return nc.dram_tensor(name, list(shape), dt).ap()

    x_loc = din("x_loc", [NLOC, D])
    ctx_loc = din("ctx_loc", [CTX, D])
    c_fm = din("c_fm", [128, KC, 2])
    w_mod = din("w_mod", [2, D, 3 * D])
    bmod_fm = din("bmod_fm", [128, 2, 96])
    normw_fm = din("normw_fm", [128, 2, KC])
    rg_w_in = din("rg_w_in", [D, 2 * D])
    conv5_fm = din("conv5_fm", [128, KC, 5])
    convb_fm = din("convb_fm", [128, KC])
    gate_w = din("gate_w", [4, 16, 256, 256])
    gate_b_fm = din("gate_b_fm", [128, 4, KC])
    lam_fm = din("lam_fm", [128, 2, KC])
    rg_w_out = din("rg_w_out", [D, D])
    at_w_in = din("at_w_in", [D, 10240])
    qk_norm_fm = din("qk_norm_fm", [128, 2])
    sink_bc = din("sink_bc", [128, 32])
    at_w_out = din("at_w_out", [D, D])
    cos_fm = din("cos_fm", [128, NLOC])
    sin_fm = din("sin_fm", [128, NLOC])
    rot_m = din("rot_m", [128, 128])
    ident_in = din("ident_in", [128, 128])
    mask_in = din("mask_in", [2, 128, 512])
    sel_in = din("sel_in", [128, 2])
    out = nc.dram_tensor("out", [NOWN, D], F32, kind="ExternalOutput").ap()

    hT = dscr("hT", [18, 128, KC, 128], BF16)
    hTc = dscr("hTc", [2, 128, KC, 128], BF16)
    Z0 = dscr("Z0", [D, NH], F32)
    ZC = dscr("ZC", [D, NH], F32)
    ZCTX = dscr("ZCTX", [D, CTX], BF16)
    X1 = dscr("X1", [NH, D], F32)
    CTX1 = dscr("CTX1", [CTX, D], F32)
    Z1T = dscr("Z1T", [D, NOWN], BF16)
    grow = dscr("grow", [2, 2, D], F32)
    gin = nc.dram_tensor("gin", [128, KC], F32)
    gout = nc.dram_tensor("gout", [256, KC], F32)

    with contextlib.ExitStack() as gst:
        p = Prog(nc, gst)
        gst.enter_context(nc.allow_non_contiguous_dma(reason="small param scatter"))
        gst.enter_context(nc.allow_low_precision(reason="bf16 matmul operands"))
        ccsem = gst.enter_context(nc.semaphore("ccsem"))
        p.cnt[ccsem] = 0

        ident_bf = p.sb(gst, "ident_bf", [128, 128], BF16, dma=True)
        ident_f = p.sb(gst, "ident_f", [128, 128], F32, dma=True)
        mod = [p.sb(gst, "mod%d" % l, [128, 96, 2], F32) for l in range(2)]
        Avec = [p.sb(gst, "Avec%d" % l, [128, KC, 2], F32) for l in range(2)]
        normw = p.sb(gst, "normw", [128, 2, KC], F32, dma=True)
        bmod = p.sb(gst, "bmod", [128, 2, 96], F32, dma=True)
        Sin = p.sb(gst, "Sin", [128, KC], F32)

        def dma_ld(eng, dst_tk, dst_ap, src_ap, reads=(), extra_w=()):
            p.op(eng, lambda e: e.dma_start(out=dst_ap, in_=src_ap), reads=reads,
                 writes=(dst_tk,) + tuple(extra_w), dma=dst_tk)

        def dma_st(eng, src_tk, dst_ap, src_ap, dst_tk=None):
            p.op(eng, lambda e: e.dma_start(out=dst_ap, in_=src_ap), reads=(src_tk,),
                 writes=(dst_tk,) if dst_tk is not None else (), dma=src_tk)

        dma_ld("sp", ident_f, ident_f[:, :], ident_in)
        dma_ld("pool", ident_bf, ident_bf[:, :], ident_in)
        dma_ld("sp", normw, normw[:, :, :], normw_fm)
        dma_ld("sp", bmod, bmod[:, :, :], bmod_fm)

        with contextlib.ExitStack() as st:
          if only is None:
              cf = p.sb(st, "cf", [128, KC, 2], F32, dma=True)
              scb = p.sb(st, "scb", [128, KC, 2], BF16)
              Wm = Ring([p.sb(st, "Wm%d" % i, [128, KC, 512], BF16, dma=True) for i in range(2)])
              psm = [p.ps(st, "psm%d" % l, [128, 512], F32) for l in range(2)]
              dma_ld("sp", cf, cf[:, :, :], c_fm)
              p.op("act", lambda e: e.activation(out=scb[:, :, :], in_=cf[:, :, :], func=AF.Silu),
                   reads=(cf,), writes=(scb,))
              for l in range(2):
                  for blk in range(24):
                      W = Wm.next()
                      src = w_mod[l, :, blk * 512:(blk + 1) * 512].rearrange("(k p) n -> p k n", p=128)
                      dma_ld("pool", W, W[:, :, :], src)

                      def mm(e, W=W, blk=blk, l=l):
                          ins = None
                          for j in range(4):
                              n = blk * 4 + j
                              for k in range(KC):
                                  ins = e.matmul(psm[l][:, n * 2:n * 2 + 2], lhsT=W[:, k, j * 128:(j + 1) * 128],
                                                 rhs=scb[:, k, :], start=(k == 0), stop=(k == KC - 1))
                          return ins
                      p.op("pe", mm, reads=(W, scb), writes=(psm[l],))
                  for r in range(2):
                      p.op("dve", lambda e, l=l, r=r: e.tensor_tensor(
                          out=mod[l][:, :, r], in0=psm[l][:, 0:192].rearrange("p (n r) -> p n r", r=2)[:, :, r],
                          in1=bmod[:, l, :], op=ALU.add), reads=(psm[l], bmod), writes=(mod[l],))
                  for r in range(2):
                      p.op("dve", lambda e, l=l, r=r: e.scalar_tensor_tensor(
                          out=Avec[l][:, :, r], in0=mod[l][:, 32:64, r], scalar=1.0, in1=normw[:, l, :],
                          op0=ALU.add, op1=ALU.mult), reads=(mod[l], normw), writes=(Avec[l],))
                      p.op("sp", lambda e, l=l, r=r: e.dma_start(
                          out=grow[l, r].rearrange("(k p) -> p k", p=128), in_=mod[l][:, 64:96, r]),
                          reads=(mod[l],), writes=(), dma=cf)
              p.end_stage()

        def stage_norm(l, src_lat, ntiles, src_ctx):
            with contextlib.ExitStack() as st:
                xt_r = Ring([p.sb(st, "xt%d" % i, [128, D], F32, dma=True) for i in range(2)])
                xn_r = Ring([p.sb(st, "xn%d" % i, [128, D], BF16) for i in range(2)])
                junk = p.sb(st, "junk", [128, D], BF16)
                ss_r = Ring([p.sb(st, "ss%d" % i, [128, 1], F32) for i in range(4)])
                rs_r = Ring([p.sb(st, "rs%d" % i, [128, 1], F32) for i in range(4)])
                pt_r = Ring([p.ps(st, "pt%d" % i, [128, D], BF16) for i in range(2)])
                hb_r = Ring([p.sb(st, "hb%d" % i, [128, 4, KC, 128], BF16, dma=True) for i in range(2)])
                jobs = []
                nblk = (ntiles + 3) // 4
                for b in range(nblk):
                    tl = list(range(b * 4, min(ntiles, b * 4 + 4)))
                    jobs.append((src_lat, tl, hT, b * 512, 0))
                jobs.append((src_ctx, [0, 1], hTc, 0, 1))
                for src, tl, dst, c0, r in jobs:
                    hb = hb_r.next()
                    for ti, t in enumerate(tl):
                        xt = xt_r.next(); xn = xn_r.next(); ss = ss_r.next(); rs = rs_r.next(); pt = pt_r.next()
                        dma_ld("sp", xt, xt[:, :], src[t * 128:(t + 1) * 128, :])
                        p.op("act", lambda e, xt=xt, ss=ss: e.activation(
                            out=junk[:, :], in_=xt[:, :], func=AF.Square, accum_out=ss[:, :]),
                            reads=(xt,), writes=(junk, ss))
                        p.op("dve", lambda e, ss=ss, rs=rs: e.tensor_scalar(
                            out=rs[:, :], in0=ss[:, :], scalar1=1.0 / D, scalar2=EPS, op0=ALU.mult, op1=ALU.add),
                            reads=(ss,), writes=(rs,))
                        p.op("act", lambda e, rs=rs: e.activation(out=rs[:, :], in_=rs[:, :], func=AF.Sqrt),
                             reads=(rs,), writes=(rs,))
                        p.op("dve", lambda e, rs=rs: e.reciprocal(out=rs[:, :], in_=rs[:, :]),
                            reads=(rs,), writes=(rs,))
                        p.op("dve", lambda e, xt=xt, xn=xn, rs=rs: e.tensor_scalar(
                            out=xn[:, :], in0=xt[:, :], scalar1=rs[:, 0:1], scalar2=None, op0=ALU.mult),
                            reads=(xt, rs), writes=(xn,))

                        def tr(e, xn=xn, pt=pt):
                            ins = None
                            for j in range(KC):
                                ins = e.transpose(out=pt[:, j * 128:(j + 1) * 128], in_=xn[:, j * 128:(j + 1) * 128],
                                                  identity=ident_bf[:, :])
                            return ins
                        p.op("pe", tr, reads=(xn, ident_bf), writes=(pt,))

                        def ev(e, pt=pt, hb=hb, ti=ti, r=r):
                            ins = None
                            for j in range(KC):
                                ins = e.activation(out=hb[:, ti, j, :], in_=pt[:, j * 128:(j + 1) * 128],
                                                   func=AF.Identity, scale=Avec[l][:, j, r:r + 1],
                                                   bias=mod[l][:, j, r:r + 1])
                            return ins
                        p.op("act", ev, reads=(pt, Avec[l], mod[l]), writes=(hb,))
                    dma_st("sp", hb, dst[tl[0]:tl[0] + len(tl)].rearrange("t p k c -> p t k c"), hb[:, 0:len(tl), :, :])
                p.end_stage()

        order = ["M", "N0", "R", "O0", "N1", "A", "O1"]
        nstage = len(order) if stop_after is None else order.index(stop_after) + 1
        if only is not None:
            nstage = 6 if "A" in only else 0
        if nstage >= 2 and only is None:
            stage_norm(0, x_loc, 18, ctx_loc)

        with contextlib.ExitStack() as st:
          if nstage >= 3 and only is None:
              Wr = Ring([p.sb(st, "Wp%d" % i, [128, KC, 256], BF16, dma=True) for i in range(3)])
              hb_r = Ring([p.sb(st, "hs%d" % i, [128, 2, KC, 128], BF16, dma=True) for i in range(3)])
              u = [p.sb(st, "u%d" % j, [128, NU + 4], F32) for j in range(2)]
              ucx = [p.sb(st, "ucx%d" % j, [128, CTX + 4], F32) for j in range(2)]
              uc = [p.sb(st, "uc%d" % j, [128, NH], F32) for j in range(2)]
              ucc = [p.sb(st, "ucc%d" % j, [128, CTX], F32) for j in range(2)]
              ucb = [p.sb(st, "ucb%d" % j, [128, NH], BF16) for j in range(2)]
              uccb = [p.sb(st, "uccb%d" % j, [128, CTX], BF16) for j in range(2)]
              sg = [[p.sb(st, "sg%d_%d" % (q, j), [128, NH], BF16) for j in range(2)] for q in range(2)]
              sgc = [[p.sb(st, "sgc%d_%d" % (q, j), [128, CTX], BF16) for j in range(2)] for q in range(2)]
              hA = p.sb(st, "hA", [128, NH], F32)
              hAc = p.sb(st, "hAc", [128, CTX], F32)
              hBc = p.sb(st, "hBc", [128, CTX], F32)
              zcb_r = Ring([p.sb(st, "zcb%d" % i, [128, CTX], BF16, dma=True) for i in range(1)])
              t1_r = Ring([p.sb(st, "t1_%d" % i, [128, 512], F32) for i in range(2)])
              t2_r = Ring([p.sb(st, "t2_%d" % i, [128, 512], F32) for i in range(2)])
              t3_r = Ring([p.sb(st, "t3_%d" % i, [128, 512], F32, dma=True) for i in range(2)])
              t4_r = Ring([p.sb(st, "t4_%d" % i, [128, 512], F32, dma=True) for i in range(2)])
              zeros = p.sb(st, "zeros", [128, 512], F32, const=True)
              GW_r = Ring([p.sb(st, "GW%d" % i, [128, 4, 2, 256], BF16, dma=True) for i in range(2)])
              st3_r = Ring([p.sb(st, "st3_%d" % i, [128, 1], F32) for i in range(4)])
              st4_r = Ring([p.sb(st, "st4_%d" % i, [128, 1], F32) for i in range(4)])
              conv5 = p.sb(st, "conv5", [128, KC, 5], F32, dma=True)
              convb = p.sb(st, "convb", [128, KC], F32, dma=True)
              gb_t = p.sb(st, "gb_t", [128, 4, KC], F32, dma=True)
              lam_t = p.sb(st, "lam_t", [128, 2, KC], F32, dma=True)
              cneg = p.sb(st, "cneg", [128, 2, KC], F32)
              SAbuf = p.sb(st, "SAbuf", [128, KC], F32, dma=True)
              gsb = p.sb(st, "gsb", [128, 2, KC], F32, dma=True)
              sel = p.sb(st, "sel", [128, 2], F32, dma=True)
              pp_r = Ring([p.ps(st, "pp%d" % i, [128, 512], F32) for i in range(3)])
              pr_r = Ring([p.ps(st, "pr%d" % i, [128, 512], F32) for i in range(2)])
              pi_r = Ring([p.ps(st, "pi%d" % i, [128, 512], F32) for i in range(2)])

              dma_ld("sp", conv5, conv5[:, :, :], conv5_fm)
              dma_ld("sp", convb, convb[:, :], convb_fm)
              dma_ld("sp", gb_t, gb_t[:, :, :], gate_b_fm)
              dma_ld("sp", lam_t, lam_t[:, :, :], lam_fm)
              dma_ld("sp", sel, sel[:, :], sel_in)
              p.op("dve", lambda e: e.memset(zeros[:, :], 0.0), writes=(zeros,))
              for j in range(2):
                  p.op("dve", lambda e, j=j: e.memset(u[j][:, :], 0.0), writes=(u[j],))
                  p.op("dve", lambda e, j=j: e.memset(ucx[j][:, :], 0.0), writes=(ucx[j],))
              p.op("act", lambda e: e.activation(out=cneg[:, :, :], in_=lam_t[:, :, :], func=AF.Exp, scale=-1.0),
                   reads=(lam_t,), writes=(cneg,))
              p.op("act", lambda e: e.activation(out=cneg[:, :, :], in_=cneg[:, :, :], func=AF.Ln, bias=1.0),
                   reads=(cneg,), writes=(cneg,))
              p.op("dve", lambda e: e.tensor_scalar(out=cneg[:, :, :], in0=cneg[:, :, :], scalar1=-8.0, scalar2=None,
                                                     op0=ALU.mult), reads=(cneg,), writes=(cneg,))

              tblocks = [(i * 256, 256) for i in range(8)] + [(2048, 130)]
              chunks = [(0, 512), (512, 512), (1024, 512), (1536, 512), (2048, 128)]

              def proj_gen(gbi):
                  q = gbi % 2
                  Ws = []
                  for c0 in (gbi * 256, D + gbi * 256):
                      W = Wr.next()
                      dma_ld("pool", W, W[:, :, :], rg_w_in[:, c0:c0 + 256].rearrange("(k p) n -> p k n", p=128))
                      Ws.append(W)
                  GW = GW_r.next()
                  GWs[gbi] = GW
                  for gi in range(4):
                      p.op("pool", lambda e, GW=GW, gi=gi, gbi=gbi: e.dma_start(
                          out=GW[:, gi, :, :], in_=gate_w[gi, gbi].rearrange("(kh p) j -> p kh j", p=128)),
                          writes=(GW,), dma=GW)
                  seqs = [("c", 0, CTX)] + [("l", t0, n) for t0, n in tblocks]
                  for kind, t0, n in seqs:
                      hb = hb_r.next()
                      srcv = hTc[0:2] if kind == "c" else hT[t0 // 128:t0 // 128 + 2]
                      dma_ld("sp", hb, hb[:, :, :, :], srcv.rearrange("t p k c -> p t k c"))
                      for ci in range(4):
                          pp = pp_r.next()

                          def mm(e, W=Ws[ci // 2], jj=ci % 2, hb=hb, pp=pp, n=n):
                              ins = None
                              if n == 130:
                                  for k in range(KC):
                                      ins = e.matmul(pp[:, 0:128], lhsT=W[:, k, jj * 128:(jj + 1) * 128],
                                                     rhs=hb[:, 0, k, :], start=(k == 0), stop=(k == KC - 1))
                                  for k in range(KC):
                                      ins = e.matmul(pp[:, 128:130], lhsT=W[:, k, jj * 128:(jj + 1) * 128],
                                                     rhs=hb[:, 1, k, 0:2], start=(k == 0), stop=(k == KC - 1))
                              else:
                                  for k in range(KC):
                                      ins = e.matmul(pp[:, 0:256].rearrange("p (t c) -> p t c", c=128),
                                                     lhsT=W[:, k, jj * 128:(jj + 1) * 128], rhs=hb[:, 0:2, k, :],
                                                     start=(k == 0), stop=(k == KC - 1))
                              return ins
                          p.op("pe", mm, reads=(Ws[ci // 2], hb), writes=(pp,))
                          j = ci % 2
                          if ci < 2:
                              if kind == "c":
                                  dstt, dsta = ucx[j], ucx[j][:, 2:2 + CTX]
                              else:
                                  dstt, dsta = u[j], u[j][:, 2 + t0:2 + t0 + n]
                              p.op("act", lambda e, pp=pp, dsta=dsta, n=n: e.activation(
                                  out=dsta, in_=pp[:, 0:n], func=AF.Copy), reads=(pp,), writes=(dstt,))
                          else:
                              n2 = min(n, 128) if (kind == "l" and t0 == 2048) else n
                              if kind == "c":
                                  dstt, dsta = sgc[q][j], sgc[q][j][:, 0:CTX]
                              else:
                                  dstt, dsta = sg[q][j], sg[q][j][:, t0:t0 + n2]
                              p.op("act", lambda e, pp=pp, dsta=dsta, n2=n2: e.activation(
                                  out=dsta, in_=pp[:, 0:n2], func=AF.Silu), reads=(pp,), writes=(dstt,))
                      yield
              def elem_gen(gbi):
                  q = gbi % 2
                  GW = GWs[gbi]
                  for j in range(2):
                      ch = gbi * 2 + j
                      for (ut, uct, ucbt, nn) in ((ucx[j], ucc[j], uccb[j], CTX), (u[j], uc[j], ucb[j], NH)):
                          p.op("dve", lambda e, ut=ut, uct=uct, nn=nn, ch=ch: e.tensor_scalar(
                              out=uct[:, 0:nn], in0=ut[:, 0:nn], scalar1=conv5[:, ch, 0:1], scalar2=convb[:, ch:ch + 1],
                              op0=ALU.mult, op1=ALU.add), reads=(ut, conv5, convb), writes=(uct,))
                          for k in range(1, 5):
                              p.op("dve", lambda e, ut=ut, uct=uct, nn=nn, ch=ch, k=k: e.scalar_tensor_tensor(
                                  out=uct[:, 0:nn], in0=ut[:, k:k + nn], scalar=conv5[:, ch, k:k + 1], in1=uct[:, 0:nn],
                                  op0=ALU.mult, op1=ALU.add), reads=(ut, conv5, uct), writes=(uct,))
                          p.op("act", lambda e, uct=uct, ucbt=ucbt, nn=nn: e.activation(
                              out=ucbt[:, 0:nn], in_=uct[:, 0:nn], func=AF.Copy), reads=(uct,), writes=(ucbt,))
                  yield
                  def half_gen(j):
                      ch = gbi * 2 + j

                      def gate_ab(d, ucb_pair, uct, c0, n):
                          pr = pr_r.next(); pi = pi_r.next()
                          t1 = t1_r.next(); t2 = t2_r.next()

                          def mm(e, pr=pr, pi=pi):
                              ins = None
                              for (pt_, gi) in ((pr, 2 * d), (pi, 2 * d + 1)):
                                  for kh in range(2):
                                      ins = e.matmul(pt_[:, 0:n], lhsT=GW[:, gi, kh, j * 128:(j + 1) * 128],
                                                     rhs=ucb_pair[kh][:, c0:c0 + n], start=(kh == 0), stop=(kh == 1))
                              return ins
                          p.op("pe", mm, reads=(GW, ucb_pair[0], ucb_pair[1]), writes=(pr, pi))
                          p.op("act", lambda e: e.activation(out=t1[:, 0:n], in_=pr[:, 0:n], func=AF.Sigmoid,
                                                             bias=gb_t[:, 2 * d, ch:ch + 1]),
                               reads=(pr, gb_t), writes=(t1,))
                          p.op("act", lambda e: e.activation(out=t2[:, 0:n], in_=pi[:, 0:n], func=AF.Sigmoid,
                                                             bias=gb_t[:, 2 * d + 1, ch:ch + 1]),
                               reads=(pi, gb_t), writes=(t2,))
                          p.op("act", lambda e: e.activation(out=t1[:, 0:n], in_=t1[:, 0:n], func=AF.Exp,
                                                             scale=cneg[:, d, ch:ch + 1]),
                               reads=(t1, cneg), writes=(t1,))
                          return t1, t2

                      def finish_b(t1, t2, t3, uct, c0, n):
                          p.op("dve", lambda e: e.tensor_tensor(out=t2[:, 0:n], in0=t2[:, 0:n], in1=uct[:, c0:c0 + n],
                                                                op=ALU.mult), reads=(t2, uct), writes=(t2,))
                          p.op("act", lambda e: e.activation(out=t3[:, 0:n], in_=t1[:, 0:n], func=AF.Square),
                               reads=(t1,), writes=(t3,))
                          p.op("act", lambda e: e.activation(out=t3[:, 0:n], in_=t3[:, 0:n], func=AF.Sqrt,
                                                             scale=-1.0, bias=1.0), reads=(t3,), writes=(t3,))
                          p.op("dve", lambda e: e.tensor_tensor(out=t2[:, 0:n], in0=t2[:, 0:n], in1=t3[:, 0:n],
                                                                op=ALU.mult), reads=(t2, t3), writes=(t2,))

                      t1, t2 = gate_ab(0, uccb, ucc[j], 0, CTX)
                      t3 = t3_r.next()
                      finish_b(t1, t2, t3, ucc[j], 0, CTX)
                      p.op("dve", lambda e, t1=t1, t2=t2: e.tensor_tensor_scan(
                          out=hAc[:, :], data0=t1[:, 0:CTX], data1=t2[:, 0:CTX], initial=0.0,
                          op0=ALU.mult, op1=ALU.add), reads=(t1, t2), writes=(hAc,))
                      yield
                      for ci_, (c0, n) in enumerate(chunks):
                          t1, t2 = gate_ab(0, ucb, uc[j], c0, n)
                          t3 = t3_r.next()
                          finish_b(t1, t2, t3, uc[j], c0, n)
                          init = hAc[:, CTX - 1:CTX] if ci_ == 0 else hA[:, c0 - 1:c0]
                          p.op("dve", lambda e, t1=t1, t2=t2, c0=c0, n=n, init=init: e.tensor_tensor_scan(
                              out=hA[:, c0:c0 + n], data0=t1[:, 0:n], data1=t2[:, 0:n], initial=init,
                              op0=ALU.mult, op1=ALU.add), reads=(t1, t2, hA, hAc), writes=(hA,))
                          yield
                      p.op("act", lambda e, ch=ch: e.activation(out=SAbuf[:, ch:ch + 1], in_=hA[:, 1919:1920],
                                                                func=AF.Copy), reads=(hA,), writes=(SAbuf,))
                      t1, t2 = gate_ab(1, uccb, ucc[j], 0, CTX)
                      t3 = t3_r.next()
                      finish_b(t1, t2, t3, ucc[j], 0, CTX)
                      p.op("dve", lambda e, t1=t1, t2=t2: e.tensor_tensor_scan(
                          out=hBc[:, ::-1], data0=t1[:, 0:CTX][:, ::-1], data1=t2[:, 0:CTX][:, ::-1], initial=0.0,
                          op0=ALU.mult, op1=ALU.add), reads=(t1, t2), writes=(hBc,))
                      zcb = zcb_r.next()
                      p.op("dve", lambda e: e.tensor_tensor(out=hBc[:, :], in0=hBc[:, :], in1=hAc[:, :], op=ALU.add),
                           reads=(hBc, hAc), writes=(hBc,))
                      p.op("dve", lambda e, zcb=zcb: e.tensor_tensor(out=zcb[:, :], in0=hBc[:, :], in1=sgc[q][j][:, :],
                                                                    op=ALU.mult), reads=(hBc, sgc[q][j]), writes=(zcb,))
                      dma_st("sp", zcb, ZCTX[ch * 128:(ch + 1) * 128, :], zcb[:, :])
                      yield
                      prev3 = None; prev4 = None; prevn = None
                      for ci_ in range(len(chunks) - 1, -1, -1):
                          c0, n = chunks[ci_]
                          t1, t2 = gate_ab(1, ucb, uc[j], c0, n)
                          t3 = t3_r.next(); t4 = t4_r.next()
                          finish_b(t1, t2, t3, uc[j], c0, n)
                          if prev4 is None:
                              p.op("dve", lambda e, t1=t1, t4=t4, n=n: e.tensor_tensor_scan(
                                  out=t4[:, 0:n][:, ::-1], data0=t1[:, 0:n][:, ::-1], data1=zeros[:, 0:n], initial=1.0,
                                  op0=ALU.mult, op1=ALU.add), reads=(t1, zeros), writes=(t4,))
                              p.op("dve", lambda e, t1=t1, t2=t2, t3=t3, n=n: e.tensor_tensor_scan(
                                  out=t3[:, 0:n][:, ::-1], data0=t1[:, 0:n][:, ::-1], data1=t2[:, 0:n][:, ::-1],
                                  initial=0.0, op0=ALU.mult, op1=ALU.add), reads=(t1, t2), writes=(t3,))
                          else:
                              p.op("dve", lambda e, t1=t1, t4=t4, n=n, st4=st4: e.tensor_tensor_scan(
                                  out=t4[:, 0:n][:, ::-1], data0=t1[:, 0:n][:, ::-1], data1=zeros[:, 0:n],
                                  initial=st4[:, 0:1], op0=ALU.mult, op1=ALU.add), reads=(t1, zeros, st4), writes=(t4,))
                              p.op("dve", lambda e, t1=t1, t2=t2, t3=t3, n=n, st3=st3: e.tensor_tensor_scan(
                                  out=t3[:, 0:n][:, ::-1], data0=t1[:, 0:n][:, ::-1], data1=t2[:, 0:n][:, ::-1],
                                  initial=st3[:, 0:1], op0=ALU.mult, op1=ALU.add), reads=(t1, t2, st3), writes=(t3,))
                          st3 = st3_r.next()
                          st4 = st4_r.next()
                          p.op("act", lambda e, t3=t3, st3=st3: e.activation(out=st3[:, :], in_=t3[:, 0:1], func=AF.Copy),
                               reads=(t3,), writes=(st3,))
                          p.op("act", lambda e, t4=t4, st4=st4: e.activation(out=st4[:, :], in_=t4[:, 0:1], func=AF.Copy),
                               reads=(t4,), writes=(st4,))
                          prev4 = t4
                          p.op("dve", lambda e, t3=t3, c0=c0, n=n: e.tensor_tensor(
                              out=t3[:, 0:n], in0=t3[:, 0:n], in1=hA[:, c0:c0 + n], op=ALU.add),
                              reads=(t3, hA), writes=(t3,))
                          p.op("dve", lambda e, t3=t3, c0=c0, n=n: e.tensor_tensor(
                              out=t3[:, 0:n], in0=t3[:, 0:n], in1=sg[q][j][:, c0:c0 + n], op=ALU.mult),
                              reads=(t3, sg[q][j]), writes=(t3,))
                          p.op("dve", lambda e, t4=t4, c0=c0, n=n: e.tensor_tensor(
                              out=t4[:, 0:n], in0=t4[:, 0:n], in1=sg[q][j][:, c0:c0 + n], op=ALU.mult),
                              reads=(t4, sg[q][j]), writes=(t4,))
                          dma_st("sp", t3, Z0[ch * 128:(ch + 1) * 128, c0:c0 + n], t3[:, 0:n])
                          dma_st("sp", t4, ZC[ch * 128:(ch + 1) * 128, c0:c0 + n], t4[:, 0:n])
                          yield
                  for j_ in range(2):
                      yield from half_gen(j_)
              GWs = {}
              for _ in proj_gen(0):
                  pass
              for gbi_ in range(16):
                  eg = elem_gen(gbi_)
                  pg = proj_gen(gbi_ + 1) if gbi_ < 15 else None
                  next(eg)
                  ne = 0
                  for _ in eg:
                      ne += 1
                      if pg is not None and ne % 2 == 0:
                          if next(pg, "end") == "end":
                              pg = None
                  if pg is not None:
                      for _ in pg:
                          pass
              p.op("pool", lambda e: e.dma_start(out=gin[:, :], in_=SAbuf[:, :]), reads=(SAbuf,), dma=SAbuf)
              p.barrier()
              p.cnt[ccsem] += 1

              def cc(e):
                  return e.collective_compute("AllGather", ALU.bypass,
                                              replica_groups=[[0, 1], [2, 3], [4, 5], [6, 7]],
                                              ins=[gin.ap().opt()], outs=[gout.ap().opt()])
              p.prog["pool"].append(([], cc, (ccsem, 1)))
              p.prog["pool"].append(([(ccsem, p.cnt[ccsem])], None, None))
              p.seen["pool"][ccsem] = p.cnt[ccsem]
              dma_ld("pool", gsb, gsb[:, :, :], gout.ap().rearrange("(r p) k -> p r k", p=128))
              p.op("dve", lambda e: e.tensor_scalar(out=Sin[:, :], in0=gsb[:, 0, :], scalar1=sel[:, 0:1], scalar2=None,
                                                     op0=ALU.mult), reads=(gsb, sel), writes=(Sin,))
              p.op("dve", lambda e: e.scalar_tensor_tensor(out=Sin[:, :], in0=gsb[:, 1, :], scalar=sel[:, 1:2],
                                                            in1=Sin[:, :], op0=ALU.mult, op1=ALU.add),
                   reads=(gsb, sel, Sin), writes=(Sin,))
              p.end_stage()

        def stage_out(l, w_out_ap, blocks):
            with contextlib.ExitStack() as st:
                zT_r = Ring([p.sb(st, "zT%d" % i, [128, KC, 512], BF16, dma=True) for i in range(2)])
                Wo_r = Ring([p.sb(st, "Wo%d" % i, [128, KC, 512], BF16, dma=True) for i in range(2)])
                z0_r = Ring([p.sb(st, "z0_%d" % i, [128, 512], F32, dma=True) for i in range(3)])
                zc_r = Ring([p.sb(st, "zc_%d" % i, [128, 512], F32, dma=True) for i in range(3)])
                gbc = [p.sb(st, "gbc%d" % r, [128, D], F32, dma=True) for r in range(2)]
                xc_r = Ring([p.sb(st, "xc%d" % i, [128, 512], F32, dma=True) for i in range(3)])
                yo_r = Ring([p.sb(st, "yo%d" % i, [128, 512], F32, dma=True) for i in range(3)])
                po_r = Ring([p.ps(st, "po%d" % i, [128, 512], F32) for i in range(4)])
                for r in range(2):
                    dma_ld("sp", gbc[r], gbc[r][:, :], grow[l, r].partition_broadcast(128))
                for blk in blocks:
                    n = blk["n"]; t0 = blk["t0"]
                    zT = zT_r.next()
                    if blk["zmode"] == "corr":
                        for c in range(KC):
                            z0 = z0_r.next(); zc = zc_r.next()
                            dma_ld("sp", z0, z0[:, 0:n], Z0[c * 128:(c + 1) * 128, t0:t0 + n])
                            dma_ld("sp", zc, zc[:, 0:n], ZC[c * 128:(c + 1) * 128, t0:t0 + n])
                            p.op("dve", lambda e, z0=z0, zc=zc, zT=zT, c=c, n=n: e.scalar_tensor_tensor(
                                out=zT[:, c, 0:n], in0=zc[:, 0:n], scalar=Sin[:, c:c + 1], in1=z0[:, 0:n],
                                op0=ALU.mult, op1=ALU.add), reads=(z0, zc, Sin), writes=(zT,))
                    else:
                        zsrc = blk["zsrc"]
                        dma_ld("sp", zT, zT[:, :, 0:n], zsrc.rearrange("(k p) t -> p k t", p=128)[:, :, t0:t0 + n])
                    for nb in range(8):
                        Wo = Wo_r.next()
                        dma_ld("pool", Wo, Wo[:, :, :],
                               w_out_ap[:, nb * 512:(nb + 1) * 512].rearrange("(k p) n -> p k n", p=128))
                        for tt in range(n // 128):
                            po = po_r.next(); xc = xc_r.next(); yo = yo_r.next()
                            r0 = t0 + tt * 128

                            def mm(e, zT=zT, Wo=Wo, po=po, tt=tt):
                                ins = None
                                for k in range(KC):
                                    ins = e.matmul(po[:, :], lhsT=zT[:, k, tt * 128:(tt + 1) * 128], rhs=Wo[:, k, :],
                                                   start=(k == 0), stop=(k == KC - 1))
                                return ins
                            p.op("pe", mm, reads=(zT, Wo), writes=(po,))
                            dma_ld("sp", xc, xc[:, :], blk["xsrc"][r0:r0 + 128, nb * 512:(nb + 1) * 512])
                            g = gbc[blk["grow"]]
                            p.op("dve", lambda e, po=po, yo=yo, g=g, nb=nb: e.tensor_tensor(
                                out=yo[:, :], in0=po[:, :], in1=g[:, nb * 512:(nb + 1) * 512], op=ALU.mult),
                                reads=(po, g), writes=(yo,))
                            p.op("dve", lambda e, yo=yo, xc=xc: e.tensor_tensor(
                                out=yo[:, :], in0=yo[:, :], in1=xc[:, :], op=ALU.add), reads=(yo, xc), writes=(yo,))
                            dma_st("sp", yo, blk["dst"][r0:r0 + 128, nb * 512:(nb + 1) * 512], yo[:, :])
                p.end_stage()

        blocks0 = [dict(zmode="ctx", zsrc=ZCTX, t0=0, n=CTX, xsrc=ctx_loc, dst=CTX1, grow=1)]
        for t0, n in [(0, 512), (512, 512), (1024, 512), (1536, 512), (2048, 128)]:
            blocks0.append(dict(zmode="corr", t0=t0, n=n, xsrc=x_loc, dst=X1, grow=0))
        if nstage >= 4 and only is None:
            stage_out(0, rg_w_out, blocks0)
        if (nstage >= 5 and only is None) or (only is not None and 'N1' in only):
            stage_norm(1, X1, 17, CTX1)

        with contextlib.ExitStack() as st:
          if nstage >= 6:
              Wr = Ring([p.sb(st, "Wq%d" % i, [128, KC, 128], BF16, dma=True) for i in range(6)])
              hb_r = Ring([p.sb(st, "ha%d" % i, [128, 4, KC, 128], BF16, dma=True) for i in range(2)])
              QT = p.sb(st, "QT", [128, 4, NOWN], BF16)
              KT = p.sb(st, "KT", [128, NH + CTX], BF16)
              Vt = p.sb(st, "Vt", [128, 19, 128], BF16)
              OT = p.sb(st, "OT", [128, 4, NOWN], BF16)
              cosT = p.sb(st, "cosT", [128, NH], F32, dma=True)
              sinT = p.sb(st, "sinT", [128, NH], F32, dma=True)
              rotm = p.sb(st, "rotm", [128, 128], F32, dma=True)
              ones_f = p.sb(st, "ones_f", [128, 128], F32)
              ones_b = p.sb(st, "ones_b", [128, 128], BF16)
              qkn = p.sb(st, "qkn", [128, 2], F32, dma=True)
              esink = p.sb(st, "esink", [128, 32], F32, dma=True)
              masks = p.sb(st, "masks", [128, 2, 512], BF16, dma=True)
              sq_r = Ring([p.sb(st, "sq%d" % i, [128, 512], F32) for i in range(1)])
              qr_r = Ring([p.sb(st, "qr%d" % i, [128, 512], F32) for i in range(2)])
              rs_r = Ring([p.sb(st, "rsa%d" % i, [128, 512], F32) for i in range(2)])
              tq_r = Ring([p.sb(st, "tq%d" % i, [128, 512], F32) for i in range(1)])
              vb_r = Ring([p.sb(st, "vb%d" % i, [128, 512], BF16) for i in range(2)])
              PT_r = Ring([p.sb(st, "PT%d" % i, [128, 512], BF16) for i in range(3)])
              den_r = Ring([p.sb(st, "den%d" % i, [128, 512], F32) for i in range(1)])
              zt_r = Ring([p.sb(st, "zt%d" % i, [128, 512], BF16, dma=True) for i in range(3)])
              gs_r = Ring([p.sb(st, "gs%d" % i, [128, 512], F32) for i in range(2)])
              pp_r = Ring([p.ps(st, "pa%d" % i, [128, 512], F32) for i in range(2)])
              px_r = Ring([p.ps(st, "px%d" % i, [128, 512], F32) for i in range(2)])
              pS_r = Ring([p.ps(st, "pS%d" % i, [128, 512], F32) for i in range(2)])
              pO = p.ps(st, "pO", [128, 512], F32)
              pR = p.ps(st, "pR", [128, 512], F32)

              dma_ld("sp", cosT, cosT[:, :], cos_fm[:, 0:NH])
              dma_ld("sp", sinT, sinT[:, :], sin_fm[:, 0:NH])
              dma_ld("sp", rotm, rotm[:, :], rot_m)
              dma_ld("sp", qkn, qkn[:, :], qk_norm_fm)
              dma_ld("sp", esink, esink[:, :], sink_bc)
              p.op("act", lambda e: e.activation(out=esink[:, :], in_=esink[:, :], func=AF.Exp),
                   reads=(esink,), writes=(esink,))
              p.op("pool", lambda e: e.dma_start(out=masks[:, :, :], in_=mask_in.rearrange("m p n -> p m n")),
                   writes=(masks,), dma=masks)
              p.op("dve", lambda e: e.memset(ones_f[:, :], 1.0), writes=(ones_f,))
              p.op("dve", lambda e: e.memset(ones_b[:, :], 1.0), writes=(ones_b,))

              SCALE = 128.0 ** -0.5

              def project(cols, seqs, evac):
                  Ws = []
                  for c0 in cols:
                      W = Wr.next()
                      dma_ld("pool", W, W[:, :, :], at_w_in[:, c0:c0 + 128].rearrange("(k p) n -> p k n", p=128))
                      Ws.append(W)
                  for kind, t0, n in seqs:
                      hb = hb_r.next()
                      nt_ = n // 128
                      srcv = hTc[0:2] if kind == "c" else hT[t0 // 128:t0 // 128 + nt_]
                      dma_ld("sp", hb, hb[:, 0:nt_, :, :], srcv.rearrange("t p k c -> p t k c"))
                      for ci in range(len(cols)):
                          if not evac(ci, kind, t0, n, None):
                              continue
                          pp = pp_r.next()

                          def mm(e, W=Ws[ci], hb=hb, pp=pp, n=n, nt_=nt_):
                              ins = None
                              for k in range(KC):
                                  if nt_ == 1:
                                      ins = e.matmul(pp[:, 0:128], lhsT=W[:, k, :], rhs=hb[:, 0, k, :],
                                                     start=(k == 0), stop=(k == KC - 1))
                                  else:
                                      ins = e.matmul(pp[:, 0:n].rearrange("p (t c) -> p t c", c=128), lhsT=W[:, k, :],
                                                     rhs=hb[:, 0:nt_, k, :], start=(k == 0), stop=(k == KC - 1))
                              return ins
                          p.op("pe", mm, reads=(Ws[ci], hb), writes=(pp,))
                          evac(ci, kind, t0, n, pp)

              def norm_rope(pp, n, wcol, rope_t0, dst_tk, dst_ap):
                  sq = sq_r.next(); qr = qr_r.next(); rs = rs_r.next(); tq = tq_r.next()
                  px = px_r.next()
                  p.op("act", lambda e: e.activation(out=sq[:, 0:n], in_=pp[:, 0:n], func=AF.Square),
                       reads=(pp,), writes=(sq,))
                  p.op("act", lambda e: e.activation(out=qr[:, 0:n], in_=pp[:, 0:n], func=AF.Copy),
                       reads=(pp,), writes=(qr,))
                  p.op("pe", lambda e: e.matmul(px[:, 0:n], lhsT=ones_f[:, :], rhs=sq[:, 0:n], start=True, stop=True),
                       reads=(ones_f, sq), writes=(px,))
                  p.op("dve", lambda e: e.tensor_scalar(out=rs[:, 0:n], in0=px[:, 0:n], scalar1=1.0 / 128, scalar2=EPS,
                                                         op0=ALU.mult, op1=ALU.add), reads=(px,), writes=(rs,))
                  p.op("act", lambda e: e.activation(out=rs[:, 0:n], in_=rs[:, 0:n], func=AF.Sqrt),
                       reads=(rs,), writes=(rs,))
                  p.op("dve", lambda e: e.reciprocal(out=rs[:, 0:n], in_=rs[:, 0:n]), reads=(rs,), writes=(rs,))
                  if rope_t0 is None:
                      p.op("dve", lambda e: e.scalar_tensor_tensor(
                          out=dst_ap, in0=qr[:, 0:n], scalar=qkn[:, wcol:wcol + 1], in1=rs[:, 0:n],
                          op0=ALU.mult, op1=ALU.mult), reads=(qr, qkn, rs), writes=(dst_tk,))
                      return
                  p.op("dve", lambda e: e.scalar_tensor_tensor(
                      out=qr[:, 0:n], in0=qr[:, 0:n], scalar=qkn[:, wcol:wcol + 1], in1=rs[:, 0:n],
                      op0=ALU.mult, op1=ALU.mult), reads=(qr, qkn, rs), writes=(qr,))
                  px2 = px_r.next()
                  p.op("pe", lambda e: e.matmul(px2[:, 0:n], lhsT=rotm[:, :], rhs=qr[:, 0:n], start=True, stop=True),
                       reads=(rotm, qr), writes=(px2,))
                  p.op("dve", lambda e: e.tensor_tensor(out=tq[:, 0:n], in0=px2[:, 0:n],
                                                         in1=sinT[:, rope_t0:rope_t0 + n], op=ALU.mult),
                       reads=(px2, sinT), writes=(tq,))
                  p.op("dve", lambda e: e.tensor_tensor(out=qr[:, 0:n], in0=qr[:, 0:n],
                                                         in1=cosT[:, rope_t0:rope_t0 + n], op=ALU.mult),
                       reads=(qr, cosT), writes=(qr,))
                  p.op("dve", lambda e: e.tensor_tensor(out=dst_ap, in0=qr[:, 0:n], in1=tq[:, 0:n], op=ALU.add),
                       reads=(qr, tq), writes=(dst_tk,))

              own_blocks = [("l", 0, 512), ("l", 512, 512), ("l", 1024, 512), ("l", 1536, 512)]
              seqsA = [("c", 0, CTX)] + own_blocks + [("l", 2048, 128)]

              for h in range(8 if asub is None else asub.get('nh', 8)):
                  colsA = [D + h * 128, D + 1024 + h * 128] + [h * 512 + g * 128 for g in range(4)]

                  def evacA(ci, kind, t0, n, pp, h=h):
                      if ci >= 2 and (kind == "c" or t0 >= NOWN):
                          return False
                      if pp is None:
                          return True
                      if asub is not None and asub.get('simple', 0):
                          vb = vb_r.next()
                          p.op("act", lambda e: e.activation(out=vb[:, 0:n], in_=pp[:, 0:n], func=AF.Copy),
                               reads=(pp,), writes=(vb,))
                          return True
                      if ci == 0:
                          if kind == "c":
                              norm_rope(pp, n, 1, None, KT, KT[:, NH:NH + CTX])
                          else:
                              norm_rope(pp, n, 1, t0, KT, KT[:, t0:t0 + n])
                      elif ci == 1:
                          vb = vb_r.next()
                          p.op("act", lambda e: e.activation(out=vb[:, 0:n], in_=pp[:, 0:n], func=AF.Copy),
                               reads=(pp,), writes=(vb,))
                          px = px_r.next()
                          pxb = px[:, :].bitcast(BF16)

                          def tr(e):
                              ins = None
                              for i in range(n // 128):
                                  ins = e.transpose(out=pxb[:, i * 128:(i + 1) * 128], in_=vb[:, i * 128:(i + 1) * 128],
                                                    identity=ident_bf[:, :])
                              return ins
                          p.op("pe", tr, reads=(vb, ident_bf), writes=(px,))
                          kb0 = 17 if kind == "c" else t0 // 128
                          nb_ = n // 128
                          p.op("act", lambda e: e.activation(
                              out=Vt[:, kb0:kb0 + nb_, :],
                              in_=pxb[:, 0:nb_ * 128].rearrange("p (b d) -> p b d", d=128), func=AF.Copy),
                              reads=(px,), writes=(Vt,))
                      else:
                          g = ci - 2
                          norm_rope(pp, n, 0, t0, QT, QT[:, g, t0:t0 + n])
                      return True
                  project(colsA, seqsA, evacA)

                  for i in range(16 if asub is None else asub.get('nq', 16)):
                      kbs = [("c", 17, NH), ("c", 18, NH + 128)]
                      if i > 0:
                          kbs.append(("p", i - 1, (i - 1) * 128))
                      kbs.append(("o", i, i * 128))
                      kbs.append(("n", i + 1, (i + 1) * 128))
                      for ki, (kk, vb_i, kc0) in enumerate(kbs):
                          pS = pS_r.next(); PT = PT_r.next()
                          p.op("pe", lambda e, pS=pS, kc0=kc0, i=i: e.matmul(
                              pS[:, :].rearrange("p (g q) -> p g q", g=4), lhsT=KT[:, kc0:kc0 + 128],
                              rhs=QT[:, :, i * 128:(i + 1) * 128], start=True, stop=True),
                              reads=(KT, QT), writes=(pS,))
                          p.op("act", lambda e, pS=pS, PT=PT: e.activation(out=PT[:, :], in_=pS[:, :], func=AF.Exp,
                                                                           scale=SCALE), reads=(pS,), writes=(PT,))
                          if kk in ("p", "n"):
                              mi = 0 if kk == "p" else 1
                              p.op("dve", lambda e, PT=PT, mi=mi: e.tensor_tensor(
                                  out=PT[:, :], in0=PT[:, :], in1=masks[:, mi, :], op=ALU.mult),
                                  reads=(PT, masks), writes=(PT,))

                          def pv(e, PT=PT, vb_i=vb_i, ki=ki, last=(ki == len(kbs) - 1)):
                              e.matmul(pO[:, :], lhsT=Vt[:, vb_i, :], rhs=PT[:, :], start=(ki == 0), stop=last)
                              return e.matmul(pR[:, :], lhsT=ones_b[:, :], rhs=PT[:, :], start=(ki == 0), stop=last)
                          p.op("pe", pv, reads=(Vt, PT, ones_b), writes=(pO, pR))
                      den = den_r.next()
                      for g in range(4):
                          p.op("dve", lambda e, den=den, g=g, h=h: e.tensor_scalar(
                              out=den[:, g * 128:(g + 1) * 128], in0=pR[:, g * 128:(g + 1) * 128],
                              scalar1=esink[:, h * 4 + g:h * 4 + g + 1], scalar2=None, op0=ALU.add),
                              reads=(pR, esink), writes=(den,))
                      p.op("dve", lambda e, den=den: e.reciprocal(out=den[:, :], in_=den[:, :]),
                           reads=(den,), writes=(den,))
                      p.op("dve", lambda e, den=den, i=i: e.tensor_tensor(
                          out=OT[:, :, i * 128:(i + 1) * 128], in0=pO[:, :].rearrange("p (g q) -> p g q", g=4),
                          in1=den[:, :].rearrange("p (g q) -> p g q", g=4), op=ALU.mult),
                          reads=(pO, den), writes=(OT,))

                  colsB = [6144 + h * 512 + g * 128 for g in range(4)]

                  def evacB(ci, kind, t0, n, pp, h=h):
                      if pp is None:
                          return True
                      gs = gs_r.next(); zt = zt_r.next()
                      p.op("act", lambda e: e.activation(out=gs[:, 0:n], in_=pp[:, 0:n], func=AF.Silu),
                           reads=(pp,), writes=(gs,))
                      p.op("dve", lambda e: e.tensor_tensor(out=zt[:, 0:n], in0=gs[:, 0:n], in1=OT[:, ci, t0:t0 + n],
                                                             op=ALU.mult), reads=(gs, OT), writes=(zt,))
                      r0 = (h * 4 + ci) * 128
                      dma_st("sp", zt, Z1T[r0:r0 + 128, t0:t0 + n], zt[:, 0:n])
                      return True
                  if asub is None or asub.get('pb', 1):
                      project(colsB, own_blocks, evacB)
              p.end_stage()

        blocks1 = [dict(zmode="direct", zsrc=Z1T, t0=t0, n=512, xsrc=X1, dst=out, grow=0)
                   for t0 in (0, 512, 1024, 1536)]
        if nstage >= 7:
            stage_out(1, at_w_out, blocks1)

        p.check()
        p.emit()
    return nc


def _fm(v):
    v = np.asarray(v, np.float32)
    lead = v.shape[:-1]
    a = v.reshape(lead + (KC, 128))
    a = np.moveaxis(a, -1, 0)
    return np.ascontiguousarray(a)


def prepare_inputs(x, c, ctx, c_ctx, w_mod, b_mod, norm_w, rg_w_in, rg_conv_w, rg_conv_b, rg_w_r, rg_b_r,
                   rg_w_i, rg_b_i, rg_lam, rg_w_out, at_w_in, at_q_norm, at_k_norm, at_sink, at_w_out):
    f32 = np.float32
    shared = {}
    shared["w_mod"] = np.ascontiguousarray(w_mod, f32)
    shared["bmod_fm"] = np.ascontiguousarray(
        np.asarray(b_mod, f32).reshape(2, 96, 128).transpose(2, 0, 1))
    shared["normw_fm"] = _fm(norm_w)
    shared["rg_w_in"] = np.ascontiguousarray(rg_w_in[0], f32)
    shared["convb_fm"] = _fm(rg_conv_b[0])
    shared["rg_w_out"] = np.ascontiguousarray(rg_w_out[0], f32)
    shared["at_w_in"] = np.ascontiguousarray(at_w_in[0], f32)
    shared["qk_norm_fm"] = np.ascontiguousarray(np.stack([at_q_norm[0], at_k_norm[0]], axis=1), f32)
    shared["sink_bc"] = np.ascontiguousarray(np.broadcast_to(np.asarray(at_sink[0], f32)[None, :], (128, 32)))
    shared["at_w_out"] = np.ascontiguousarray(at_w_out[0], f32)
    ident = np.eye(128, dtype=f32)
    shared["ident_in"] = ident
    rot = np.zeros((128, 128), f32)
    for m in range(128):
        if (m % 64) < 32:
            rot[m + 32, m] = -1.0
        else:
            rot[m - 32, m] = 1.0
    shared["rot_m"] = rot
    kj = np.arange(128)[:, None]
    qi = np.arange(128)[None, :]
    mprev = (kj >= qi).astype(f32)
    mnext = (kj <= qi).astype(f32)
    shared["mask_in"] = np.ascontiguousarray(np.stack([np.tile(mprev, (1, 4)), np.tile(mnext, (1, 4))], 0))

    conv_w = np.asarray(rg_conv_w[0], f32)
    zero = np.zeros((1, D), f32)
    per_core = []
    for core in range(8):
        b, half = core // 2, core % 2
        m = dict(shared)
        if half == 0:
            idx = np.arange(NLOC)
            m["ctx_loc"] = np.ascontiguousarray(ctx[b], f32)
            conv5 = np.concatenate([conv_w, zero], 0)
        else:
            idx = 4095 - np.arange(NLOC)
            m["ctx_loc"] = np.ascontiguousarray(np.asarray(ctx[b], f32)[::-1])
            conv5 = np.concatenate([zero, conv_w[::-1]], 0)
        m["x_loc"] = np.ascontiguousarray(np.asarray(x[b], f32)[idx])
        m["conv5_fm"] = np.ascontiguousarray(np.moveaxis(_fm(conv5), 1, 2))
        m["c_fm"] = np.ascontiguousarray(np.moveaxis(_fm(np.stack([c[b], c_ctx], 0)), 1, 2))
        dA, dB = half, 1 - half
        m["gate_w"] = np.ascontiguousarray(np.stack([rg_w_r[0, dA], rg_w_i[0, dA], rg_w_r[0, dB], rg_w_i[0, dB]], 0), f32)
        m["gate_b_fm"] = _fm(np.stack([rg_b_r[0, dA], rg_b_i[0, dA], rg_b_r[0, dB], rg_b_i[0, dB]], 0))
        m["lam_fm"] = _fm(np.stack([rg_lam[0, dA], rg_lam[0, dB]], 0))
        t = idx.astype(np.float64)
        row = np.floor(t / 64.0)
        col = t - row * 64.0
        inv = 10000.0 ** (-np.arange(32, dtype=np.float64) * (2.0 / 64.0))
        dd = np.arange(128)
        pos = np.where(dd[:, None] < 64, row[None, :], col[None, :])
        ang = pos * inv[dd % 32][:, None]
        m["cos_fm"] = np.ascontiguousarray(np.cos(ang), f32)
        m["sin_fm"] = np.ascontiguousarray(np.sin(ang), f32)
        selv = np.zeros((128, 2), f32)
        selv[:, 1 - half] = 1.0
        m["sel_in"] = selv
        per_core.append(m)
    return per_core


_NC_CACHE = {}


def kernel(**inputs):
    inputs = {k: np.asarray(v) for k, v in inputs.items()}
    per_core = prepare_inputs(**inputs)
    if "nc" not in _NC_CACHE:
        _NC_CACHE["nc"] = build_program(DEBUG)
    nc = _NC_CACHE["nc"]
    res = run_bass_kernel_spmd(nc, per_core, core_ids=list(range(8)))
    outp = np.empty((4, 4096, D), np.float32)
    for core in range(8):
        b, half = core // 2, core % 2
        o = np.asarray(res.results[core]["out"], np.float32)
        if half == 0:
            outp[b, 0:NOWN] = o
        else:
            outp[b, NOWN:] = o[::-1]
    if DEBUG:
        kernel.last = res
    return outp
```

```python
import numpy as np
import ml_dtypes
import concourse.bass as bass
import concourse.mybir as mybir
from concourse.bass_utils import run_bass_kernel_spmd

F32 = mybir.dt.float32
BF16 = mybir.dt.bfloat16
ALU = mybir.AluOpType
AF = mybir.ActivationFunctionType

D = 4096
KC = 32
NOWN = 2048
NH = 2176
NU = 2178
NLOC = 2304
CTX = 256
EPS = 1e-6
ENG = ("pe", "act", "dve", "pool", "sp")
DEBUG = False
REUSE_DSEMS = True


class Tk:
    __slots__ = ("ap", "lw", "rd", "dsem", "const")

    def __init__(self, ap=None, dsem=None, const=False):
        self.ap = ap
        self.lw = None
        self.rd = {}
        self.dsem = dsem
        self.const = const

    def __getitem__(self, k):
        return self.ap[k]


class Prog:
    def __init__(self, nc, stack):
        self.nc = nc
        self.stack = stack
        self.prog = {e: [] for e in ENG}
        self.cnt = {}
        self.seen = {e: {} for e in ENG}
        self.esem = {}
        for e in ENG:
            s = stack.enter_context(nc.semaphore("es_" + e))
            self.esem[e] = s
            self.cnt[s] = 0
        self.free_dsems = []
        self.ndsem = 0
        self.stage_dsems = []

    def dsem(self):
        if self.free_dsems:
            s = self.free_dsems.pop()
        else:
            s = self.stack.enter_context(self.nc.semaphore("ds%d" % self.ndsem))
            self.ndsem += 1
            self.cnt[s] = 0
        self.stage_dsems.append(s)
        return s

    def sb(self, st, name, shape, dt, dma=False, const=False):
        self.uid = getattr(self, "uid", 0) + 1
        name = "%s_u%d" % (name, self.uid)
        t = st.enter_context(self.nc.sbuf_tensor(name, list(shape), dt))
        return Tk(t, self.dsem() if dma else None, const)

    def ps(self, st, name, shape, dt):
        self.uid = getattr(self, "uid", 0) + 1
        name = "%s_u%d" % (name, self.uid)
        t = st.enter_context(self.nc.psum_tensor(name, list(shape), dt))
        return Tk(t)

    def dram(self, ap=None):
        return Tk(ap)

    def op(self, eng, fn, reads=(), writes=(), dma=None):
        waits = {}
        seen = self.seen[eng]

        def need(tok):
            if tok is None:
                return
            sem, val = tok
            if seen.get(sem, 0) >= val:
                return
            if waits.get(sem, 0) < val:
                waits[sem] = val

        for t in reads:
            need(t.lw)
        for t in writes:
            need(t.lw)
            for sem, val in t.rd.items():
                need((sem, val))
        if eng == "pool" and dma is not None and len(writes) > 0:
            hist = self.__dict__.setdefault("pool_hist", [])
            if len(hist) >= 3:
                need(hist[-3])
        for sem, val in waits.items():
            seen[sem] = val
        if dma is not None:
            sem = dma.dsem
            self.cnt[sem] += 16
            inc = (sem, 16)
        else:
            sem = self.esem[eng]
            self.cnt[sem] += 1
            inc = (sem, 1)
        tok = (sem, self.cnt[sem])
        if eng == "pool" and dma is not None and len(writes) > 0:
            self.pool_hist.append(tok)
        self.prog[eng].append((list(waits.items()), fn, inc))
        for t in reads:
            if not t.const:
                if t.rd.get(sem, 0) < tok[1]:
                    t.rd[sem] = tok[1]
        for t in writes:
            t.lw = tok
            t.rd = {}
        return tok

    def barrier(self):
        for e in ENG:
            waits = []
            for sem, c in self.cnt.items():
                if c > 0 and self.seen[e].get(sem, 0) < c and sem is not self.esem[e]:
                    waits.append((sem, c))
                    self.seen[e][sem] = c
            self.prog[e].append((waits, None, None))

    def end_stage(self):
        self.barrier()
        if REUSE_DSEMS:
            self.free_dsems.extend(self.stage_dsems)
        self.stage_dsems = []

    def check(self):
        pos = {e: 0 for e in ENG}
        val = {}
        progress = True
        while progress:
            progress = False
            for e in ENG:
                lst = self.prog[e]
                while pos[e] < len(lst):
                    waits, fn, inc = lst[pos[e]]
                    if any(val.get(sem, 0) < v for sem, v in waits):
                        break
                    if inc is not None:
                        val[inc[0]] = val.get(inc[0], 0) + inc[1]
                    pos[e] += 1
                    progress = True
        stuck = {e: (pos[e], len(self.prog[e])) for e in ENG if pos[e] < len(self.prog[e])}
        for e in stuck:
            waits, fn, inc = self.prog[e][pos[e]]
            print("STUCK", e, stuck[e], [(str(sem), v, val.get(sem, 0)) for sem, v in waits])
        bad = {str(sem): (val.get(sem, 0), c) for sem, c in self.cnt.items() if val.get(sem, 0) != c}
        print("check: stuck=%s mismatched=%s" % (bool(stuck), bad))

    def emit(self):
        nc = self.nc
        prog = self.prog

        def replay(lst, e):
            for waits, fn, inc in lst:
                for sem, val in waits:
                    e.wait_ge(sem, val)
                if fn is not None:
                    ins = fn(e)
                    ins.then_inc(inc[0], inc[1])

        with nc.Block() as block:
            @block.tensor
            def _(e):
                replay(prog["pe"], e)

            @block.scalar
            def _(e):
                replay(prog["act"], e)

            @block.vector
            def _(e):
                replay(prog["dve"], e)

            @block.gpsimd
            def _(e):
                replay(prog["pool"], e)

            @block.sync
            def _(e):
                replay(prog["sp"], e)


class Ring:
    def __init__(self, tiles):
        self.tiles = tiles
        self.i = 0

    def next(self):
        t = self.tiles[self.i % len(self.tiles)]
        self.i += 1
        return t


def build_program(debug=False, stop_after=None, only=None, asub=None):
    import contextlib
    nc = bass.Bass("TRN2", target_bir_lowering=False)
    dk = "ExternalOutput" if debug else "Internal"

    NEED_A = ("at_w_in", "qk_norm_fm", "sink_bc", "cos_fm", "sin_fm", "rot_m", "ident_in", "mask_in",
              "normw_fm", "bmod_fm")

    def din(name, shape, dt=F32):
        if only is not None and name not in NEED_A:
            return None
        return nc.dram_tensor(name, list(shape), dt, kind="ExternalInput").ap()

    def dscr(name, shape, dt):
        if debug and (debug is True or name in debug):
            return nc.dram_tensor(name, list(shape), dt, kind="ExternalOutput").ap()
        return nc.dram_tensor(name, list(shape), dt).ap()

    x_loc = din("x_loc", [NLOC, D])
    ctx_loc = din("ctx_loc", [CTX, D])
    c_fm = din("c_fm", [128, KC, 2])
    w_mod = din("w_mod", [2, D, 3 * D])
    bmod_fm = din("bmod_fm", [128, 2, 96])
    normw_fm = din("normw_fm", [128, 2, KC])
    rg_w_in = din("rg_w_in", [D, 2 * D])
    conv5_fm = din("conv5_fm", [128, KC, 5])
    convb_fm = din("convb_fm", [128, KC])
    gate_w = din("gate_w", [4, 16, 256, 256])
    gate_b_fm = din("gate_b_fm", [128, 4, KC])
    lam_fm = din("lam_fm", [128, 2, KC])
    rg_w_out = din("rg_w_out", [D, D])
    at_w_in = din("at_w_in", [D, 10240])
    qk_norm_fm = din("qk_norm_fm", [128, 2])
    sink_bc = din("sink_bc", [128, 32])
    at_w_out = din("at_w_out", [D, D])
    cos_fm = din("cos_fm", [128, NLOC])
    sin_fm = din("sin_fm", [128, NLOC])
    rot_m = din("rot_m", [128, 128])
    ident_in = din("ident_in", [128, 128])
    mask_in = din("mask_in", [2, 128, 512])
    sel_in = din("sel_in", [128, 2])
    out = nc.dram_tensor("out", [NOWN, D], F32, kind="ExternalOutput").ap()

    hT = dscr("hT", [18, 128, KC, 128], BF16)
    hTc = dscr("hTc", [2, 128, KC, 128], BF16)
    Z0 = dscr("Z0", [D, NH], F32)
    ZC = dscr("ZC", [D, NH], F32)
    ZCTX = dscr("ZCTX", [D, CTX], BF16)
    X1 = dscr("X1", [NH, D], F32)
    CTX1 = dscr("CTX1", [CTX, D], F32)
    Z1T = dscr("Z1T", [D, NOWN], BF16)
    grow = dscr("grow", [2, 2, D], F32)
    gin = nc.dram_tensor("gin", [128, KC], F32)
    gout = nc.dram_tensor("gout", [256, KC], F32)

    with contextlib.ExitStack() as gst:
        p = Prog(nc, gst)
        gst.enter_context(nc.allow_non_contiguous_dma(reason="small param scatter"))
        gst.enter_context(nc.allow_low_precision(reason="bf16 matmul operands"))
        ccsem = gst.enter_context(nc.semaphore("ccsem"))
        p.cnt[ccsem] = 0

        ident_bf = p.sb(gst, "ident_bf", [128, 128], BF16, dma=True)
        ident_f = p.sb(gst, "ident_f", [128, 128], F32, dma=True)
        mod = [p.sb(gst, "mod%d" % l, [128, 96, 2], F32) for l in range(2)]
        Avec = [p.sb(gst, "Avec%d" % l, [128, KC, 2], F32) for l in range(2)]
        normw = p.sb(gst, "normw", [128, 2, KC], F32, dma=True)
        bmod = p.sb(gst, "bmod", [128, 2, 96], F32, dma=True)
        Sin = p.sb(gst, "Sin", [128, KC], F32)

        def dma_ld(eng, dst_tk, dst_ap, src_ap, reads=(), extra_w=()):
            p.op(eng, lambda e: e.dma_start(out=dst_ap, in_=src_ap), reads=reads,
                 writes=(dst_tk,) + tuple(extra_w), dma=dst_tk)

        def dma_st(eng, src_tk, dst_ap, src_ap, dst_tk=None):
            p.op(eng, lambda e: e.dma_start(out=dst_ap, in_=src_ap), reads=(src_tk,),
                 writes=(dst_tk,) if dst_tk is not None else (), dma=src_tk)

        dma_ld("sp", ident_f, ident_f[:, :], ident_in)
        dma_ld("pool", ident_bf, ident_bf[:, :], ident_in)
        dma_ld("sp", normw, normw[:, :, :], normw_fm)
        dma_ld("sp", bmod, bmod[:, :, :], bmod_fm)

        with contextlib.ExitStack() as st:
          if only is None:
              cf = p.sb(st, "cf", [128, KC, 2], F32, dma=True)
              scb = p.sb(st, "scb", [128, KC, 2], BF16)
              Wm = Ring([p.sb(st, "Wm%d" % i, [128, KC, 512], BF16, dma=True) for i in range(2)])
              psm = [p.ps(st, "psm%d" % l, [128, 512], F32) for l in range(2)]
              dma_ld("sp", cf, cf[:, :, :], c_fm)
              p.op("act", lambda e: e.activation(out=scb[:, :, :], in_=cf[:, :, :], func=AF.Silu),
                   reads=(cf,), writes=(scb,))
              for l in range(2):
                  for blk in range(24):
                      W = Wm.next()
                      src = w_mod[l, :, blk * 512:(blk + 1) * 512].rearrange("(k p) n -> p k n", p=128)
                      dma_ld("pool", W, W[:, :, :], src)

                      def mm(e, W=W, blk=blk, l=l):
                          ins = None
                          for j in range(4):
                              n = blk * 4 + j
                              for k in range(KC):
                                  ins = e.matmul(psm[l][:, n * 2:n * 2 + 2], lhsT=W[:, k, j * 128:(j + 1) * 128],
                                                 rhs=scb[:, k, :], start=(k == 0), stop=(k == KC - 1))
                          return ins
                      p.op("pe", mm, reads=(W, scb), writes=(psm[l],))
                  for r in range(2):
                      p.op("dve", lambda e, l=l, r=r: e.tensor_tensor(
                          out=mod[l][:, :, r], in0=psm[l][:, 0:192].rearrange("p (n r) -> p n r", r=2)[:, :, r],
                          in1=bmod[:, l, :], op=ALU.add), reads=(psm[l], bmod), writes=(mod[l],))
                  for r in range(2):
                      p.op("dve", lambda e, l=l, r=r: e.scalar_tensor_tensor(
                          out=Avec[l][:, :, r], in0=mod[l][:, 32:64, r], scalar=1.0, in1=normw[:, l, :],
                          op0=ALU.add, op1=ALU.mult), reads=(mod[l], normw), writes=(Avec[l],))
                      p.op("sp", lambda e, l=l, r=r: e.dma_start(
                          out=grow[l, r].rearrange("(k p) -> p k", p=128), in_=mod[l][:, 64:96, r]),
                          reads=(mod[l],), writes=(), dma=cf)
              p.end_stage()

        def stage_norm(l, src_lat, ntiles, src_ctx):
            with contextlib.ExitStack() as st:
                xt_r = Ring([p.sb(st, "xt%d" % i, [128, D], F32, dma=True) for i in range(2)])
                xn_r = Ring([p.sb(st, "xn%d" % i, [128, D], BF16) for i in range(2)])
                junk = p.sb(st, "junk", [128, D], BF16)
                ss_r = Ring([p.sb(st, "ss%d" % i, [128, 1], F32) for i in range(4)])
                rs_r = Ring([p.sb(st, "rs%d" % i, [128, 1], F32) for i in range(4)])
                pt_r = Ring([p.ps(st, "pt%d" % i, [128, D], BF16) for i in range(2)])
                hb_r = Ring([p.sb(st, "hb%d" % i, [128, 4, KC, 128], BF16, dma=True) for i in range(2)])
                jobs = []
                nblk = (ntiles + 3) // 4
                for b in range(nblk):
                    tl = list(range(b * 4, min(ntiles, b * 4 + 4)))
                    jobs.append((src_lat, tl, hT, b * 512, 0))
                jobs.append((src_ctx, [0, 1], hTc, 0, 1))
                for src, tl, dst, c0, r in jobs:
                    hb = hb_r.next()
                    for ti, t in enumerate(tl):
                        xt = xt_r.next(); xn = xn_r.next(); ss = ss_r.next(); rs = rs_r.next(); pt = pt_r.next()
                        dma_ld("sp", xt, xt[:, :], src[t * 128:(t + 1) * 128, :])
                        p.op("act", lambda e, xt=xt, ss=ss: e.activation(
                            out=junk[:, :], in_=xt[:, :], func=AF.Square, accum_out=ss[:, :]),
                            reads=(xt,), writes=(junk, ss))
                        p.op("dve", lambda e, ss=ss, rs=rs: e.tensor_scalar(
                            out=rs[:, :], in0=ss[:, :], scalar1=1.0 / D, scalar2=EPS, op0=ALU.mult, op1=ALU.add),
                            reads=(ss,), writes=(rs,))
                        p.op("act", lambda e, rs=rs: e.activation(out=rs[:, :], in_=rs[:, :], func=AF.Sqrt),
                             reads=(rs,), writes=(rs,))
                        p.op("dve", lambda e, rs=rs: e.reciprocal(out=rs[:, :], in_=rs[:, :]),
                            reads=(rs,), writes=(rs,))
                        p.op("dve", lambda e, xt=xt, xn=xn, rs=rs: e.tensor_scalar(
                            out=xn[:, :], in0=xt[:, :], scalar1=rs[:, 0:1], scalar2=None, op0=ALU.mult),
                            reads=(xt, rs), writes=(xn,))

                        def tr(e, xn=xn, pt=pt):
                            ins = None
                            for j in range(KC):
                                ins = e.transpose(out=pt[:, j * 128:(j + 1) * 128], in_=xn[:, j * 128:(j + 1) * 128],
                                                  identity=ident_bf[:, :])
                            return ins
                        p.op("pe", tr, reads=(xn, ident_bf), writes=(pt,))

                        def ev(e, pt=pt, hb=hb, ti=ti, r=r):
                            ins = None
                            for j in range(KC):
                                ins = e.activation(out=hb[:, ti, j, :], in_=pt[:, j * 128:(j + 1) * 128],
                                                   func=AF.Identity, scale=Avec[l][:, j, r:r + 1],
                                                   bias=mod[l][:, j, r:r + 1])
                            return ins
                        p.op("act", ev, reads=(pt, Avec[l], mod[l]), writes=(hb,))
                    dma_st("act", hb, dst[tl[0]:tl[0] + len(tl)].rearrange("t p k c -> p t k c"), hb[:, 0:len(tl), :, :])
                p.end_stage()

        order = ["M", "N0", "R", "O0", "N1", "A", "O1"]
        nstage = len(order) if stop_after is None else order.index(stop_after) + 1
        if only is not None:
            nstage = 6 if "A" in only else 0
        if nstage >= 2 and only is None:
            stage_norm(0, x_loc, 18, ctx_loc)

        with contextlib.ExitStack() as st:
          if nstage >= 3 and only is None:
              Wr = Ring([p.sb(st, "Wp%d" % i, [128, KC, 256], BF16, dma=True) for i in range(3)])
              hb_r = Ring([p.sb(st, "hs%d" % i, [128, 2, KC, 128], BF16, dma=True) for i in range(3)])
              u = [p.sb(st, "u%d" % j, [128, NU + 4], F32) for j in range(2)]
              ucx = [p.sb(st, "ucx%d" % j, [128, CTX + 4], F32) for j in range(2)]
              uc = [p.sb(st, "uc%d" % j, [128, NH], F32) for j in range(2)]
              ucc = [p.sb(st, "ucc%d" % j, [128, CTX], F32) for j in range(2)]
              ucb = [p.sb(st, "ucb%d" % j, [128, NH], BF16) for j in range(2)]
              uccb = [p.sb(st, "uccb%d" % j, [128, CTX], BF16) for j in range(2)]
              sg = [[p.sb(st, "sg%d_%d" % (q, j), [128, NH], BF16) for j in range(2)] for q in range(2)]
              sgc = [[p.sb(st, "sgc%d_%d" % (q, j), [128, CTX], BF16) for j in range(2)] for q in range(2)]
              hA = p.sb(st, "hA", [128, NH], F32)
              hAc = p.sb(st, "hAc", [128, CTX], F32)
              hBc = p.sb(st, "hBc", [128, CTX], F32)
              zcb_r = Ring([p.sb(st, "zcb%d" % i, [128, CTX], BF16, dma=True) for i in range(1)])
              t1_r = Ring([p.sb(st, "t1_%d" % i, [128, 512], F32) for i in range(2)])
              t2_r = Ring([p.sb(st, "t2_%d" % i, [128, 512], F32) for i in range(2)])
              t3_r = Ring([p.sb(st, "t3_%d" % i, [128, 512], F32, dma=True) for i in range(2)])
              t4_r = Ring([p.sb(st, "t4_%d" % i, [128, 512], F32, dma=True) for i in range(2)])
              zeros = p.sb(st, "zeros", [128, 512], F32, const=True)
              GW_r = Ring([p.sb(st, "GW%d" % i, [128, 4, 2, 256], BF16, dma=True) for i in range(2)])
              st3_r = Ring([p.sb(st, "st3_%d" % i, [128, 1], F32) for i in range(4)])
              st4_r = Ring([p.sb(st, "st4_%d" % i, [128, 1], F32) for i in range(4)])
              conv5 = p.sb(st, "conv5", [128, KC, 5], F32, dma=True)
              convb = p.sb(st, "convb", [128, KC], F32, dma=True)
              gb_t = p.sb(st, "gb_t", [128, 4, KC], F32, dma=True)
              lam_t = p.sb(st, "lam_t", [128, 2, KC], F32, dma=True)
              cneg = p.sb(st, "cneg", [128, 2, KC], F32)
              SAbuf = p.sb(st, "SAbuf", [128, KC], F32, dma=True)
              gsb = p.sb(st, "gsb", [128, 2, KC], F32, dma=True)
              sel = p.sb(st, "sel", [128, 2], F32, dma=True)
              pp_r = Ring([p.ps(st, "pp%d" % i, [128, 512], F32) for i in range(3)])
              pr_r = Ring([p.ps(st, "pr%d" % i, [128, 512], F32) for i in range(2)])
              pi_r = Ring([p.ps(st, "pi%d" % i, [128, 512], F32) for i in range(2)])

              dma_ld("sp", conv5, conv5[:, :, :], conv5_fm)
              dma_ld("sp", convb, convb[:, :], convb_fm)
              dma_ld("sp", gb_t, gb_t[:, :, :], gate_b_fm)
              dma_ld("sp", lam_t, lam_t[:, :, :], lam_fm)
              dma_ld("sp", sel, sel[:, :], sel_in)
              p.op("dve", lambda e: e.memset(zeros[:, :], 0.0), writes=(zeros,))
              for j in range(2):
                  p.op("dve", lambda e, j=j: e.memset(u[j][:, :], 0.0), writes=(u[j],))
                  p.op("dve", lambda e, j=j: e.memset(ucx[j][:, :], 0.0), writes=(ucx[j],))
              p.op("act", lambda e: e.activation(out=cneg[:, :, :], in_=lam_t[:, :, :], func=AF.Exp, scale=-1.0),
                   reads=(lam_t,), writes=(cneg,))
              p.op("act", lambda e: e.activation(out=cneg[:, :, :], in_=cneg[:, :, :], func=AF.Ln, bias=1.0),
                   reads=(cneg,), writes=(cneg,))
              p.op("dve", lambda e: e.tensor_scalar(out=cneg[:, :, :], in0=cneg[:, :, :], scalar1=-8.0, scalar2=None,
                                                     op0=ALU.mult), reads=(cneg,), writes=(cneg,))

              tblocks = [(i * 256, 256) for i in range(8)] + [(2048, 130)]
              chunks = [(0, 512), (512, 512), (1024, 512), (1536, 512), (2048, 128)]

              def proj_gen(gbi):
                  q = gbi % 2
                  Ws = []
                  for c0 in (gbi * 256, D + gbi * 256):
                      W = Wr.next()
                      dma_ld("pool", W, W[:, :, :], rg_w_in[:, c0:c0 + 256].rearrange("(k p) n -> p k n", p=128))
                      Ws.append(W)
                  GW = GW_r.next()
                  GWs[gbi] = GW
                  for gi in range(4):
                      p.op("pool", lambda e, GW=GW, gi=gi, gbi=gbi: e.dma_start(
                          out=GW[:, gi, :, :], in_=gate_w[gi, gbi].rearrange("(kh p) j -> p kh j", p=128)),
                          writes=(GW,), dma=GW)
                  seqs = [("c", 0, CTX)] + [("l", t0, n) for t0, n in tblocks]
                  for kind, t0, n in seqs:
                      hb = hb_r.next()
                      srcv = hTc[0:2] if kind == "c" else hT[t0 // 128:t0 // 128 + 2]
                      dma_ld("sp", hb, hb[:, :, :, :], srcv.rearrange("t p k c -> p t k c"))
                      for ci in range(4):
                          pp = pp_r.next()

                          def mm(e, W=Ws[ci // 2], jj=ci % 2, hb=hb, pp=pp, n=n):
                              ins = None
                              if n == 130:
                                  for k in range(KC):
                                      ins = e.matmul(pp[:, 0:128], lhsT=W[:, k, jj * 128:(jj + 1) * 128],
                                                     rhs=hb[:, 0, k, :], start=(k == 0), stop=(k == KC - 1))
                                  for k in range(KC):
                                      ins = e.matmul(pp[:, 128:130], lhsT=W[:, k, jj * 128:(jj + 1) * 128],
                                                     rhs=hb[:, 1, k, 0:2], start=(k == 0), stop=(k == KC - 1))
                              else:
                                  for k in range(KC):
                                      ins = e.matmul(pp[:, 0:256].rearrange("p (t c) -> p t c", c=128),
                                                     lhsT=W[:, k, jj * 128:(jj + 1) * 128], rhs=hb[:, 0:2, k, :],
                                                     start=(k == 0), stop=(k == KC - 1))
                              return ins
                          p.op("pe", mm, reads=(Ws[ci // 2], hb), writes=(pp,))
                          j = ci % 2
                          if ci < 2:
                              if kind == "c":
                                  dstt, dsta = ucx[j], ucx[j][:, 2:2 + CTX]
                              else:
                                  dstt, dsta = u[j], u[j][:, 2 + t0:2 + t0 + n]
                              p.op("act", lambda e, pp=pp, dsta=dsta, n=n: e.activation(
                                  out=dsta, in_=pp[:, 0:n], func=AF.Copy), reads=(pp,), writes=(dstt,))
                          else:
                              n2 = min(n, 128) if (kind == "l" and t0 == 2048) else n
                              if kind == "c":
                                  dstt, dsta = sgc[q][j], sgc[q][j][:, 0:CTX]
                              else:
                                  dstt, dsta = sg[q][j], sg[q][j][:, t0:t0 + n2]
                              p.op("act", lambda e, pp=pp, dsta=dsta, n2=n2: e.activation(
                                  out=dsta, in_=pp[:, 0:n2], func=AF.Silu), reads=(pp,), writes=(dstt,))
                      yield
              def elem_gen(gbi):
                  q = gbi % 2
                  GW = GWs[gbi]
                  for j in range(2):
                      ch = gbi * 2 + j
                      for (ut, uct, ucbt, nn) in ((ucx[j], ucc[j], uccb[j], CTX), (u[j], uc[j], ucb[j], NH)):
                          p.op("dve", lambda e, ut=ut, uct=uct, nn=nn, ch=ch: e.tensor_scalar(
                              out=uct[:, 0:nn], in0=ut[:, 0:nn], scalar1=conv5[:, ch, 0:1], scalar2=convb[:, ch:ch + 1],
                              op0=ALU.mult, op1=ALU.add), reads=(ut, conv5, convb), writes=(uct,))
                          for k in range(1, 5):
                              p.op("dve", lambda e, ut=ut, uct=uct, nn=nn, ch=ch, k=k: e.scalar_tensor_tensor(
                                  out=uct[:, 0:nn], in0=ut[:, k:k + nn], scalar=conv5[:, ch, k:k + 1], in1=uct[:, 0:nn],
                                  op0=ALU.mult, op1=ALU.add), reads=(ut, conv5, uct), writes=(uct,))
                          p.op("act", lambda e, uct=uct, ucbt=ucbt, nn=nn: e.activation(
                              out=ucbt[:, 0:nn], in_=uct[:, 0:nn], func=AF.Copy), reads=(uct,), writes=(ucbt,))
                  yield
                  def half_gen(j):
                      ch = gbi * 2 + j

                      def gate_ab(d, ucb_pair, uct, c0, n):
                          pr = pr_r.next(); pi = pi_r.next()
                          t1 = t1_r.next(); t2 = t2_r.next()

                          def mm(e, pr=pr, pi=pi):
                              ins = None
                              for (pt_, gi) in ((pr, 2 * d), (pi, 2 * d + 1)):
                                  for kh in range(2):
                                      ins = e.matmul(pt_[:, 0:n], lhsT=GW[:, gi, kh, j * 128:(j + 1) * 128],
                                                     rhs=ucb_pair[kh][:, c0:c0 + n], start=(kh == 0), stop=(kh == 1))
                              return ins
                          p.op("pe", mm, reads=(GW, ucb_pair[0], ucb_pair[1]), writes=(pr, pi))
                          p.op("act", lambda e: e.activation(out=t1[:, 0:n], in_=pr[:, 0:n], func=AF.Sigmoid,
                                                             bias=gb_t[:, 2 * d, ch:ch + 1]),
                               reads=(pr, gb_t), writes=(t1,))
                          p.op("act", lambda e: e.activation(out=t2[:, 0:n], in_=pi[:, 0:n], func=AF.Sigmoid,
                                                             bias=gb_t[:, 2 * d + 1, ch:ch + 1]),
                               reads=(pi, gb_t), writes=(t2,))
                          p.op("act", lambda e: e.activation(out=t1[:, 0:n], in_=t1[:, 0:n], func=AF.Exp,
                                                             scale=cneg[:, d, ch:ch + 1]),
                               reads=(t1, cneg), writes=(t1,))
                          return t1, t2

                      def finish_b(t1, t2, t3, uct, c0, n):
                          p.op("dve", lambda e: e.tensor_tensor(out=t2[:, 0:n], in0=t2[:, 0:n], in1=uct[:, c0:c0 + n],
                                                                op=ALU.mult), reads=(t2, uct), writes=(t2,))
                          p.op("act", lambda e: e.activation(out=t3[:, 0:n], in_=t1[:, 0:n], func=AF.Square),
                               reads=(t1,), writes=(t3,))
                          p.op("act", lambda e: e.activation(out=t3[:, 0:n], in_=t3[:, 0:n], func=AF.Sqrt,
                                                             scale=-1.0, bias=1.0), reads=(t3,), writes=(t3,))
                          p.op("dve", lambda e: e.tensor_tensor(out=t2[:, 0:n], in0=t2[:, 0:n], in1=t3[:, 0:n],
                                                                op=ALU.mult), reads=(t2, t3), writes=(t2,))

                      t1, t2 = gate_ab(0, uccb, ucc[j], 0, CTX)
                      t3 = t3_r.next()
                      finish_b(t1, t2, t3, ucc[j], 0, CTX)
                      p.op("dve", lambda e, t1=t1, t2=t2: e.tensor_tensor_scan(
                          out=hAc[:, :], data0=t1[:, 0:CTX], data1=t2[:, 0:CTX], initial=0.0,
                          op0=ALU.mult, op1=ALU.add), reads=(t1, t2), writes=(hAc,))
                      yield
                      for ci_, (c0, n) in enumerate(chunks):
                          t1, t2 = gate_ab(0, ucb, uc[j], c0, n)
                          t3 = t3_r.next()
                          finish_b(t1, t2, t3, uc[j], c0, n)
                          init = hAc[:, CTX - 1:CTX] if ci_ == 0 else hA[:, c0 - 1:c0]
                          p.op("dve", lambda e, t1=t1, t2=t2, c0=c0, n=n, init=init: e.tensor_tensor_scan(
                              out=hA[:, c0:c0 + n], data0=t1[:, 0:n], data1=t2[:, 0:n], initial=init,
                              op0=ALU.mult, op1=ALU.add), reads=(t1, t2, hA, hAc), writes=(hA,))
                          yield
                      p.op("act", lambda e, ch=ch: e.activation(out=SAbuf[:, ch:ch + 1], in_=hA[:, 1919:1920],
                                                                func=AF.Copy), reads=(hA,), writes=(SAbuf,))
                      t1, t2 = gate_ab(1, uccb, ucc[j], 0, CTX)
                      t3 = t3_r.next()
                      finish_b(t1, t2, t3, ucc[j], 0, CTX)
                      p.op("dve", lambda e, t1=t1, t2=t2: e.tensor_tensor_scan(
                          out=hBc[:, ::-1], data0=t1[:, 0:CTX][:, ::-1], data1=t2[:, 0:CTX][:, ::-1], initial=0.0,
                          op0=ALU.mult, op1=ALU.add), reads=(t1, t2), writes=(hBc,))
                      zcb = zcb_r.next()
                      p.op("dve", lambda e: e.tensor_tensor(out=hBc[:, :], in0=hBc[:, :], in1=hAc[:, :], op=ALU.add),
                           reads=(hBc, hAc), writes=(hBc,))
                      p.op("dve", lambda e, zcb=zcb: e.tensor_tensor(out=zcb[:, :], in0=hBc[:, :], in1=sgc[q][j][:, :],
                                                                    op=ALU.mult), reads=(hBc, sgc[q][j]), writes=(zcb,))
                      dma_st("pool", zcb, ZCTX[ch * 128:(ch + 1) * 128, :], zcb[:, :])
                      yield
                      prev3 = None; prev4 = None; prevn = None
                      for ci_ in range(len(chunks) - 1, -1, -1):
                          c0, n = chunks[ci_]
                          t1, t2 = gate_ab(1, ucb, uc[j], c0, n)
                          t3 = t3_r.next(); t4 = t4_r.next()
                          finish_b(t1, t2, t3, uc[j], c0, n)
                          if prev4 is None:
                              p.op("dve", lambda e, t1=t1, t4=t4, n=n: e.tensor_tensor_scan(
                                  out=t4[:, 0:n][:, ::-1], data0=t1[:, 0:n][:, ::-1], data1=zeros[:, 0:n], initial=1.0,
                                  op0=ALU.mult, op1=ALU.add), reads=(t1, zeros), writes=(t4,))
                              p.op("dve", lambda e, t1=t1, t2=t2, t3=t3, n=n: e.tensor_tensor_scan(
                                  out=t3[:, 0:n][:, ::-1], data0=t1[:, 0:n][:, ::-1], data1=t2[:, 0:n][:, ::-1],
                                  initial=0.0, op0=ALU.mult, op1=ALU.add), reads=(t1, t2), writes=(t3,))
                          else:
                              p.op("dve", lambda e, t1=t1, t4=t4, n=n, st4=st4: e.tensor_tensor_scan(
                                  out=t4[:, 0:n][:, ::-1], data0=t1[:, 0:n][:, ::-1], data1=zeros[:, 0:n],
                                  initial=st4[:, 0:1], op0=ALU.mult, op1=ALU.add), reads=(t1, zeros, st4), writes=(t4,))
                              p.op("dve", lambda e, t1=t1, t2=t2, t3=t3, n=n, st3=st3: e.tensor_tensor_scan(
                                  out=t3[:, 0:n][:, ::-1], data0=t1[:, 0:n][:, ::-1], data1=t2[:, 0:n][:, ::-1],
                                  initial=st3[:, 0:1], op0=ALU.mult, op1=ALU.add), reads=(t1, t2, st3), writes=(t3,))
                          st3 = st3_r.next()
                          st4 = st4_r.next()
                          p.op("act", lambda e, t3=t3, st3=st3: e.activation(out=st3[:, :], in_=t3[:, 0:1], func=AF.Copy),
                               reads=(t3,), writes=(st3,))
                          p.op("act", lambda e, t4=t4, st4=st4: e.activation(out=st4[:, :], in_=t4[:, 0:1], func=AF.Copy),
                               reads=(t4,), writes=(st4,))
                          prev4 = t4
                          p.op("dve", lambda e, t3=t3, c0=c0, n=n: e.tensor_tensor(
                              out=t3[:, 0:n], in0=t3[:, 0:n], in1=hA[:, c0:c0 + n], op=ALU.add),
                              reads=(t3, hA), writes=(t3,))
                          p.op("dve", lambda e, t3=t3, c0=c0, n=n: e.tensor_tensor(
                              out=t3[:, 0:n], in0=t3[:, 0:n], in1=sg[q][j][:, c0:c0 + n], op=ALU.mult),
                              reads=(t3, sg[q][j]), writes=(t3,))
                          p.op("dve", lambda e, t4=t4, c0=c0, n=n: e.tensor_tensor(
                              out=t4[:, 0:n], in0=t4[:, 0:n], in1=sg[q][j][:, c0:c0 + n], op=ALU.mult),
                              reads=(t4, sg[q][j]), writes=(t4,))
                          dma_st("pool", t3, Z0[ch * 128:(ch + 1) * 128, c0:c0 + n], t3[:, 0:n])
                          dma_st("pool", t4, ZC[ch * 128:(ch + 1) * 128, c0:c0 + n], t4[:, 0:n])
                          yield
                  for j_ in range(2):
                      yield from half_gen(j_)
              GWs = {}
              for _ in proj_gen(0):
                  pass
              for gbi_ in range(16):
                  eg = elem_gen(gbi_)
                  pg = proj_gen(gbi_ + 1) if gbi_ < 15 else None
                  next(eg)
                  ne = 0
                  for _ in eg:
                      ne += 1
                      if pg is not None and ne % 2 == 0:
                          if next(pg, "end") == "end":
                              pg = None
                  if pg is not None:
                      for _ in pg:
                          pass
              p.op("pool", lambda e: e.dma_start(out=gin[:, :], in_=SAbuf[:, :]), reads=(SAbuf,), dma=SAbuf)
              p.barrier()
              p.cnt[ccsem] += 1

              def cc(e):
                  return e.collective_compute("AllGather", ALU.bypass,
                                              replica_groups=[[0, 1], [2, 3], [4, 5], [6, 7]],
                                              ins=[gin.ap().opt()], outs=[gout.ap().opt()])
              p.prog["pool"].append(([], cc, (ccsem, 1)))
              p.prog["pool"].append(([(ccsem, p.cnt[ccsem])], None, None))
              p.seen["pool"][ccsem] = p.cnt[ccsem]
              dma_ld("pool", gsb, gsb[:, :, :], gout.ap().rearrange("(r p) k -> p r k", p=128))
              p.op("dve", lambda e: e.tensor_scalar(out=Sin[:, :], in0=gsb[:, 0, :], scalar1=sel[:, 0:1], scalar2=None,
                                                     op0=ALU.mult), reads=(gsb, sel), writes=(Sin,))
              p.op("dve", lambda e: e.scalar_tensor_tensor(out=Sin[:, :], in0=gsb[:, 1, :], scalar=sel[:, 1:2],
                                                            in1=Sin[:, :], op0=ALU.mult, op1=ALU.add),
                   reads=(gsb, sel, Sin), writes=(Sin,))
              p.end_stage()

        def stage_out(l, w_out_ap, blocks):
            with contextlib.ExitStack() as st:
                zT_r = Ring([p.sb(st, "zT%d" % i, [128, KC, 512], BF16, dma=True) for i in range(2)])
                Wo_r = Ring([p.sb(st, "Wo%d" % i, [128, KC, 512], BF16, dma=True) for i in range(2)])
                z0_r = Ring([p.sb(st, "z0_%d" % i, [128, 512], F32, dma=True) for i in range(3)])
                zc_r = Ring([p.sb(st, "zc_%d" % i, [128, 512], F32, dma=True) for i in range(3)])
                gbc = [p.sb(st, "gbc%d" % r, [128, D], F32, dma=True) for r in range(2)]
                xc_r = Ring([p.sb(st, "xc%d" % i, [128, 512], F32, dma=True) for i in range(3)])
                yo_r = Ring([p.sb(st, "yo%d" % i, [128, 512], F32, dma=True) for i in range(3)])
                po_r = Ring([p.ps(st, "po%d" % i, [128, 512], F32) for i in range(4)])
                for r in range(2):
                    dma_ld("sp", gbc[r], gbc[r][:, :], grow[l, r].partition_broadcast(128))
                for blk in blocks:
                    n = blk["n"]; t0 = blk["t0"]
                    zT = zT_r.next()
                    if blk["zmode"] == "corr":
                        for c in range(KC):
                            z0 = z0_r.next(); zc = zc_r.next()
                            dma_ld("sp", z0, z0[:, 0:n], Z0[c * 128:(c + 1) * 128, t0:t0 + n])
                            dma_ld("sp", zc, zc[:, 0:n], ZC[c * 128:(c + 1) * 128, t0:t0 + n])
                            p.op("dve", lambda e, z0=z0, zc=zc, zT=zT, c=c, n=n: e.scalar_tensor_tensor(
                                out=zT[:, c, 0:n], in0=zc[:, 0:n], scalar=Sin[:, c:c + 1], in1=z0[:, 0:n],
                                op0=ALU.mult, op1=ALU.add), reads=(z0, zc, Sin), writes=(zT,))
                    else:
                        zsrc = blk["zsrc"]
                        dma_ld("sp", zT, zT[:, :, 0:n], zsrc.rearrange("(k p) t -> p k t", p=128)[:, :, t0:t0 + n])
                    for nb in range(8):
                        Wo = Wo_r.next()
                        dma_ld("pool", Wo, Wo[:, :, :],
                               w_out_ap[:, nb * 512:(nb + 1) * 512].rearrange("(k p) n -> p k n", p=128))
                        for tt in range(n // 128):
                            po = po_r.next(); xc = xc_r.next(); yo = yo_r.next()
                            r0 = t0 + tt * 128

                            def mm(e, zT=zT, Wo=Wo, po=po, tt=tt):
                                ins = None
                                for k in range(KC):
                                    ins = e.matmul(po[:, :], lhsT=zT[:, k, tt * 128:(tt + 1) * 128], rhs=Wo[:, k, :],
                                                   start=(k == 0), stop=(k == KC - 1))
                                return ins
                            p.op("pe", mm, reads=(zT, Wo), writes=(po,))
                            dma_ld("sp", xc, xc[:, :], blk["xsrc"][r0:r0 + 128, nb * 512:(nb + 1) * 512])
                            g = gbc[blk["grow"]]
                            p.op("dve", lambda e, po=po, yo=yo, g=g, nb=nb: e.tensor_tensor(
                                out=yo[:, :], in0=po[:, :], in1=g[:, nb * 512:(nb + 1) * 512], op=ALU.mult),
                                reads=(po, g), writes=(yo,))
                            p.op("dve", lambda e, yo=yo, xc=xc: e.tensor_tensor(
                                out=yo[:, :], in0=yo[:, :], in1=xc[:, :], op=ALU.add), reads=(yo, xc), writes=(yo,))
                            dma_st("act", yo, blk["dst"][r0:r0 + 128, nb * 512:(nb + 1) * 512], yo[:, :])
                p.end_stage()

        blocks0 = [dict(zmode="ctx", zsrc=ZCTX, t0=0, n=CTX, xsrc=ctx_loc, dst=CTX1, grow=1)]
        for t0, n in [(0, 512), (512, 512), (1024, 512), (1536, 512), (2048, 128)]:
            blocks0.append(dict(zmode="corr", t0=t0, n=n, xsrc=x_loc, dst=X1, grow=0))
        if nstage >= 4 and only is None:
            stage_out(0, rg_w_out, blocks0)
        if (nstage >= 5 and only is None) or (only is not None and 'N1' in only):
            stage_norm(1, X1, 17, CTX1)

        with contextlib.ExitStack() as st:
          if nstage >= 6:
              Wr = Ring([p.sb(st, "Wq%d" % i, [128, KC, 128], BF16, dma=True) for i in range(6)])
              hb_r = Ring([p.sb(st, "ha%d" % i, [128, 4, KC, 128], BF16, dma=True) for i in range(2)])
              QT = p.sb(st, "QT", [128, 4, NOWN], BF16)
              KT = p.sb(st, "KT", [128, NH + CTX], BF16)
              Vt = p.sb(st, "Vt", [128, 19, 128], BF16)
              OT = p.sb(st, "OT", [128, 4, NOWN], BF16)
              cosT = p.sb(st, "cosT", [128, NH], F32, dma=True)
              sinT = p.sb(st, "sinT", [128, NH], F32, dma=True)
              rotm = p.sb(st, "rotm", [128, 128], F32, dma=True)
              ones_f = p.sb(st, "ones_f", [128, 128], F32)
              ones_b = p.sb(st, "ones_b", [128, 128], BF16)
              qkn = p.sb(st, "qkn", [128, 2], F32, dma=True)
              esink = p.sb(st, "esink", [128, 32], F32, dma=True)
              masks = p.sb(st, "masks", [128, 2, 512], BF16, dma=True)
              sq_r = Ring([p.sb(st, "sq%d" % i, [128, 512], F32) for i in range(1)])
              qr_r = Ring([p.sb(st, "qr%d" % i, [128, 512], F32) for i in range(2)])
              rs_r = Ring([p.sb(st, "rsa%d" % i, [128, 512], F32) for i in range(2)])
              tq_r = Ring([p.sb(st, "tq%d" % i, [128, 512], F32) for i in range(1)])
              vb_r = Ring([p.sb(st, "vb%d" % i, [128, 512], BF16) for i in range(2)])
              PT_r = Ring([p.sb(st, "PT%d" % i, [128, 512], BF16) for i in range(3)])
              den_r = Ring([p.sb(st, "den%d" % i, [128, 512], F32) for i in range(1)])
              zt_r = Ring([p.sb(st, "zt%d" % i, [128, 512], BF16, dma=True) for i in range(3)])
              gs_r = Ring([p.sb(st, "gs%d" % i, [128, 512], F32) for i in range(2)])
              pp_r = Ring([p.ps(st, "pa%d" % i, [128, 512], F32) for i in range(2)])
              px_r = Ring([p.ps(st, "px%d" % i, [128, 512], F32) for i in range(2)])
              pS_r = Ring([p.ps(st, "pS%d" % i, [128, 512], F32) for i in range(2)])
              pO = p.ps(st, "pO", [128, 512], F32)
              pR = p.ps(st, "pR", [128, 512], F32)

              dma_ld("sp", cosT, cosT[:, :], cos_fm[:, 0:NH])
              dma_ld("sp", sinT, sinT[:, :], sin_fm[:, 0:NH])
              dma_ld("sp", rotm, rotm[:, :], rot_m)
              dma_ld("sp", qkn, qkn[:, :], qk_norm_fm)
              dma_ld("sp", esink, esink[:, :], sink_bc)
              p.op("act", lambda e: e.activation(out=esink[:, :], in_=esink[:, :], func=AF.Exp),
                   reads=(esink,), writes=(esink,))
              p.op("pool", lambda e: e.dma_start(out=masks[:, :, :], in_=mask_in.rearrange("m p n -> p m n")),
                   writes=(masks,), dma=masks)
              p.op("dve", lambda e: e.memset(ones_f[:, :], 1.0), writes=(ones_f,))
              p.op("dve", lambda e: e.memset(ones_b[:, :], 1.0), writes=(ones_b,))

              SCALE = 128.0 ** -0.5

              def project(cols, seqs, evac):
                  Ws = []
                  for c0 in cols:
                      W = Wr.next()
                      dma_ld("pool", W, W[:, :, :], at_w_in[:, c0:c0 + 128].rearrange("(k p) n -> p k n", p=128))
                      Ws.append(W)
                  for kind, t0, n in seqs:
                      hb = hb_r.next()
                      nt_ = n // 128
                      srcv = hTc[0:2] if kind == "c" else hT[t0 // 128:t0 // 128 + nt_]
                      dma_ld("sp", hb, hb[:, 0:nt_, :, :], srcv.rearrange("t p k c -> p t k c"))
                      for ci in range(len(cols)):
                          if not evac(ci, kind, t0, n, None):
                              continue
                          pp = pp_r.next()

                          def mm(e, W=Ws[ci], hb=hb, pp=pp, n=n, nt_=nt_):
                              ins = None
                              for k in range(KC):
                                  if nt_ == 1:
                                      ins = e.matmul(pp[:, 0:128], lhsT=W[:, k, :], rhs=hb[:, 0, k, :],
                                                     start=(k == 0), stop=(k == KC - 1))
                                  else:
                                      ins = e.matmul(pp[:, 0:n].rearrange("p (t c) -> p t c", c=128), lhsT=W[:, k, :],
                                                     rhs=hb[:, 0:nt_, k, :], start=(k == 0), stop=(k == KC - 1))
                              return ins
                          p.op("pe", mm, reads=(Ws[ci], hb), writes=(pp,))
                          evac(ci, kind, t0, n, pp)

              def norm_rope(pp, n, wcol, rope_t0, dst_tk, dst_ap):
                  sq = sq_r.next(); qr = qr_r.next(); rs = rs_r.next(); tq = tq_r.next()
                  px = px_r.next()
                  p.op("act", lambda e: e.activation(out=sq[:, 0:n], in_=pp[:, 0:n], func=AF.Square),
                       reads=(pp,), writes=(sq,))
                  p.op("act", lambda e: e.activation(out=qr[:, 0:n], in_=pp[:, 0:n], func=AF.Copy),
                       reads=(pp,), writes=(qr,))
                  p.op("pe", lambda e: e.matmul(px[:, 0:n], lhsT=ones_f[:, :], rhs=sq[:, 0:n], start=True, stop=True),
                       reads=(ones_f, sq), writes=(px,))
                  p.op("dve", lambda e: e.tensor_scalar(out=rs[:, 0:n], in0=px[:, 0:n], scalar1=1.0 / 128, scalar2=EPS,
                                                         op0=ALU.mult, op1=ALU.add), reads=(px,), writes=(rs,))
                  p.op("act", lambda e: e.activation(out=rs[:, 0:n], in_=rs[:, 0:n], func=AF.Sqrt),
                       reads=(rs,), writes=(rs,))
                  p.op("dve", lambda e: e.reciprocal(out=rs[:, 0:n], in_=rs[:, 0:n]), reads=(rs,), writes=(rs,))
                  if rope_t0 is None:
                      p.op("dve", lambda e: e.scalar_tensor_tensor(
                          out=dst_ap, in0=qr[:, 0:n], scalar=qkn[:, wcol:wcol + 1], in1=rs[:, 0:n],
                          op0=ALU.mult, op1=ALU.mult), reads=(qr, qkn, rs), writes=(dst_tk,))
                      return
                  p.op("dve", lambda e: e.scalar_tensor_tensor(
                      out=qr[:, 0:n], in0=qr[:, 0:n], scalar=qkn[:, wcol:wcol + 1], in1=rs[:, 0:n],
                      op0=ALU.mult, op1=ALU.mult), reads=(qr, qkn, rs), writes=(qr,))
                  px2 = px_r.next()
                  p.op("pe", lambda e: e.matmul(px2[:, 0:n], lhsT=rotm[:, :], rhs=qr[:, 0:n], start=True, stop=True),
                       reads=(rotm, qr), writes=(px2,))
                  p.op("dve", lambda e: e.tensor_tensor(out=tq[:, 0:n], in0=px2[:, 0:n],
                                                         in1=sinT[:, rope_t0:rope_t0 + n], op=ALU.mult),
                       reads=(px2, sinT), writes=(tq,))
                  p.op("dve", lambda e: e.tensor_tensor(out=qr[:, 0:n], in0=qr[:, 0:n],
                                                         in1=cosT[:, rope_t0:rope_t0 + n], op=ALU.mult),
                       reads=(qr, cosT), writes=(qr,))
                  p.op("dve", lambda e: e.tensor_tensor(out=dst_ap, in0=qr[:, 0:n], in1=tq[:, 0:n], op=ALU.add),
                       reads=(qr, tq), writes=(dst_tk,))

              own_blocks = [("l", 0, 512), ("l", 512, 512), ("l", 1024, 512), ("l", 1536, 512)]
              seqsA = [("c", 0, CTX)] + own_blocks + [("l", 2048, 128)]

              for h in range(8 if asub is None else asub.get('nh', 8)):
                  colsA = [D + h * 128, D + 1024 + h * 128] + [h * 512 + g * 128 for g in range(4)]

                  def evacA(ci, kind, t0, n, pp, h=h):
                      if ci >= 2 and (kind == "c" or t0 >= NOWN):
                          return False
                      if pp is None:
                          return True
                      if asub is not None and asub.get('simple', 0):
                          vb = vb_r.next()
                          p.op("act", lambda e: e.activation(out=vb[:, 0:n], in_=pp[:, 0:n], func=AF.Copy),
                               reads=(pp,), writes=(vb,))
                          return True
                      if ci == 0:
                          if kind == "c":
                              norm_rope(pp, n, 1, None, KT, KT[:, NH:NH + CTX])
                          else:
                              norm_rope(pp, n, 1, t0, KT, KT[:, t0:t0 + n])
                      elif ci == 1:
                          vb = vb_r.next()
                          p.op("act", lambda e: e.activation(out=vb[:, 0:n], in_=pp[:, 0:n], func=AF.Copy),
                               reads=(pp,), writes=(vb,))
                          px = px_r.next()
                          pxb = px[:, :].bitcast(BF16)

                          def tr(e):
                              ins = None
                              for i in range(n // 128):
                                  ins = e.transpose(out=pxb[:, i * 128:(i + 1) * 128], in_=vb[:, i * 128:(i + 1) * 128],
                                                    identity=ident_bf[:, :])
                              return ins
                          p.op("pe", tr, reads=(vb, ident_bf), writes=(px,))
                          kb0 = 17 if kind == "c" else t0 // 128
                          nb_ = n // 128
                          p.op("act", lambda e: e.activation(
                              out=Vt[:, kb0:kb0 + nb_, :],
                              in_=pxb[:, 0:nb_ * 128].rearrange("p (b d) -> p b d", d=128), func=AF.Copy),
                              reads=(px,), writes=(Vt,))
                      else:
                          g = ci - 2
                          norm_rope(pp, n, 0, t0, QT, QT[:, g, t0:t0 + n])
                      return True
                  project(colsA, seqsA, evacA)

                  for i in range(16 if asub is None else asub.get('nq', 16)):
                      kbs = [("c", 17, NH), ("c", 18, NH + 128)]
                      if i > 0:
                          kbs.append(("p", i - 1, (i - 1) * 128))
                      kbs.append(("o", i, i * 128))
                      kbs.append(("n", i + 1, (i + 1) * 128))
                      for ki, (kk, vb_i, kc0) in enumerate(kbs):
                          pS = pS_r.next(); PT = PT_r.next()
                          p.op("pe", lambda e, pS=pS, kc0=kc0, i=i: e.matmul(
                              pS[:, :].rearrange("p (g q) -> p g q", g=4), lhsT=KT[:, kc0:kc0 + 128],
                              rhs=QT[:, :, i * 128:(i + 1) * 128], start=True, stop=True),
                              reads=(KT, QT), writes=(pS,))
                          p.op("act", lambda e, pS=pS, PT=PT: e.activation(out=PT[:, :], in_=pS[:, :], func=AF.Exp,
                                                                           scale=SCALE), reads=(pS,), writes=(PT,))
                          if kk in ("p", "n"):
                              mi = 0 if kk == "p" else 1
                              p.op("dve", lambda e, PT=PT, mi=mi: e.tensor_tensor(
                                  out=PT[:, :], in0=PT[:, :], in1=masks[:, mi, :], op=ALU.mult),
                                  reads=(PT, masks), writes=(PT,))

                          def pv(e, PT=PT, vb_i=vb_i, ki=ki, last=(ki == len(kbs) - 1)):
                              e.matmul(pO[:, :], lhsT=Vt[:, vb_i, :], rhs=PT[:, :], start=(ki == 0), stop=last)
                              return e.matmul(pR[:, :], lhsT=ones_b[:, :], rhs=PT[:, :], start=(ki == 0), stop=last)
                          p.op("pe", pv, reads=(Vt, PT, ones_b), writes=(pO, pR))
                      den = den_r.next()
                      for g in range(4):
                          p.op("dve", lambda e, den=den, g=g, h=h: e.tensor_scalar(
                              out=den[:, g * 128:(g + 1) * 128], in0=pR[:, g * 128:(g + 1) * 128],
                              scalar1=esink[:, h * 4 + g:h * 4 + g + 1], scalar2=None, op0=ALU.add),
                              reads=(pR, esink), writes=(den,))
                      p.op("dve", lambda e, den=den: e.reciprocal(out=den[:, :], in_=den[:, :]),
                           reads=(den,), writes=(den,))
                      p.op("dve", lambda e, den=den, i=i: e.tensor_tensor(
                          out=OT[:, :, i * 128:(i + 1) * 128], in0=pO[:, :].rearrange("p (g q) -> p g q", g=4),
                          in1=den[:, :].rearrange("p (g q) -> p g q", g=4), op=ALU.mult),
                          reads=(pO, den), writes=(OT,))

                  colsB = [6144 + h * 512 + g * 128 for g in range(4)]

                  def evacB(ci, kind, t0, n, pp, h=h):
                      if pp is None:
                          return True
                      gs = gs_r.next(); zt = zt_r.next()
                      p.op("act", lambda e: e.activation(out=gs[:, 0:n], in_=pp[:, 0:n], func=AF.Silu),
                           reads=(pp,), writes=(gs,))
                      p.op("dve", lambda e: e.tensor_tensor(out=zt[:, 0:n], in0=gs[:, 0:n], in1=OT[:, ci, t0:t0 + n],
                                                             op=ALU.mult), reads=(gs, OT), writes=(zt,))
                      r0 = (h * 4 + ci) * 128
                      dma_st("act", zt, Z1T[r0:r0 + 128, t0:t0 + n], zt[:, 0:n])
                      return True
                  if asub is None or asub.get('pb', 1):
                      project(colsB, own_blocks, evacB)
              p.end_stage()

        blocks1 = [dict(zmode="direct", zsrc=Z1T, t0=t0, n=512, xsrc=X1, dst=out, grow=0)
                   for t0 in (0, 512, 1024, 1536)]
        if nstage >= 7:
            stage_out(1, at_w_out, blocks1)

        p.check()
        p.emit()
    return nc


def _fm(v):
    v = np.asarray(v, np.float32)
    lead = v.shape[:-1]
    a = v.reshape(lead + (KC, 128))
    a = np.moveaxis(a, -1, 0)
    return np.ascontiguousarray(a)


def prepare_inputs(x, c, ctx, c_ctx, w_mod, b_mod, norm_w, rg_w_in, rg_conv_w, rg_conv_b, rg_w_r, rg_b_r,
                   rg_w_i, rg_b_i, rg_lam, rg_w_out, at_w_in, at_q_norm, at_k_norm, at_sink, at_w_out):
    f32 = np.float32
    shared = {}
    shared["w_mod"] = np.ascontiguousarray(w_mod, f32)
    shared["bmod_fm"] = np.ascontiguousarray(
        np.asarray(b_mod, f32).reshape(2, 96, 128).transpose(2, 0, 1))
    shared["normw_fm"] = _fm(norm_w)
    shared["rg_w_in"] = np.ascontiguousarray(rg_w_in[0], f32)
    shared["convb_fm"] = _fm(rg_conv_b[0])
    shared["rg_w_out"] = np.ascontiguousarray(rg_w_out[0], f32)
    shared["at_w_in"] = np.ascontiguousarray(at_w_in[0], f32)
    shared["qk_norm_fm"] = np.ascontiguousarray(np.stack([at_q_norm[0], at_k_norm[0]], axis=1), f32)
    shared["sink_bc"] = np.ascontiguousarray(np.broadcast_to(np.asarray(at_sink[0], f32)[None, :], (128, 32)))
    shared["at_w_out"] = np.ascontiguousarray(at_w_out[0], f32)
    ident = np.eye(128, dtype=f32)
    shared["ident_in"] = ident
    rot = np.zeros((128, 128), f32)
    for m in range(128):
        if (m % 64) < 32:
            rot[m + 32, m] = -1.0
        else:
            rot[m - 32, m] = 1.0
    shared["rot_m"] = rot
    kj = np.arange(128)[:, None]
    qi = np.arange(128)[None, :]
    mprev = (kj >= qi).astype(f32)
    mnext = (kj <= qi).astype(f32)
    shared["mask_in"] = np.ascontiguousarray(np.stack([np.tile(mprev, (1, 4)), np.tile(mnext, (1, 4))], 0))

    conv_w = np.asarray(rg_conv_w[0], f32)
    zero = np.zeros((1, D), f32)
    per_core = []
    for core in range(8):
        b, half = core // 2, core % 2
        m = dict(shared)
        if half == 0:
            idx = np.arange(NLOC)
            m["ctx_loc"] = np.ascontiguousarray(ctx[b], f32)
            conv5 = np.concatenate([conv_w, zero], 0)
        else:
            idx = 4095 - np.arange(NLOC)
            m["ctx_loc"] = np.ascontiguousarray(np.asarray(ctx[b], f32)[::-1])
            conv5 = np.concatenate([zero, conv_w[::-1]], 0)
        m["x_loc"] = np.ascontiguousarray(np.asarray(x[b], f32)[idx])
        m["conv5_fm"] = np.ascontiguousarray(np.moveaxis(_fm(conv5), 1, 2))
        m["c_fm"] = np.ascontiguousarray(np.moveaxis(_fm(np.stack([c[b], c_ctx], 0)), 1, 2))
        dA, dB = half, 1 - half
        m["gate_w"] = np.ascontiguousarray(np.stack([rg_w_r[0, dA], rg_w_i[0, dA], rg_w_r[0, dB], rg_w_i[0, dB]], 0), f32)
        m["gate_b_fm"] = _fm(np.stack([rg_b_r[0, dA], rg_b_i[0, dA], rg_b_r[0, dB], rg_b_i[0, dB]], 0))
        m["lam_fm"] = _fm(np.stack([rg_lam[0, dA], rg_lam[0, dB]], 0))
        t = idx.astype(np.float64)
        row = np.floor(t / 64.0)
        col = t - row * 64.0
        inv = 10000.0 ** (-np.arange(32, dtype=np.float64) * (2.0 / 64.0))
        dd = np.arange(128)
        pos = np.where(dd[:, None] < 64, row[None, :], col[None, :])
        ang = pos * inv[dd % 32][:, None]
        m["cos_fm"] = np.ascontiguousarray(np.cos(ang), f32)
        m["sin_fm"] = np.ascontiguousarray(np.sin(ang), f32)
        selv = np.zeros((128, 2), f32)
        selv[:, 1 - half] = 1.0
        m["sel_in"] = selv
        per_core.append(m)
    return per_core


_NC_CACHE = {}


def kernel(**inputs):
    inputs = {k: np.asarray(v) for k, v in inputs.items()}
    per_core = prepare_inputs(**inputs)
    if "nc" not in _NC_CACHE:
        _NC_CACHE["nc"] = build_program(DEBUG)
    nc = _NC_CACHE["nc"]
    res = run_bass_kernel_spmd(nc, per_core, core_ids=list(range(8)))
    outp = np.empty((4, 4096, D), np.float32)
    for core in range(8):
        b, half = core // 2, core % 2
        o = np.asarray(res.results[core]["out"], np.float32)
        if half == 0:
            outp[b, 0:NOWN] = o
        else:
            outp[b, NOWN:] = o[::-1]
    if DEBUG:
        kernel.last = res
    return outp
```

```python
import numpy as np
import ml_dtypes
import concourse.bass as bass
import concourse.mybir as mybir
from concourse.bass_utils import run_bass_kernel_spmd

F32 = mybir.dt.float32
BF16 = mybir.dt.bfloat16
ALU = mybir.AluOpType
AF = mybir.ActivationFunctionType

D = 4096
KC = 32
NOWN = 2048
NH = 2176
NU = 2178
NLOC = 2304
CTX = 256
EPS = 1e-6
ENG = ("pe", "act", "dve", "pool", "sp")
DEBUG = False
REUSE_DSEMS = True


class Tk:
    __slots__ = ("ap", "lw", "rd", "dsem", "const")

    def __init__(self, ap=None, dsem=None, const=False):
        self.ap = ap
        self.lw = None
        self.rd = {}
        self.dsem = dsem
        self.const = const

    def __getitem__(self, k):
        return self.ap[k]


class Prog:
    def __init__(self, nc, stack):
        self.nc = nc
        self.stack = stack
        self.prog = {e: [] for e in ENG}
        self.cnt = {}
        self.seen = {e: {} for e in ENG}
        self.esem = {}
        for e in ENG:
            s = stack.enter_context(nc.semaphore("es_" + e))
            self.esem[e] = s
            self.cnt[s] = 0
        self.free_dsems = []
        self.ndsem = 0
        self.stage_dsems = []

    def dsem(self):
        if self.free_dsems:
            s = self.free_dsems.pop()
        else:
            s = self.stack.enter_context(self.nc.semaphore("ds%d" % self.ndsem))
            self.ndsem += 1
            self.cnt[s] = 0
        self.stage_dsems.append(s)
        return s

    def sb(self, st, name, shape, dt, dma=False, const=False):
        self.uid = getattr(self, "uid", 0) + 1
        name = "%s_u%d" % (name, self.uid)
        t = st.enter_context(self.nc.sbuf_tensor(name, list(shape), dt))
        return Tk(t, self.dsem() if dma else None, const)

    def ps(self, st, name, shape, dt):
        self.uid = getattr(self, "uid", 0) + 1
        name = "%s_u%d" % (name, self.uid)
        t = st.enter_context(self.nc.psum_tensor(name, list(shape), dt))
        return Tk(t)

    def dram(self, ap=None):
        return Tk(ap)

    def op(self, eng, fn, reads=(), writes=(), dma=None):
        waits = {}
        seen = self.seen[eng]

        def need(tok):
            if tok is None:
                return
            sem, val = tok
            if seen.get(sem, 0) >= val:
                return
            if waits.get(sem, 0) < val:
                waits[sem] = val

        for t in reads:
            need(t.lw)
        for t in writes:
            need(t.lw)
            for sem, val in t.rd.items():
                need((sem, val))
        if eng == "pool" and dma is not None and len(writes) > 0:
            hist = self.__dict__.setdefault("pool_hist", [])
            if len(hist) >= 3:
                need(hist[-3])
        for sem, val in waits.items():
            seen[sem] = val
        if dma is not None:
            sem = dma.dsem
            self.cnt[sem] += 16
            inc = (sem, 16)
        else:
            sem = self.esem[eng]
            self.cnt[sem] += 1
            inc = (sem, 1)
        tok = (sem, self.cnt[sem])
        if eng == "pool" and dma is not None and len(writes) > 0:
            self.pool_hist.append(tok)
        self.prog[eng].append((list(waits.items()), fn, inc))
        for t in reads:
            if not t.const:
                if t.rd.get(sem, 0) < tok[1]:
                    t.rd[sem] = tok[1]
        for t in writes:
            t.lw = tok
            t.rd = {}
        return tok

    def barrier(self):
        for e in ENG:
            waits = []
            for sem, c in self.cnt.items():
                if c > 0 and self.seen[e].get(sem, 0) < c and sem is not self.esem[e]:
                    waits.append((sem, c))
                    self.seen[e][sem] = c
            self.prog[e].append((waits, None, None))

    def end_stage(self):
        self.barrier()
        if REUSE_DSEMS:
            self.free_dsems.extend(self.stage_dsems)
        self.stage_dsems = []

    def check(self):
        pos = {e: 0 for e in ENG}
        val = {}
        progress = True
        while progress:
            progress = False
            for e in ENG:
                lst = self.prog[e]
                while pos[e] < len(lst):
                    waits, fn, inc = lst[pos[e]]
                    if any(val.get(sem, 0) < v for sem, v in waits):
                        break
                    if inc is not None:
                        val[inc[0]] = val.get(inc[0], 0) + inc[1]
                    pos[e] += 1
                    progress = True
        stuck = {e: (pos[e], len(self.prog[e])) for e in ENG if pos[e] < len(self.prog[e])}
        for e in stuck:
            waits, fn, inc = self.prog[e][pos[e]]
            print("STUCK", e, stuck[e], [(str(sem), v, val.get(sem, 0)) for sem, v in waits])
        bad = {str(sem): (val.get(sem, 0), c) for sem, c in self.cnt.items() if val.get(sem, 0) != c}
        print("check: stuck=%s mismatched=%s" % (bool(stuck), bad))

    def emit(self):
        nc = self.nc
        prog = self.prog

        def replay(lst, e):
            for waits, fn, inc in lst:
                for sem, val in waits:
                    e.wait_ge(sem, val)
                if fn is not None:
                    ins = fn(e)
                    ins.then_inc(inc[0], inc[1])

        with nc.Block() as block:
            @block.tensor
            def _(e):
                replay(prog["pe"], e)

            @block.scalar
            def _(e):
                replay(prog["act"], e)

            @block.vector
            def _(e):
                replay(prog["dve"], e)

            @block.gpsimd
            def _(e):
                replay(prog["pool"], e)

            @block.sync
            def _(e):
                replay(prog["sp"], e)


class Ring:
    def __init__(self, tiles):
        self.tiles = tiles
        self.i = 0

    def next(self):
        t = self.tiles[self.i % len(self.tiles)]
        self.i += 1
        return t


def build_program(debug=False, stop_after=None, only=None, asub=None):
    import contextlib
    nc = bass.Bass("TRN2", target_bir_lowering=False)
    dk = "ExternalOutput" if debug else "Internal"

    NEED_A = ("at_w_in", "qk_norm_fm", "sink_bc", "cos_fm", "sin_fm", "rot_m", "ident_in", "mask_in",
              "normw_fm", "bmod_fm")

    def din(name, shape, dt=F32):
        if only is not None and name not in NEED_A:
            return None
        return nc.dram_tensor(name, list(shape), dt, kind="ExternalInput").ap()

    def dscr(name, shape, dt):
        if debug and (debug is True or name in debug):
            return nc.dram_tensor(name, list(shape), dt, kind="ExternalOutput").ap()
        return nc.dram_tensor(name, list(shape), dt).ap()

    x_loc = din("x_loc", [NLOC, D])
    ctx_loc = din("ctx_loc", [CTX, D])
    c_fm = din("c_fm", [128, KC, 5])
    w_mod = din("w_mod", [2, D, 1536])
    selb_in = din("selb_in", [128, 4])
    bmod_fm = din("bmod_fm", [128, 2, 96])
    normw_fm = din("normw_fm", [128, 2, KC])
    rg_w_in = din("rg_w_in", [D, 2 * D])
    conv5_fm = din("conv5_fm", [128, KC, 5])
    convb_fm = din("convb_fm", [128, KC])
    gate_w = din("gate_w", [4, 16, 256, 256])
    gate_b_fm = din("gate_b_fm", [128, 4, KC])
    lam_fm = din("lam_fm", [128, 2, KC])
    rg_w_out = din("rg_w_out", [D, D])
    at_w_in = din("at_w_in", [D, 10240])
    qk_norm_fm = din("qk_norm_fm", [128, 2])
    sink_bc = din("sink_bc", [128, 32])
    at_w_out = din("at_w_out", [D, D])
    cos_fm = din("cos_fm", [128, NLOC])
    sin_fm = din("sin_fm", [128, NLOC])
    rot_m = din("rot_m", [128, 128])
    ident_in = din("ident_in", [128, 128])
    mask_in = din("mask_in", [2, 128, 512])
    sel_in = din("sel_in", [128, 2])
    out = nc.dram_tensor("out", [NOWN, D], F32, kind="ExternalOutput").ap()

    hT = dscr("hT", [18, 128, KC, 128], BF16)
    hTc = dscr("hTc", [2, 128, KC, 128], BF16)
    Z0 = dscr("Z0", [D, NH], F32)
    ZC = dscr("ZC", [D, NH], F32)
    ZCTX = dscr("ZCTX", [D, CTX], BF16)
    X1 = dscr("X1", [NH, D], F32)
    CTX1 = dscr("CTX1", [CTX, D], F32)
    Z1T = dscr("Z1T", [D, NOWN], BF16)
    grow = dscr("grow", [2, 2, D], F32)
    gin = nc.dram_tensor("gin", [128, KC], F32)
    gout = nc.dram_tensor("gout", [256, KC], F32)
    gin_m = nc.dram_tensor("gin_m", [128, 120], F32)
    gout_m = nc.dram_tensor("gout_m", [1024, 120], F32)

    with contextlib.ExitStack() as gst:
        p = Prog(nc, gst)
        gst.enter_context(nc.allow_non_contiguous_dma(reason="small param scatter"))
        gst.enter_context(nc.allow_low_precision(reason="bf16 matmul operands"))
        ccsem = gst.enter_context(nc.semaphore("ccsem"))
        p.cnt[ccsem] = 0

        ident_bf = p.sb(gst, "ident_bf", [128, 128], BF16, dma=True)
        ident_f = p.sb(gst, "ident_f", [128, 128], F32, dma=True)
        mod = [p.sb(gst, "mod%d" % l, [128, 96, 2], F32) for l in range(2)]
        Avec = [p.sb(gst, "Avec%d" % l, [128, KC, 2], F32) for l in range(2)]
        normw = p.sb(gst, "normw", [128, 2, KC], F32, dma=True)
        bmod = p.sb(gst, "bmod", [128, 2, 96], F32, dma=True)
        Sin = p.sb(gst, "Sin", [128, KC], F32)

        def dma_ld(eng, dst_tk, dst_ap, src_ap, reads=(), extra_w=()):
            p.op(eng, lambda e: e.dma_start(out=dst_ap, in_=src_ap), reads=reads,
                 writes=(dst_tk,) + tuple(extra_w), dma=dst_tk)

        def dma_st(eng, src_tk, dst_ap, src_ap, dst_tk=None):
            p.op(eng, lambda e: e.dma_start(out=dst_ap, in_=src_ap), reads=(src_tk,),
                 writes=(dst_tk,) if dst_tk is not None else (), dma=src_tk)

        dma_ld("sp", ident_f, ident_f[:, :], ident_in)
        dma_ld("pool", ident_bf, ident_bf[:, :], ident_in)
        dma_ld("sp", normw, normw[:, :, :], normw_fm)
        dma_ld("sp", bmod, bmod[:, :, :], bmod_fm)

        with contextlib.ExitStack() as st:
          if only is None:
              cf = p.sb(st, "cf", [128, KC, 5], F32, dma=True)
              scb = p.sb(st, "scb", [128, KC, 5], BF16)
              Wm = Ring([p.sb(st, "Wm%d" % i, [128, KC, 512], BF16, dma=True) for i in range(2)])
              psm = p.ps(st, "psm", [128, 512], F32)
              gsm = p.sb(st, "gsm", [128, 120], F32, dma=True)
              Gm = p.sb(st, "Gm", [128, 8, 120], F32, dma=True)
              selb = p.sb(st, "selb", [128, 4], F32, dma=True)
              dma_ld("sp", cf, cf[:, :, :], c_fm)
              dma_ld("sp", selb, selb[:, :], selb_in)
              p.op("act", lambda e: e.activation(out=scb[:, :, :], in_=cf[:, :, :], func=AF.Silu),
                   reads=(cf,), writes=(scb,))
              for l in range(2):
                  for blk in range(3):
                      W = Wm.next()
                      src = w_mod[l, :, blk * 512:(blk + 1) * 512].rearrange("(k p) n -> p k n", p=128)
                      dma_ld("pool", W, W[:, :, :], src)

                      def mm(e, W=W, blk=blk, l=l):
                          ins = None
                          for j in range(4):
                              n = l * 12 + blk * 4 + j
                              for k in range(KC):
                                  ins = e.matmul(psm[:, n * 5:n * 5 + 5], lhsT=W[:, k, j * 128:(j + 1) * 128],
                                                 rhs=scb[:, k, :], start=(k == 0), stop=(k == KC - 1))
                          return ins
                      p.op("pe", mm, reads=(W, scb), writes=(psm,))
              p.op("dve", lambda e: e.tensor_copy(out=gsm[:, :], in_=psm[:, 0:120]), reads=(psm,), writes=(gsm,))
              p.op("pool", lambda e: e.dma_start(out=gin_m[:, :], in_=gsm[:, :]), reads=(gsm,), dma=gsm)
              p.barrier()
              p.cnt[ccsem] += 1

              def ccm(e):
                  return e.collective_compute("AllGather", ALU.bypass, replica_groups=[list(range(8))],
                                              ins=[gin_m.ap().opt()], outs=[gout_m.ap().opt()])
              p.prog["pool"].append(([], ccm, (ccsem, 1)))
              p.prog["pool"].append(([(ccsem, p.cnt[ccsem])], None, None))
              p.seen["pool"][ccsem] = p.cnt[ccsem]
              dma_ld("pool", Gm, Gm[:, :, :], gout_m.ap().rearrange("(r p) m -> p r m", p=128))
              for l in range(2):
                  Gl = Gm[:, :, l * 60:(l + 1) * 60].rearrange("p r (j w) -> p r j w", w=5)
                  m0 = mod[l][:, :, 0].rearrange("p (r j) -> p r j", j=12)
                  m1 = mod[l][:, :, 1].rearrange("p (r j) -> p r j", j=12)
                  p.op("dve", lambda e, Gl=Gl, m0=m0: e.tensor_scalar(
                      out=m0, in0=Gl[:, :, :, 0], scalar1=selb[:, 0:1], scalar2=None, op0=ALU.mult),
                      reads=(Gm, selb), writes=(mod[l],))
                  for w in range(1, 4):
                      p.op("dve", lambda e, Gl=Gl, m0=m0, w=w: e.scalar_tensor_tensor(
                          out=m0, in0=Gl[:, :, :, w], scalar=selb[:, w:w + 1], in1=m0, op0=ALU.mult, op1=ALU.add),
                          reads=(Gm, selb, mod[l]), writes=(mod[l],))
                  p.op("dve", lambda e, Gl=Gl, m1=m1: e.tensor_copy(out=m1, in_=Gl[:, :, :, 4]),
                       reads=(Gm,), writes=(mod[l],))
                  for r in range(2):
                      p.op("dve", lambda e, l=l, r=r: e.tensor_tensor(
                          out=mod[l][:, :, r], in0=mod[l][:, :, r], in1=bmod[:, l, :], op=ALU.add),
                          reads=(mod[l], bmod), writes=(mod[l],))
                  for r in range(2):
                      p.op("dve", lambda e, l=l, r=r: e.scalar_tensor_tensor(
                          out=Avec[l][:, :, r], in0=mod[l][:, 32:64, r], scalar=1.0, in1=normw[:, l, :],
                          op0=ALU.add, op1=ALU.mult), reads=(mod[l], normw), writes=(Avec[l],))
                      p.op("sp", lambda e, l=l, r=r: e.dma_start(
                          out=grow[l, r].rearrange("(k p) -> p k", p=128), in_=mod[l][:, 64:96, r]),
                          reads=(mod[l],), writes=(), dma=cf)
              p.end_stage()

        def stage_norm(l, src_lat, ntiles, src_ctx):
            with contextlib.ExitStack() as st:
                xt_r = Ring([p.sb(st, "xt%d" % i, [128, D], F32, dma=True) for i in range(2)])
                xn_r = Ring([p.sb(st, "xn%d" % i, [128, D], BF16) for i in range(2)])
                junk = p.sb(st, "junk", [128, D], BF16)
                ss_r = Ring([p.sb(st, "ss%d" % i, [128, 1], F32) for i in range(4)])
                rs_r = Ring([p.sb(st, "rs%d" % i, [128, 1], F32) for i in range(4)])
                pt_r = Ring([p.ps(st, "pt%d" % i, [128, D], BF16) for i in range(2)])
                hb_r = Ring([p.sb(st, "hb%d" % i, [128, 4, KC, 128], BF16, dma=True) for i in range(2)])
                jobs = []
                nblk = (ntiles + 3) // 4
                for b in range(nblk):
                    tl = list(range(b * 4, min(ntiles, b * 4 + 4)))
                    jobs.append((src_lat, tl, hT, b * 512, 0))
                jobs.append((src_ctx, [0, 1], hTc, 0, 1))
                for src, tl, dst, c0, r in jobs:
                    hb = hb_r.next()
                    for ti, t in enumerate(tl):
                        xt = xt_r.next(); xn = xn_r.next(); ss = ss_r.next(); rs = rs_r.next(); pt = pt_r.next()
                        dma_ld("sp", xt, xt[:, :], src[t * 128:(t + 1) * 128, :])
                        p.op("act", lambda e, xt=xt, ss=ss: e.activation(
                            out=junk[:, :], in_=xt[:, :], func=AF.Square, accum_out=ss[:, :]),
                            reads=(xt,), writes=(junk, ss))
                        p.op("dve", lambda e, ss=ss, rs=rs: e.tensor_scalar(
                            out=rs[:, :], in0=ss[:, :], scalar1=1.0 / D, scalar2=EPS, op0=ALU.mult, op1=ALU.add),
                            reads=(ss,), writes=(rs,))
                        p.op("act", lambda e, rs=rs: e.activation(out=rs[:, :], in_=rs[:, :], func=AF.Sqrt),
                             reads=(rs,), writes=(rs,))
                        p.op("dve", lambda e, rs=rs: e.reciprocal(out=rs[:, :], in_=rs[:, :]),
                            reads=(rs,), writes=(rs,))
                        p.op("dve", lambda e, xt=xt, xn=xn, rs=rs: e.tensor_scalar(
                            out=xn[:, :], in0=xt[:, :], scalar1=rs[:, 0:1], scalar2=None, op0=ALU.mult),
                            reads=(xt, rs), writes=(xn,))

                        def tr(e, xn=xn, pt=pt):
                            ins = None
                            for j in range(KC):
                                ins = e.transpose(out=pt[:, j * 128:(j + 1) * 128], in_=xn[:, j * 128:(j + 1) * 128],
                                                  identity=ident_bf[:, :])
                            return ins
                        p.op("pe", tr, reads=(xn, ident_bf), writes=(pt,))

                        def ev(e, pt=pt, hb=hb, ti=ti, r=r):
                            ins = None
                            for j in range(KC):
                                ins = e.activation(out=hb[:, ti, j, :], in_=pt[:, j * 128:(j + 1) * 128],
                                                   func=AF.Identity, scale=Avec[l][:, j, r:r + 1],
                                                   bias=mod[l][:, j, r:r + 1])
                            return ins
                        p.op("act", ev, reads=(pt, Avec[l], mod[l]), writes=(hb,))
                    dma_st("act", hb, dst[tl[0]:tl[0] + len(tl)].rearrange("t p k c -> p t k c"), hb[:, 0:len(tl), :, :])
                p.end_stage()

        order = ["M", "N0", "R", "O0", "N1", "A", "O1"]
        nstage = len(order) if stop_after is None else order.index(stop_after) + 1
        if only is not None:
            nstage = 6 if "A" in only else 0
        if nstage >= 2 and only is None:
            stage_norm(0, x_loc, 18, ctx_loc)

        with contextlib.ExitStack() as st:
          if nstage >= 3 and only is None:
              Wr = Ring([p.sb(st, "Wp%d" % i, [128, KC, 256], BF16, dma=True) for i in range(3)])
              hb_r = Ring([p.sb(st, "hs%d" % i, [128, 2, KC, 128], BF16, dma=True) for i in range(3)])
              u = [p.sb(st, "u%d" % j, [128, NU + 4], F32) for j in range(2)]
              ucx = [p.sb(st, "ucx%d" % j, [128, CTX + 4], F32) for j in range(2)]
              uc = [p.sb(st, "uc%d" % j, [128, NH], F32) for j in range(2)]
              ucc = [p.sb(st, "ucc%d" % j, [128, CTX], F32) for j in range(2)]
              ucb = [p.sb(st, "ucb%d" % j, [128, NH], BF16) for j in range(2)]
              uccb = [p.sb(st, "uccb%d" % j, [128, CTX], BF16) for j in range(2)]
              sg = [[p.sb(st, "sg%d_%d" % (q, j), [128, NH], BF16) for j in range(2)] for q in range(2)]
              sgc = [[p.sb(st, "sgc%d_%d" % (q, j), [128, CTX], BF16) for j in range(2)] for q in range(2)]
              hA = p.sb(st, "hA", [128, NH], F32)
              hAc = p.sb(st, "hAc", [128, CTX], F32)
              hBc = p.sb(st, "hBc", [128, CTX], F32)
              zcb_r = Ring([p.sb(st, "zcb%d" % i, [128, CTX], BF16, dma=True) for i in range(1)])
              t1_r = Ring([p.sb(st, "t1_%d" % i, [128, 512], F32) for i in range(2)])
              t2_r = Ring([p.sb(st, "t2_%d" % i, [128, 512], F32) for i in range(2)])
              t3_r = Ring([p.sb(st, "t3_%d" % i, [128, 512], F32, dma=True) for i in range(2)])
              t4_r = Ring([p.sb(st, "t4_%d" % i, [128, 512], F32, dma=True) for i in range(2)])
              zeros = p.sb(st, "zeros", [128, 512], F32, const=True)
              GW_r = Ring([p.sb(st, "GW%d" % i, [128, 4, 2, 256], BF16, dma=True) for i in range(2)])
              st3_r = Ring([p.sb(st, "st3_%d" % i, [128, 1], F32) for i in range(4)])
              st4_r = Ring([p.sb(st, "st4_%d" % i, [128, 1], F32) for i in range(4)])
              conv5 = p.sb(st, "conv5", [128, KC, 5], F32, dma=True)
              convb = p.sb(st, "convb", [128, KC], F32, dma=True)
              gb_t = p.sb(st, "gb_t", [128, 4, KC], F32, dma=True)
              lam_t = p.sb(st, "lam_t", [128, 2, KC], F32, dma=True)
              cneg = p.sb(st, "cneg", [128, 2, KC], F32)
              SAbuf = p.sb(st, "SAbuf", [128, KC], F32, dma=True)
              gsb = p.sb(st, "gsb", [128, 2, KC], F32, dma=True)
              sel = p.sb(st, "sel", [128, 2], F32, dma=True)
              pp_r = Ring([p.ps(st, "pp%d" % i, [128, 512], F32) for i in range(3)])
              pr_r = Ring([p.ps(st, "pr%d" % i, [128, 512], F32) for i in range(2)])
              pi_r = Ring([p.ps(st, "pi%d" % i, [128, 512], F32) for i in range(2)])

              dma_ld("sp", conv5, conv5[:, :, :], conv5_fm)
              dma_ld("sp", convb, convb[:, :], convb_fm)
              dma_ld("sp", gb_t, gb_t[:, :, :], gate_b_fm)
              dma_ld("sp", lam_t, lam_t[:, :, :], lam_fm)
              dma_ld("sp", sel, sel[:, :], sel_in)
              p.op("dve", lambda e: e.memset(zeros[:, :], 0.0), writes=(zeros,))
              for j in range(2):
                  p.op("dve", lambda e, j=j: e.memset(u[j][:, :], 0.0), writes=(u[j],))
                  p.op("dve", lambda e, j=j: e.memset(ucx[j][:, :], 0.0), writes=(ucx[j],))
              p.op("act", lambda e: e.activation(out=cneg[:, :, :], in_=lam_t[:, :, :], func=AF.Exp, scale=-1.0),
                   reads=(lam_t,), writes=(cneg,))
              p.op("act", lambda e: e.activation(out=cneg[:, :, :], in_=cneg[:, :, :], func=AF.Ln, bias=1.0),
                   reads=(cneg,), writes=(cneg,))
              p.op("dve", lambda e: e.tensor_scalar(out=cneg[:, :, :], in0=cneg[:, :, :], scalar1=-8.0, scalar2=None,
                                                     op0=ALU.mult), reads=(cneg,), writes=(cneg,))

              tblocks = [(i * 256, 256) for i in range(8)] + [(2048, 130)]
              chunks = [(0, 512), (512, 512), (1024, 512), (1536, 512), (2048, 128)]

              def proj_gen(gbi):
                  q = gbi % 2
                  Ws = []
                  for c0 in (gbi * 256, D + gbi * 256):
                      W = Wr.next()
                      dma_ld("pool", W, W[:, :, :], rg_w_in[:, c0:c0 + 256].rearrange("(k p) n -> p k n", p=128))
                      Ws.append(W)
                  GW = GW_r.next()
                  GWs[gbi] = GW
                  for gi in range(4):
                      p.op("pool", lambda e, GW=GW, gi=gi, gbi=gbi: e.dma_start(
                          out=GW[:, gi, :, :], in_=gate_w[gi, gbi].rearrange("(kh p) j -> p kh j", p=128)),
                          writes=(GW,), dma=GW)
                  seqs = [("c", 0, CTX)] + [("l", t0, n) for t0, n in tblocks]
                  for kind, t0, n in seqs:
                      hb = hb_r.next()
                      srcv = hTc[0:2] if kind == "c" else hT[t0 // 128:t0 // 128 + 2]
                      dma_ld("sp", hb, hb[:, :, :, :], srcv.rearrange("t p k c -> p t k c"))
                      for ci in range(4):
                          pp = pp_r.next()

                          def mm(e, W=Ws[ci // 2], jj=ci % 2, hb=hb, pp=pp, n=n):
                              ins = None
                              if n == 130:
                                  for k in range(KC):
                                      ins = e.matmul(pp[:, 0:128], lhsT=W[:, k, jj * 128:(jj + 1) * 128],
                                                     rhs=hb[:, 0, k, :], start=(k == 0), stop=(k == KC - 1))
                                  for k in range(KC):
                                      ins = e.matmul(pp[:, 128:130], lhsT=W[:, k, jj * 128:(jj + 1) * 128],
                                                     rhs=hb[:, 1, k, 0:2], start=(k == 0), stop=(k == KC - 1))
                              else:
                                  for k in range(KC):
                                      ins = e.matmul(pp[:, 0:256].rearrange("p (t c) -> p t c", c=128),
                                                     lhsT=W[:, k, jj * 128:(jj + 1) * 128], rhs=hb[:, 0:2, k, :],
                                                     start=(k == 0), stop=(k == KC - 1))
                              return ins
                          p.op("pe", mm, reads=(Ws[ci // 2], hb), writes=(pp,))
                          j = ci % 2
                          if ci < 2:
                              if kind == "c":
                                  dstt, dsta = ucx[j], ucx[j][:, 2:2 + CTX]
                              else:
                                  dstt, dsta = u[j], u[j][:, 2 + t0:2 + t0 + n]
                              p.op("act", lambda e, pp=pp, dsta=dsta, n=n: e.activation(
                                  out=dsta, in_=pp[:, 0:n], func=AF.Copy), reads=(pp,), writes=(dstt,))
                          else:
                              n2 = min(n, 128) if (kind == "l" and t0 == 2048) else n
                              if kind == "c":
                                  dstt, dsta = sgc[q][j], sgc[q][j][:, 0:CTX]
                              else:
                                  dstt, dsta = sg[q][j], sg[q][j][:, t0:t0 + n2]
                              p.op("act", lambda e, pp=pp, dsta=dsta, n2=n2: e.activation(
                                  out=dsta, in_=pp[:, 0:n2], func=AF.Silu), reads=(pp,), writes=(dstt,))
                      yield
              def elem_gen(gbi):
                  q = gbi % 2
                  GW = GWs[gbi]
                  for j in range(2):
                      ch = gbi * 2 + j
                      for (ut, uct, ucbt, nn) in ((ucx[j], ucc[j], uccb[j], CTX), (u[j], uc[j], ucb[j], NH)):
                          p.op("dve", lambda e, ut=ut, uct=uct, nn=nn, ch=ch: e.tensor_scalar(
                              out=uct[:, 0:nn], in0=ut[:, 0:nn], scalar1=conv5[:, ch, 0:1], scalar2=convb[:, ch:ch + 1],
                              op0=ALU.mult, op1=ALU.add), reads=(ut, conv5, convb), writes=(uct,))
                          for k in range(1, 5):
                              p.op("dve", lambda e, ut=ut, uct=uct, nn=nn, ch=ch, k=k: e.scalar_tensor_tensor(
                                  out=uct[:, 0:nn], in0=ut[:, k:k + nn], scalar=conv5[:, ch, k:k + 1], in1=uct[:, 0:nn],
                                  op0=ALU.mult, op1=ALU.add), reads=(ut, conv5, uct), writes=(uct,))
                          p.op("act", lambda e, uct=uct, ucbt=ucbt, nn=nn: e.activation(
                              out=ucbt[:, 0:nn], in_=uct[:, 0:nn], func=AF.Copy), reads=(uct,), writes=(ucbt,))
                  yield
                  def half_gen(j):
                      ch = gbi * 2 + j

                      def gate_ab(d, ucb_pair, uct, c0, n):
                          pr = pr_r.next(); pi = pi_r.next()
                          t1 = t1_r.next(); t2 = t2_r.next()

                          def mm(e, pr=pr, pi=pi):
                              ins = None
                              for (pt_, gi) in ((pr, 2 * d), (pi, 2 * d + 1)):
                                  for kh in range(2):
                                      ins = e.matmul(pt_[:, 0:n], lhsT=GW[:, gi, kh, j * 128:(j + 1) * 128],
                                                     rhs=ucb_pair[kh][:, c0:c0 + n], start=(kh == 0), stop=(kh == 1))
                              return ins
                          p.op("pe", mm, reads=(GW, ucb_pair[0], ucb_pair[1]), writes=(pr, pi))
                          p.op("act", lambda e: e.activation(out=t1[:, 0:n], in_=pr[:, 0:n], func=AF.Sigmoid,
                                                             bias=gb_t[:, 2 * d, ch:ch + 1]),
                               reads=(pr, gb_t), writes=(t1,))
                          p.op("act", lambda e: e.activation(out=t2[:, 0:n], in_=pi[:, 0:n], func=AF.Sigmoid,
                                                             bias=gb_t[:, 2 * d + 1, ch:ch + 1]),
                               reads=(pi, gb_t), writes=(t2,))
                          p.op("act", lambda e: e.activation(out=t1[:, 0:n], in_=t1[:, 0:n], func=AF.Exp,
                                                             scale=cneg[:, d, ch:ch + 1]),
                               reads=(t1, cneg), writes=(t1,))
                          return t1, t2

                      def finish_b(t1, t2, t3, uct, c0, n):
                          p.op("dve", lambda e: e.tensor_tensor(out=t2[:, 0:n], in0=t2[:, 0:n], in1=uct[:, c0:c0 + n],
                                                                op=ALU.mult), reads=(t2, uct), writes=(t2,))
                          p.op("act", lambda e: e.activation(out=t3[:, 0:n], in_=t1[:, 0:n], func=AF.Square),
                               reads=(t1,), writes=(t3,))
                          p.op("act", lambda e: e.activation(out=t3[:, 0:n], in_=t3[:, 0:n], func=AF.Sqrt,
                                                             scale=-1.0, bias=1.0), reads=(t3,), writes=(t3,))
                          p.op("dve", lambda e: e.tensor_tensor(out=t2[:, 0:n], in0=t2[:, 0:n], in1=t3[:, 0:n],
                                                                op=ALU.mult), reads=(t2, t3), writes=(t2,))

                      t1, t2 = gate_ab(0, uccb, ucc[j], 0, CTX)
                      t3 = t3_r.next()
                      finish_b(t1, t2, t3, ucc[j], 0, CTX)
                      p.op("dve", lambda e, t1=t1, t2=t2: e.tensor_tensor_scan(
                          out=hAc[:, :], data0=t1[:, 0:CTX], data1=t2[:, 0:CTX], initial=0.0,
                          op0=ALU.mult, op1=ALU.add), reads=(t1, t2), writes=(hAc,))
                      yield
                      for ci_, (c0, n) in enumerate(chunks):
                          t1, t2 = gate_ab(0, ucb, uc[j], c0, n)
                          t3 = t3_r.next()
                          finish_b(t1, t2, t3, uc[j], c0, n)
                          init = hAc[:, CTX - 1:CTX] if ci_ == 0 else hA[:, c0 - 1:c0]
                          p.op("dve", lambda e, t1=t1, t2=t2, c0=c0, n=n, init=init: e.tensor_tensor_scan(
                              out=hA[:, c0:c0 + n], data0=t1[:, 0:n], data1=t2[:, 0:n], initial=init,
                              op0=ALU.mult, op1=ALU.add), reads=(t1, t2, hA, hAc), writes=(hA,))
                          yield
                      p.op("act", lambda e, ch=ch: e.activation(out=SAbuf[:, ch:ch + 1], in_=hA[:, 1919:1920],
                                                                func=AF.Copy), reads=(hA,), writes=(SAbuf,))
                      t1, t2 = gate_ab(1, uccb, ucc[j], 0, CTX)
                      t3 = t3_r.next()
                      finish_b(t1, t2, t3, ucc[j], 0, CTX)
                      p.op("dve", lambda e, t1=t1, t2=t2: e.tensor_tensor_scan(
                          out=hBc[:, ::-1], data0=t1[:, 0:CTX][:, ::-1], data1=t2[:, 0:CTX][:, ::-1], initial=0.0,
                          op0=ALU.mult, op1=ALU.add), reads=(t1, t2), writes=(hBc,))
                      zcb = zcb_r.next()
                      p.op("dve", lambda e: e.tensor_tensor(out=hBc[:, :], in0=hBc[:, :], in1=hAc[:, :], op=ALU.add),
                           reads=(hBc, hAc), writes=(hBc,))
                      p.op("dve", lambda e, zcb=zcb: e.tensor_tensor(out=zcb[:, :], in0=hBc[:, :], in1=sgc[q][j][:, :],
                                                                    op=ALU.mult), reads=(hBc, sgc[q][j]), writes=(zcb,))
                      dma_st("pool", zcb, ZCTX[ch * 128:(ch + 1) * 128, :], zcb[:, :])
                      yield
                      prev3 = None; prev4 = None; prevn = None
                      for ci_ in range(len(chunks) - 1, -1, -1):
                          c0, n = chunks[ci_]
                          t1, t2 = gate_ab(1, ucb, uc[j], c0, n)
                          t3 = t3_r.next(); t4 = t4_r.next()
                          finish_b(t1, t2, t3, uc[j], c0, n)
                          if prev4 is None:
                              p.op("dve", lambda e, t1=t1, t4=t4, n=n: e.tensor_tensor_scan(
                                  out=t4[:, 0:n][:, ::-1], data0=t1[:, 0:n][:, ::-1], data1=zeros[:, 0:n], initial=1.0,
                                  op0=ALU.mult, op1=ALU.add), reads=(t1, zeros), writes=(t4,))
                              p.op("dve", lambda e, t1=t1, t2=t2, t3=t3, n=n: e.tensor_tensor_scan(
                                  out=t3[:, 0:n][:, ::-1], data0=t1[:, 0:n][:, ::-1], data1=t2[:, 0:n][:, ::-1],
                                  initial=0.0, op0=ALU.mult, op1=ALU.add), reads=(t1, t2), writes=(t3,))
                          else:
                              p.op("dve", lambda e, t1=t1, t4=t4, n=n, st4=st4: e.tensor_tensor_scan(
                                  out=t4[:, 0:n][:, ::-1], data0=t1[:, 0:n][:, ::-1], data1=zeros[:, 0:n],
                                  initial=st4[:, 0:1], op0=ALU.mult, op1=ALU.add), reads=(t1, zeros, st4), writes=(t4,))
                              p.op("dve", lambda e, t1=t1, t2=t2, t3=t3, n=n, st3=st3: e.tensor_tensor_scan(
                                  out=t3[:, 0:n][:, ::-1], data0=t1[:, 0:n][:, ::-1], data1=t2[:, 0:n][:, ::-1],
                                  initial=st3[:, 0:1], op0=ALU.mult, op1=ALU.add), reads=(t1, t2, st3), writes=(t3,))
                          st3 = st3_r.next()
                          st4 = st4_r.next()
                          p.op("act", lambda e, t3=t3, st3=st3: e.activation(out=st3[:, :], in_=t3[:, 0:1], func=AF.Copy),
                               reads=(t3,), writes=(st3,))
                          p.op("act", lambda e, t4=t4, st4=st4: e.activation(out=st4[:, :], in_=t4[:, 0:1], func=AF.Copy),
                               reads=(t4,), writes=(st4,))
                          prev4 = t4
                          p.op("dve", lambda e, t3=t3, c0=c0, n=n: e.tensor_tensor(
                              out=t3[:, 0:n], in0=t3[:, 0:n], in1=hA[:, c0:c0 + n], op=ALU.add),
                              reads=(t3, hA), writes=(t3,))
                          p.op("dve", lambda e, t3=t3, c0=c0, n=n: e.tensor_tensor(
                              out=t3[:, 0:n], in0=t3[:, 0:n], in1=sg[q][j][:, c0:c0 + n], op=ALU.mult),
                              reads=(t3, sg[q][j]), writes=(t3,))
                          p.op("dve", lambda e, t4=t4, c0=c0, n=n: e.tensor_tensor(
                              out=t4[:, 0:n], in0=t4[:, 0:n], in1=sg[q][j][:, c0:c0 + n], op=ALU.mult),
                              reads=(t4, sg[q][j]), writes=(t4,))
                          dma_st("pool", t3, Z0[ch * 128:(ch + 1) * 128, c0:c0 + n], t3[:, 0:n])
                          dma_st("pool", t4, ZC[ch * 128:(ch + 1) * 128, c0:c0 + n], t4[:, 0:n])
                          yield
                  for j_ in range(2):
                      yield from half_gen(j_)
              GWs = {}
              for _ in proj_gen(0):
                  pass
              for gbi_ in range(16):
                  eg = elem_gen(gbi_)
                  pg = proj_gen(gbi_ + 1) if gbi_ < 15 else None
                  next(eg)
                  ne = 0
                  for _ in eg:
                      ne += 1
                      if pg is not None and ne % 2 == 0:
                          if next(pg, "end") == "end":
                              pg = None
                  if pg is not None:
                      for _ in pg:
                          pass
              p.op("pool", lambda e: e.dma_start(out=gin[:, :], in_=SAbuf[:, :]), reads=(SAbuf,), dma=SAbuf)
              p.barrier()
              p.cnt[ccsem] += 1

              def cc(e):
                  return e.collective_compute("AllGather", ALU.bypass,
                                              replica_groups=[[0, 1], [2, 3], [4, 5], [6, 7]],
                                              ins=[gin.ap().opt()], outs=[gout.ap().opt()])
              p.prog["pool"].append(([], cc, (ccsem, 1)))
              p.prog["pool"].append(([(ccsem, p.cnt[ccsem])], None, None))
              p.seen["pool"][ccsem] = p.cnt[ccsem]
              dma_ld("pool", gsb, gsb[:, :, :], gout.ap().rearrange("(r p) k -> p r k", p=128))
              p.op("dve", lambda e: e.tensor_scalar(out=Sin[:, :], in0=gsb[:, 0, :], scalar1=sel[:, 0:1], scalar2=None,
                                                     op0=ALU.mult), reads=(gsb, sel), writes=(Sin,))
              p.op("dve", lambda e: e.scalar_tensor_tensor(out=Sin[:, :], in0=gsb[:, 1, :], scalar=sel[:, 1:2],
                                                            in1=Sin[:, :], op0=ALU.mult, op1=ALU.add),
                   reads=(gsb, sel, Sin), writes=(Sin,))
              p.end_stage()

        def stage_out(l, w_out_ap, blocks):
            with contextlib.ExitStack() as st:
                zT_r = Ring([p.sb(st, "zT%d" % i, [128, KC, 512], BF16, dma=True) for i in range(2)])
                Wo_r = Ring([p.sb(st, "Wo%d" % i, [128, KC, 512], BF16, dma=True) for i in range(2)])
                z0_r = Ring([p.sb(st, "z0_%d" % i, [128, 512], F32, dma=True) for i in range(3)])
                zc_r = Ring([p.sb(st, "zc_%d" % i, [128, 512], F32, dma=True) for i in range(3)])
                gbc = [p.sb(st, "gbc%d" % r, [128, D], F32, dma=True) for r in range(2)]
                xc_r = Ring([p.sb(st, "xc%d" % i, [128, 512], F32, dma=True) for i in range(3)])
                yo_r = Ring([p.sb(st, "yo%d" % i, [128, 512], F32, dma=True) for i in range(3)])
                po_r = Ring([p.ps(st, "po%d" % i, [128, 512], F32) for i in range(4)])
                for r in range(2):
                    dma_ld("sp", gbc[r], gbc[r][:, :], grow[l, r].partition_broadcast(128))
                for blk in blocks:
                    n = blk["n"]; t0 = blk["t0"]
                    zT = zT_r.next()
                    if blk["zmode"] == "corr":
                        for c in range(KC):
                            z0 = z0_r.next(); zc = zc_r.next()
                            dma_ld("sp", z0, z0[:, 0:n], Z0[c * 128:(c + 1) * 128, t0:t0 + n])
                            dma_ld("sp", zc, zc[:, 0:n], ZC[c * 128:(c + 1) * 128, t0:t0 + n])
                            p.op("dve", lambda e, z0=z0, zc=zc, zT=zT, c=c, n=n: e.scalar_tensor_tensor(
                                out=zT[:, c, 0:n], in0=zc[:, 0:n], scalar=Sin[:, c:c + 1], in1=z0[:, 0:n],
                                op0=ALU.mult, op1=ALU.add), reads=(z0, zc, Sin), writes=(zT,))
                    else:
                        zsrc = blk["zsrc"]
                        dma_ld("sp", zT, zT[:, :, 0:n], zsrc.rearrange("(k p) t -> p k t", p=128)[:, :, t0:t0 + n])
                    for nb in range(8):
                        Wo = Wo_r.next()
                        dma_ld("pool", Wo, Wo[:, :, :],
                               w_out_ap[:, nb * 512:(nb + 1) * 512].rearrange("(k p) n -> p k n", p=128))
                        for tt in range(n // 128):
                            po = po_r.next(); xc = xc_r.next(); yo = yo_r.next()
                            r0 = t0 + tt * 128

                            def mm(e, zT=zT, Wo=Wo, po=po, tt=tt):
                                ins = None
                                for k in range(KC):
                                    ins = e.matmul(po[:, :], lhsT=zT[:, k, tt * 128:(tt + 1) * 128], rhs=Wo[:, k, :],
                                                   start=(k == 0), stop=(k == KC - 1))
                                return ins
                            p.op("pe", mm, reads=(zT, Wo), writes=(po,))
                            dma_ld("sp", xc, xc[:, :], blk["xsrc"][r0:r0 + 128, nb * 512:(nb + 1) * 512])
                            g = gbc[blk["grow"]]
                            p.op("dve", lambda e, po=po, yo=yo, g=g, nb=nb: e.tensor_tensor(
                                out=yo[:, :], in0=po[:, :], in1=g[:, nb * 512:(nb + 1) * 512], op=ALU.mult),
                                reads=(po, g), writes=(yo,))
                            p.op("dve", lambda e, yo=yo, xc=xc: e.tensor_tensor(
                                out=yo[:, :], in0=yo[:, :], in1=xc[:, :], op=ALU.add), reads=(yo, xc), writes=(yo,))
                            dma_st("act", yo, blk["dst"][r0:r0 + 128, nb * 512:(nb + 1) * 512], yo[:, :])
                p.end_stage()

        blocks0 = [dict(zmode="ctx", zsrc=ZCTX, t0=0, n=CTX, xsrc=ctx_loc, dst=CTX1, grow=1)]
        for t0, n in [(0, 512), (512, 512), (1024, 512), (1536, 512), (2048, 128)]:
            blocks0.append(dict(zmode="corr", t0=t0, n=n, xsrc=x_loc, dst=X1, grow=0))
        if nstage >= 4 and only is None:
            stage_out(0, rg_w_out, blocks0)
        if (nstage >= 5 and only is None) or (only is not None and 'N1' in only):
            stage_norm(1, X1, 17, CTX1)

        with contextlib.ExitStack() as st:
          if nstage >= 6:
              Wr = Ring([p.sb(st, "Wq%d" % i, [128, KC, 128], BF16, dma=True) for i in range(6)])
              hb_r = Ring([p.sb(st, "ha%d" % i, [128, 4, KC, 128], BF16, dma=True) for i in range(2)])
              QT = p.sb(st, "QT", [128, 4, NOWN], BF16)
              KT = p.sb(st, "KT", [128, NH + CTX], BF16)
              Vt = p.sb(st, "Vt", [128, 19, 128], BF16)
              OT = p.sb(st, "OT", [128, 4, NOWN], BF16)
              cosT = p.sb(st, "cosT", [128, NH], F32, dma=True)
              sinT = p.sb(st, "sinT", [128, NH], F32, dma=True)
              rotm = p.sb(st, "rotm", [128, 128], F32, dma=True)
              ones_f = p.sb(st, "ones_f", [128, 128], F32)
              ones_b = p.sb(st, "ones_b", [128, 128], BF16)
              qkn = p.sb(st, "qkn", [128, 2], F32, dma=True)
              esink = p.sb(st, "esink", [128, 32], F32, dma=True)
              masks = p.sb(st, "masks", [128, 2, 512], BF16, dma=True)
              sq_r = Ring([p.sb(st, "sq%d" % i, [128, 512], F32) for i in range(1)])
              qr_r = Ring([p.sb(st, "qr%d" % i, [128, 512], F32) for i in range(2)])
              rs_r = Ring([p.sb(st, "rsa%d" % i, [128, 512], F32) for i in range(2)])
              tq_r = Ring([p.sb(st, "tq%d" % i, [128, 512], F32) for i in range(1)])
              vb_r = Ring([p.sb(st, "vb%d" % i, [128, 512], BF16) for i in range(2)])
              PT_r = Ring([p.sb(st, "PT%d" % i, [128, 512], BF16) for i in range(3)])
              den_r = Ring([p.sb(st, "den%d" % i, [128, 512], F32) for i in range(1)])
              zt_r = Ring([p.sb(st, "zt%d" % i, [128, 512], BF16, dma=True) for i in range(3)])
              gs_r = Ring([p.sb(st, "gs%d" % i, [128, 512], F32) for i in range(2)])
              pp_r = Ring([p.ps(st, "pa%d" % i, [128, 512], F32) for i in range(2)])
              px_r = Ring([p.ps(st, "px%d" % i, [128, 512], F32) for i in range(2)])
              pS_r = Ring([p.ps(st, "pS%d" % i, [128, 512], F32) for i in range(2)])
              pO = p.ps(st, "pO", [128, 512], F32)
              pR = p.ps(st, "pR", [128, 512], F32)

              dma_ld("sp", cosT, cosT[:, :], cos_fm[:, 0:NH])
              dma_ld("sp", sinT, sinT[:, :], sin_fm[:, 0:NH])
              dma_ld("sp", rotm, rotm[:, :], rot_m)
              dma_ld("sp", qkn, qkn[:, :], qk_norm_fm)
              dma_ld("sp", esink, esink[:, :], sink_bc)
              p.op("act", lambda e: e.activation(out=esink[:, :], in_=esink[:, :], func=AF.Exp),
                   reads=(esink,), writes=(esink,))
              p.op("pool", lambda e: e.dma_start(out=masks[:, :, :], in_=mask_in.rearrange("m p n -> p m n")),
                   writes=(masks,), dma=masks)
              p.op("dve", lambda e: e.memset(ones_f[:, :], 1.0), writes=(ones_f,))
              p.op("dve", lambda e: e.memset(ones_b[:, :], 1.0), writes=(ones_b,))

              SCALE = 128.0 ** -0.5

              def project(cols, seqs, evac):
                  Ws = []
                  for c0 in cols:
                      W = Wr.next()
                      dma_ld("pool", W, W[:, :, :], at_w_in[:, c0:c0 + 128].rearrange("(k p) n -> p k n", p=128))
                      Ws.append(W)
                  for kind, t0, n in seqs:
                      hb = hb_r.next()
                      nt_ = n // 128
                      srcv = hTc[0:2] if kind == "c" else hT[t0 // 128:t0 // 128 + nt_]
                      dma_ld("sp", hb, hb[:, 0:nt_, :, :], srcv.rearrange("t p k c -> p t k c"))
                      for ci in range(len(cols)):
                          if not evac(ci, kind, t0, n, None):
                              continue
                          pp = pp_r.next()

                          def mm(e, W=Ws[ci], hb=hb, pp=pp, n=n, nt_=nt_):
                              ins = None
                              for k in range(KC):
                                  if nt_ == 1:
                                      ins = e.matmul(pp[:, 0:128], lhsT=W[:, k, :], rhs=hb[:, 0, k, :],
                                                     start=(k == 0), stop=(k == KC - 1))
                                  else:
                                      ins = e.matmul(pp[:, 0:n].rearrange("p (t c) -> p t c", c=128), lhsT=W[:, k, :],
                                                     rhs=hb[:, 0:nt_, k, :], start=(k == 0), stop=(k == KC - 1))
                              return ins
                          p.op("pe", mm, reads=(Ws[ci], hb), writes=(pp,))
                          evac(ci, kind, t0, n, pp)

              def norm_rope(pp, n, wcol, rope_t0, dst_tk, dst_ap):
                  sq = sq_r.next(); qr = qr_r.next(); rs = rs_r.next(); tq = tq_r.next()
                  px = px_r.next()
                  p.op("act", lambda e: e.activation(out=sq[:, 0:n], in_=pp[:, 0:n], func=AF.Square),
                       reads=(pp,), writes=(sq,))
                  p.op("act", lambda e: e.activation(out=qr[:, 0:n], in_=pp[:, 0:n], func=AF.Copy),
                       reads=(pp,), writes=(qr,))
                  p.op("pe", lambda e: e.matmul(px[:, 0:n], lhsT=ones_f[:, :], rhs=sq[:, 0:n], start=True, stop=True),
                       reads=(ones_f, sq), writes=(px,))
                  p.op("dve", lambda e: e.tensor_scalar(out=rs[:, 0:n], in0=px[:, 0:n], scalar1=1.0 / 128, scalar2=EPS,
                                                         op0=ALU.mult, op1=ALU.add), reads=(px,), writes=(rs,))
                  p.op("act", lambda e: e.activation(out=rs[:, 0:n], in_=rs[:, 0:n], func=AF.Sqrt),
                       reads=(rs,), writes=(rs,))
                  p.op("dve", lambda e: e.reciprocal(out=rs[:, 0:n], in_=rs[:, 0:n]), reads=(rs,), writes=(rs,))
                  if rope_t0 is None:
                      p.op("dve", lambda e: e.scalar_tensor_tensor(
                          out=dst_ap, in0=qr[:, 0:n], scalar=qkn[:, wcol:wcol + 1], in1=rs[:, 0:n],
                          op0=ALU.mult, op1=ALU.mult), reads=(qr, qkn, rs), writes=(dst_tk,))
                      return
                  p.op("dve", lambda e: e.scalar_tensor_tensor(
                      out=qr[:, 0:n], in0=qr[:, 0:n], scalar=qkn[:, wcol:wcol + 1], in1=rs[:, 0:n],
                      op0=ALU.mult, op1=ALU.mult), reads=(qr, qkn, rs), writes=(qr,))
                  px2 = px_r.next()
                  p.op("pe", lambda e: e.matmul(px2[:, 0:n], lhsT=rotm[:, :], rhs=qr[:, 0:n], start=True, stop=True),
                       reads=(rotm, qr), writes=(px2,))
                  p.op("dve", lambda e: e.tensor_tensor(out=tq[:, 0:n], in0=px2[:, 0:n],
                                                         in1=sinT[:, rope_t0:rope_t0 + n], op=ALU.mult),
                       reads=(px2, sinT), writes=(tq,))
                  p.op("dve", lambda e: e.tensor_tensor(out=qr[:, 0:n], in0=qr[:, 0:n],
                                                         in1=cosT[:, rope_t0:rope_t0 + n], op=ALU.mult),
                       reads=(qr, cosT), writes=(qr,))
                  p.op("dve", lambda e: e.tensor_tensor(out=dst_ap, in0=qr[:, 0:n], in1=tq[:, 0:n], op=ALU.add),
                       reads=(qr, tq), writes=(dst_tk,))

              own_blocks = [("l", 0, 512), ("l", 512, 512), ("l", 1024, 512), ("l", 1536, 512)]
              seqsA = [("c", 0, CTX)] + own_blocks + [("l", 2048, 128)]

              for h in range(8 if asub is None else asub.get('nh', 8)):
                  colsA = [D + h * 128, D + 1024 + h * 128] + [h * 512 + g * 128 for g in range(4)]

                  def evacA(ci, kind, t0, n, pp, h=h):
                      if ci >= 2 and (kind == "c" or t0 >= NOWN):
                          return False
                      if pp is None:
                          return True
                      if asub is not None and asub.get('simple', 0):
                          vb = vb_r.next()
                          p.op("act", lambda e: e.activation(out=vb[:, 0:n], in_=pp[:, 0:n], func=AF.Copy),
                               reads=(pp,), writes=(vb,))
                          return True
                      if ci == 0:
                          if kind == "c":
                              norm_rope(pp, n, 1, None, KT, KT[:, NH:NH + CTX])
                          else:
                              norm_rope(pp, n, 1, t0, KT, KT[:, t0:t0 + n])
                      elif ci == 1:
                          vb = vb_r.next()
                          p.op("act", lambda e: e.activation(out=vb[:, 0:n], in_=pp[:, 0:n], func=AF.Copy),
                               reads=(pp,), writes=(vb,))
                          px = px_r.next()
                          pxb = px[:, :].bitcast(BF16)

                          def tr(e):
                              ins = None
                              for i in range(n // 128):
                                  ins = e.transpose(out=pxb[:, i * 128:(i + 1) * 128], in_=vb[:, i * 128:(i + 1) * 128],
                                                    identity=ident_bf[:, :])
                              return ins
                          p.op("pe", tr, reads=(vb, ident_bf), writes=(px,))
                          kb0 = 17 if kind == "c" else t0 // 128
                          nb_ = n // 128
                          p.op("act", lambda e: e.activation(
                              out=Vt[:, kb0:kb0 + nb_, :],
                              in_=pxb[:, 0:nb_ * 128].rearrange("p (b d) -> p b d", d=128), func=AF.Copy),
                              reads=(px,), writes=(Vt,))
                      else:
                          g = ci - 2
                          norm_rope(pp, n, 0, t0, QT, QT[:, g, t0:t0 + n])
                      return True
                  project(colsA, seqsA, evacA)

                  for i in range(16 if asub is None else asub.get('nq', 16)):
                      kbs = [("c", 17, NH), ("c", 18, NH + 128)]
                      if i > 0:
                          kbs.append(("p", i - 1, (i - 1) * 128))
                      kbs.append(("o", i, i * 128))
                      kbs.append(("n", i + 1, (i + 1) * 128))
                      for ki, (kk, vb_i, kc0) in enumerate(kbs):
                          pS = pS_r.next(); PT = PT_r.next()
                          p.op("pe", lambda e, pS=pS, kc0=kc0, i=i: e.matmul(
                              pS[:, :].rearrange("p (g q) -> p g q", g=4), lhsT=KT[:, kc0:kc0 + 128],
                              rhs=QT[:, :, i * 128:(i + 1) * 128], start=True, stop=True),
                              reads=(KT, QT), writes=(pS,))
                          p.op("act", lambda e, pS=pS, PT=PT: e.activation(out=PT[:, :], in_=pS[:, :], func=AF.Exp,
                                                                           scale=SCALE), reads=(pS,), writes=(PT,))
                          if kk in ("p", "n"):
                              mi = 0 if kk == "p" else 1
                              p.op("dve", lambda e, PT=PT, mi=mi: e.tensor_tensor(
                                  out=PT[:, :], in0=PT[:, :], in1=masks[:, mi, :], op=ALU.mult),
                                  reads=(PT, masks), writes=(PT,))

                          def pv(e, PT=PT, vb_i=vb_i, ki=ki, last=(ki == len(kbs) - 1)):
                              e.matmul(pO[:, :], lhsT=Vt[:, vb_i, :], rhs=PT[:, :], start=(ki == 0), stop=last)
                              return e.matmul(pR[:, :], lhsT=ones_b[:, :], rhs=PT[:, :], start=(ki == 0), stop=last)
                          p.op("pe", pv, reads=(Vt, PT, ones_b), writes=(pO, pR))
                      den = den_r.next()
                      for g in range(4):
                          p.op("dve", lambda e, den=den, g=g, h=h: e.tensor_scalar(
                              out=den[:, g * 128:(g + 1) * 128], in0=pR[:, g * 128:(g + 1) * 128],
                              scalar1=esink[:, h * 4 + g:h * 4 + g + 1], scalar2=None, op0=ALU.add),
                              reads=(pR, esink), writes=(den,))
                      p.op("dve", lambda e, den=den: e.reciprocal(out=den[:, :], in_=den[:, :]),
                           reads=(den,), writes=(den,))
                      p.op("dve", lambda e, den=den, i=i: e.tensor_tensor(
                          out=OT[:, :, i * 128:(i + 1) * 128], in0=pO[:, :].rearrange("p (g q) -> p g q", g=4),
                          in1=den[:, :].rearrange("p (g q) -> p g q", g=4), op=ALU.mult),
                          reads=(pO, den), writes=(OT,))

                  colsB = [6144 + h * 512 + g * 128 for g in range(4)]

                  def evacB(ci, kind, t0, n, pp, h=h):
                      if pp is None:
                          return True
                      gs = gs_r.next(); zt = zt_r.next()
                      p.op("act", lambda e: e.activation(out=gs[:, 0:n], in_=pp[:, 0:n], func=AF.Silu),
                           reads=(pp,), writes=(gs,))
                      p.op("dve", lambda e: e.tensor_tensor(out=zt[:, 0:n], in0=gs[:, 0:n], in1=OT[:, ci, t0:t0 + n],
                                                             op=ALU.mult), reads=(gs, OT), writes=(zt,))
                      r0 = (h * 4 + ci) * 128
                      dma_st("act", zt, Z1T[r0:r0 + 128, t0:t0 + n], zt[:, 0:n])
                      return True
                  if asub is None or asub.get('pb', 1):
                      project(colsB, own_blocks, evacB)
              p.end_stage()

        blocks1 = [dict(zmode="direct", zsrc=Z1T, t0=t0, n=512, xsrc=X1, dst=out, grow=0)
                   for t0 in (0, 512, 1024, 1536)]
        if nstage >= 7:
            stage_out(1, at_w_out, blocks1)

        p.check()
        p.emit()
    return nc


def _fm(v):
    v = np.asarray(v, np.float32)
    lead = v.shape[:-1]
    a = v.reshape(lead + (KC, 128))
    a = np.moveaxis(a, -1, 0)
    return np.ascontiguousarray(a)


def prepare_inputs(x, c, ctx, c_ctx, w_mod, b_mod, norm_w, rg_w_in, rg_conv_w, rg_conv_b, rg_w_r, rg_b_r,
                   rg_w_i, rg_b_i, rg_lam, rg_w_out, at_w_in, at_q_norm, at_k_norm, at_sink, at_w_out):
    f32 = np.float32
    shared = {}
    shared["bmod_fm"] = np.ascontiguousarray(
        np.asarray(b_mod, f32).reshape(2, 96, 128).transpose(2, 0, 1))
    shared["normw_fm"] = _fm(norm_w)
    shared["rg_w_in"] = np.ascontiguousarray(rg_w_in[0], f32)
    shared["convb_fm"] = _fm(rg_conv_b[0])
    shared["rg_w_out"] = np.ascontiguousarray(rg_w_out[0], f32)
    shared["at_w_in"] = np.ascontiguousarray(at_w_in[0], f32)
    shared["qk_norm_fm"] = np.ascontiguousarray(np.stack([at_q_norm[0], at_k_norm[0]], axis=1), f32)
    shared["sink_bc"] = np.ascontiguousarray(np.broadcast_to(np.asarray(at_sink[0], f32)[None, :], (128, 32)))
    shared["at_w_out"] = np.ascontiguousarray(at_w_out[0], f32)
    ident = np.eye(128, dtype=f32)
    shared["ident_in"] = ident
    rot = np.zeros((128, 128), f32)
    for m in range(128):
        if (m % 64) < 32:
            rot[m + 32, m] = -1.0
        else:
            rot[m - 32, m] = 1.0
    shared["rot_m"] = rot
    kj = np.arange(128)[:, None]
    qi = np.arange(128)[None, :]
    mprev = (kj >= qi).astype(f32)
    mnext = (kj <= qi).astype(f32)
    shared["mask_in"] = np.ascontiguousarray(np.stack([np.tile(mprev, (1, 4)), np.tile(mnext, (1, 4))], 0))

    conv_w = np.asarray(rg_conv_w[0], f32)
    zero = np.zeros((1, D), f32)
    per_core = []
    for core in range(8):
        b, half = core // 2, core % 2
        m = dict(shared)
        if half == 0:
            idx = np.arange(NLOC)
            m["ctx_loc"] = np.ascontiguousarray(ctx[b], f32)
            conv5 = np.concatenate([conv_w, zero], 0)
        else:
            idx = 4095 - np.arange(NLOC)
            m["ctx_loc"] = np.ascontiguousarray(np.asarray(ctx[b], f32)[::-1])
            conv5 = np.concatenate([zero, conv_w[::-1]], 0)
        m["x_loc"] = np.ascontiguousarray(np.asarray(x[b], f32)[idx])
        m["conv5_fm"] = np.ascontiguousarray(np.moveaxis(_fm(conv5), 1, 2))
        m["c_fm"] = np.ascontiguousarray(np.moveaxis(_fm(np.concatenate([np.asarray(c, f32), np.asarray(c_ctx, f32)[None]], 0)), 1, 2))
        m["w_mod"] = np.ascontiguousarray(np.asarray(w_mod, f32)[:, :, core * 1536:(core + 1) * 1536])
        sb_ = np.zeros((128, 4), f32)
        sb_[:, b] = 1.0
        m["selb_in"] = sb_
        dA, dB = half, 1 - half
        m["gate_w"] = np.ascontiguousarray(np.stack([rg_w_r[0, dA], rg_w_i[0, dA], rg_w_r[0, dB], rg_w_i[0, dB]], 0), f32)
        m["gate_b_fm"] = _fm(np.stack([rg_b_r[0, dA], rg_b_i[0, dA], rg_b_r[0, dB], rg_b_i[0, dB]], 0))
        m["lam_fm"] = _fm(np.stack([rg_lam[0, dA], rg_lam[0, dB]], 0))
        t = idx.astype(np.float64)
        row = np.floor(t / 64.0)
        col = t - row * 64.0
        inv = 10000.0 ** (-np.arange(32, dtype=np.float64) * (2.0 / 64.0))
        dd = np.arange(128)
        pos = np.where(dd[:, None] < 64, row[None, :], col[None, :])
        ang = pos * inv[dd % 32][:, None]
        m["cos_fm"] = np.ascontiguousarray(np.cos(ang), f32)
        m["sin_fm"] = np.ascontiguousarray(np.sin(ang), f32)
        selv = np.zeros((128, 2), f32)
        selv[:, 1 - half] = 1.0
        m["sel_in"] = selv
        per_core.append(m)
    return per_core


_NC_CACHE = {}


def kernel(**inputs):
    inputs = {k: np.asarray(v) for k, v in inputs.items()}
    per_core = prepare_inputs(**inputs)
    if "nc" not in _NC_CACHE:
        _NC_CACHE["nc"] = build_program(DEBUG)
    nc = _NC_CACHE["nc"]
    res = run_bass_kernel_spmd(nc, per_core, core_ids=list(range(8)))
    outp = np.empty((4, 4096, D), np.float32)
    for core in range(8):
        b, half = core // 2, core % 2
        o = np.asarray(res.results[core]["out"], np.float32)
        if half == 0:
            outp[b, 0:NOWN] = o
        else:
            outp[b, NOWN:] = o[::-1]
    if DEBUG:
        kernel.last = res
    return outp
```
